# Optimizing a Trainium2 kernel written in Bass

```python
import math
import jax, jax.numpy as jnp
from jax import lax
import numpy as np

D_MODEL = 1024
BATCH = 2
SEQ = 8192
DEPTH = 4

N_MIXERS = 2
N_HEADS = 16
HEAD_DIM = D_MODEL // N_HEADS
BLOCK_Q = 128
POOL_WINDOWS = (2, 4, 8, 16)
N_POOL_GROUPS = len(POOL_WINDOWS)
POOL_GROUP = D_MODEL // N_POOL_GROUPS
D_FF = 2816
PLE_DIM = 256
EPS = 1e-6

kernel_name = "hybrid_stickbreak_pool_macaron"


def rms_norm(x, g):
    xf = x.astype(jnp.float32)
    y = xf * lax.rsqrt(jnp.mean(xf * xf, axis=-1, keepdims=True) + EPS)
    return (y * g.astype(jnp.float32)).astype(x.dtype)


def swiglu(h, w_gu, w_down):
    gate, up = jnp.split(h @ w_gu, 2, axis=-1)
    return (jax.nn.silu(gate) * up) @ w_down


def stick_breaking_attention(h, w_qkv, q_gain, k_gain, w_o):
    B, S, _ = h.shape
    q, k, v = jnp.split(h @ w_qkv, 3, axis=-1)
    q = rms_norm(q.reshape(B, S, N_HEADS, HEAD_DIM), q_gain)
    k = rms_norm(k.reshape(B, S, N_HEADS, HEAD_DIM), k_gain)
    v = v.reshape(B, S, N_HEADS, HEAD_DIM)
    q, k, v = (t.transpose(0, 2, 1, 3) for t in (q, k, v))
    scale = 1.0 / math.sqrt(HEAD_DIM)
    outs = []
    for blk in range(S // BLOCK_Q):
        t0 = blk * BLOCK_Q
        t1 = t0 + BLOCK_Q
        qb = q[:, :, t0:t1]
        kb = k[:, :, :t1]
        vb = v[:, :, :t1]
        z = jnp.einsum('bhqd,bhkd->bhqk', qb, kb).astype(jnp.float32) * scale
        qpos = t0 + jnp.arange(BLOCK_Q)
        kpos = jnp.arange(t1)
        causal = kpos[None, :] < qpos[:, None]
        log_stay = jnp.where(causal, jax.nn.log_sigmoid(-z), 0.0)
        log_after = lax.cumsum(log_stay, axis=3, reverse=True) - log_stay
        weights = jnp.where(causal, jnp.exp(jax.nn.log_sigmoid(z) + log_after), 0.0)
        outs.append(jnp.einsum('bhqk,bhkd->bhqd', weights.astype(vb.dtype), vb))
    o = jnp.concatenate(outs, axis=2)
    o = o.transpose(0, 2, 1, 3).reshape(B, S, D_MODEL)
    return o @ w_o


def multiscale_pool_mixer(h, w_in, w_grp, scale):
    B, S, _ = h.shape
    u = (h @ w_in).reshape(B, S, N_POOL_GROUPS, POOL_GROUP)
    uf = u.astype(jnp.float32)
    c = jnp.cumsum(uf, axis=1)
    pos = jnp.arange(S)
    outs = []
    for gi, w in enumerate(POOL_WINDOWS):
        cg = c[:, :, gi]
        cpad = jnp.pad(cg, ((0, 0), (w, 0), (0, 0)))
        wsum = cpad[:, w:] - cpad[:, :S]
        cnt = jnp.minimum(pos + 1, w).astype(jnp.float32)
        outs.append(wsum / cnt[None, :, None] - uf[:, :, gi])
    pooled = jnp.stack(outs, axis=2).astype(h.dtype)
    y = jnp.einsum('bsgc,gcd->bsgd', pooled, w_grp).reshape(B, S, D_MODEL)
    return y * scale


def setup_inputs(seed: int = 0) -> dict:
    key = jax.random.key(seed)
    ks = iter(jax.random.split(key, 32))
    n_a = (DEPTH + 1) // 2
    n_b = DEPTH // 2
    f32 = jnp.float32

    def w(shape, fan_in):
        return jax.random.normal(next(ks), shape, f32) * fan_in ** -0.5

    def gain(shape):
        return 1.0 + 0.05 * jax.random.normal(next(ks), shape, f32)

    return {
        "x": jax.random.normal(next(ks), (BATCH, SEQ, D_MODEL), f32),
        "p": jax.random.normal(next(ks), (DEPTH, BATCH, SEQ, PLE_DIM), f32),
        "norm_ffn1": gain((DEPTH, D_MODEL)),
        "w_ffn1_gu": w((DEPTH, D_MODEL, 2 * D_FF), D_MODEL),
        "w_ffn1_down": w((DEPTH, D_FF, D_MODEL), D_FF),
        "norm_mix": gain((DEPTH, D_MODEL)),
        "w_qkv": w((n_a, D_MODEL, 3 * D_MODEL), D_MODEL),
        "q_norm": gain((n_a, HEAD_DIM)),
        "k_norm": gain((n_a, HEAD_DIM)),
        "w_o": w((n_a, D_MODEL, D_MODEL), D_MODEL),
        "w_pool_in": w((n_b, D_MODEL, D_MODEL), D_MODEL),
        "w_pool_grp": w((n_b, N_POOL_GROUPS, POOL_GROUP, POOL_GROUP), POOL_GROUP),
        "pool_scale": gain((n_b, D_MODEL)),
        "norm_ffn2": gain((DEPTH, D_MODEL)),
        "w_ffn2_gu": w((DEPTH, D_MODEL, 2 * D_FF), D_MODEL),
        "w_ffn2_down": w((DEPTH, D_FF, D_MODEL), D_FF),
        "norm_ple": gain((DEPTH, D_MODEL)),
        "w_ple_gate": w((DEPTH, D_MODEL, D_MODEL), D_MODEL),
        "w_ple_proj": w((DEPTH, PLE_DIM, D_MODEL), PLE_DIM),
    }


def reference(x, p, norm_ffn1, w_ffn1_gu, w_ffn1_down, norm_mix, w_qkv, q_norm,
              k_norm, w_o, w_pool_in, w_pool_grp, pool_scale, norm_ffn2,
              w_ffn2_gu, w_ffn2_down, norm_ple, w_ple_gate, w_ple_proj):
    for i in range(DEPTH):
        x = x + 0.5 * swiglu(rms_norm(x, norm_ffn1[i]), w_ffn1_gu[i], w_ffn1_down[i])
        h = rms_norm(x, norm_mix[i])
        j = i // N_MIXERS
        if i % N_MIXERS == 0:
            mix = stick_breaking_attention(h, w_qkv[j], q_norm[j], k_norm[j], w_o[j])
        else:
            mix = multiscale_pool_mixer(h, w_pool_in[j], w_pool_grp[j], pool_scale[j])
        x = x + mix
        x = x + 0.5 * swiglu(rms_norm(x, norm_ffn2[i]), w_ffn2_gu[i], w_ffn2_down[i])
        gate = jax.nn.sigmoid(rms_norm(x, norm_ple[i]) @ w_ple_gate[i])
        x = x + gate * (p[i] @ w_ple_proj[i])
    return x
```

```python
import contextlib
import numpy as np
import ml_dtypes
import concourse.bass as bass
import concourse.mybir as mybir
from concourse.bass_utils import run_bass_kernel_spmd

F32 = mybir.dt.float32
BF16 = mybir.dt.bfloat16
AF = mybir.ActivationFunctionType
ALU = mybir.AluOpType

D = 1024
T = 2048
S = 8192
DFF = 2816
NF = 22
DEPTH = 4
EPS = 1e-6
MASKV = -128.0
ARENA = 52224
COMPUTE = ("pe", "act", "dve")


class Sched:
    def __init__(self, nc, es):
        self.nc = nc
        self.es = es
        self.prog = {e: [] for e in ("pe", "act", "dve", "pool", "sp")}
        self.esem = {e: es.enter_context(nc.semaphore("s_" + e)) for e in COMPUTE}
        self.ecnt = {e: 0 for e in COMPUTE}
        self.slots = {}
        self.waited = {e: {} for e in self.prog}
        self.lastw = {}
        self.readers = {}
        self.barrier_toks = []
        self.pending = {e: [] for e in self.prog}
        self.last_barrier = []

    def slot(self, name):
        if name not in self.slots:
            self.slots[name] = [self.es.enter_context(self.nc.semaphore("d_" + name)), 0]
        return self.slots[name]

    def barrier(self, engines=("pe", "act", "dve", "sp", "pool")):
        toks = [(self.esem[e], self.ecnt[e], e) for e in COMPUTE if self.ecnt[e] > 0]
        toks += [(s[0], s[1], None) for s in self.slots.values() if s[1] > 0]
        self.last_barrier = list(toks)
        for e in engines:
            self.pending[e] = list(toks)

    def op(self, eng, fn, reads=(), writes=(), slot=None, ndma=1, after_barrier=False):
        toks = list(self.pending[eng])
        self.pending[eng] = []
        if after_barrier:
            toks += self.last_barrier
        for k in reads:
            if k in self.lastw:
                toks.append(self.lastw[k])
        for k in writes:
            if k in self.lastw:
                toks.append(self.lastw[k])
            toks.extend(self.readers.get(k, ()))
        waits = {}
        for (sem, val, src) in toks:
            if eng == "pe" and src == "pe":
                continue
            key = id(sem)
            if key not in waits or waits[key][1] < val:
                waits[key] = (sem, val)
        wl = []
        for key, (sem, val) in waits.items():
            if self.waited[eng].get(key, 0) >= val:
                continue
            self.waited[eng][key] = val
            wl.append((sem, val))
        if eng in COMPUTE:
            self.ecnt[eng] += 1
            tok = (self.esem[eng], self.ecnt[eng], eng)
            inc = (self.esem[eng], 1)
        else:
            sl = self.slot(slot)
            step = 1 if slot.startswith("cc_") else 16
            sl[1] += step * ndma
            tok = (sl[0], sl[1], None)
            inc = (sl[0], step)
        self.prog[eng].append((wl, fn, inc))
        for k in writes:
            self.lastw[k] = tok
            self.readers[k] = []
        for k in reads:
            self.readers.setdefault(k, []).append(tok)
        return tok

    def replay(self, eng_name, eng):
        compute = eng_name in COMPUTE
        for (wl, fn, inc) in self.prog[eng_name]:
            for (sem, val) in wl:
                eng.wait_ge(sem, val)
            r = fn(eng)
            if r is None:
                continue
            if not isinstance(r, (list, tuple)):
                r = [r]
            if compute:
                r[-1].then_inc(inc[0], inc[1])
            else:
                for ins in r:
                    ins.then_inc(inc[0], inc[1])


def build(stop_after=None, nl=DEPTH, debug=False):
    nc = bass.Bass("TRN2", target_bir_lowering=False)
    es = contextlib.ExitStack()

    def din(name, shape, dt=F32):
        return nc.dram_tensor(name, list(shape), dt, kind="ExternalInput").ap()

    xT_d = din("xT", [D, T])
    pT_d = din("pT", [DEPTH, 256, T])
    wgu_d = din("wgu", [nl, 2, NF, 128, 8, 256])
    wdn_d = din("wdn", [nl, 2, 8, 128, NF * 128])
    wqk_d = din("wqk", [2, 4, 128, 8, 128])
    wv_d = din("wv", [2, 128, 8, 256])
    wo_d = din("wo", [2, 8, 128, 8, 128])
    wpi_d = din("wpi", [2, 8, 128, 8, 128])
    wpg_d = din("wpg", [DEPTH, 8, 128, 8, 128])
    wgrp_d = din("wgrp", [2, 128, 4 * 2 * 256])
    wpp_d = din("wpp", [DEPTH, 128, 2, 1024])
    gn_d = din("gn", [128, 4 * DEPTH * 8])
    qkg_d = din("qkg", [128, 4])
    psc_d = din("psc", [128, 16])
    sel_d = din("sel", [128, 4])
    icnt_d = din("icnt", [128, 64])
    cmat_d = din("cmat", [128, 7 * 128], BF16)
    mneg_d = din("mneg", [128, 4 * 512], BF16)
    yT_d = nc.dram_tensor("yT", [D, T], F32, kind="ExternalOutput").ap()
    if debug:
        dbg_mine = nc.dram_tensor("dbg_mine", [4 * 768, 2048], BF16, kind="ExternalOutput").ap()
        dbg_ogin = nc.dram_tensor("dbg_ogin", [1024, 2048], BF16, kind="ExternalOutput").ap()

    hgin_a = [nc.dram_tensor(f"hgina{j}", [2048, 1024], BF16, kind="Internal").ap() for j in range(2)]
    hgout_a = [nc.dram_tensor(f"hgouta{j}", [4 * 2048, 1024], BF16, kind="Internal").ap() for j in range(2)]
    ogin = [nc.dram_tensor(f"ogin{j}", [1024, 2048], BF16, kind="Internal").ap() for j in range(2)]
    ogout = [nc.dram_tensor(f"ogout{j}", [4096, 2048], BF16, kind="Internal").ap() for j in range(2)]
    mine = [nc.dram_tensor(f"mine{j}", [4 * 768, 2048], BF16, kind="Internal").ap() for j in range(2)]
    hgin = [nc.dram_tensor(f"hgin{j}", [128, 128], F32, kind="Internal").ap() for j in range(2)]
    hgout = [nc.dram_tensor(f"hgout{j}", [4 * 128, 128], F32, kind="Internal").ap() for j in range(2)]

    def sb(name, shape, dt):
        return es.enter_context(nc.sbuf_tensor(name, list(shape), dt))

    xT = sb("xT_sb", [128, 8, T], F32)
    arena = sb("arena", [128, ARENA], BF16)
    wgu = sb("wgu_sb", [128, 2, 8 * 256], BF16)
    wdn = sb("wdn_sb", [128, 2, NF * 128], BF16)
    w8 = sb("w8_sb", [128, 3, 8 * 128], BF16)
    cmat = sb("cmat_sb", [128, 7, 128], BF16)
    mneg = sb("mneg_sb", [128, 4, 512], BF16)
    gn = sb("gn_sb", [128, 4, DEPTH, 8], F32)
    qkg = sb("qkg_sb", [128, 2, 2], F32)
    qg8 = sb("qg8_sb", [128, 2], F32)
    psc = sb("psc_sb", [128, 2, 8], F32)
    sel = sb("sel_sb", [128, 4], F32)
    icnt = sb("icnt_sb", [128, 4, 16], F32)
    wgrp = sb("wgrp_sb", [128, 4, 2, 256], BF16)
    ps = es.enter_context(nc.psum_tensor("ps", [128, 8, 512], F32))

    IDENT, ONES_MS, ONES_HD, NEGTRI, NEGREST, NEGIDENT, TRI01 = range(7)

    def av(off, shape, dt=BF16):
        n = int(np.prod(shape))
        if dt == F32:
            v = arena[:, off:off + 2 * n].bitcast(F32)
        else:
            v = arena[:, off:off + n]
        if len(shape) == 1:
            return v
        if len(shape) == 2:
            return v.rearrange("p (a b) -> p a b", a=shape[0])
        return v.rearrange("p (a b c) -> p a b c", a=shape[0], b=shape[1])

    sc = Sched(nc, es)
    me4 = {}

    def ld_consts(e):
        return [
            e.dma_start(out=cmat[:].rearrange("p a b -> p (a b)"), in_=cmat_d),
            e.dma_start(out=mneg[:].rearrange("p a b -> p (a b)"), in_=mneg_d),
            e.dma_start(out=gn[:].rearrange("p a b c -> p (a b c)"), in_=gn_d),
            e.dma_start(out=qkg[:].rearrange("p a b -> p (a b)"), in_=qkg_d),
            e.dma_start(out=psc[:].rearrange("p a b -> p (a b)"), in_=psc_d),
            e.dma_start(out=sel[:], in_=sel_d),
            e.dma_start(out=icnt[:].rearrange("p a b -> p (a b)"), in_=icnt_d),
        ]
    sc.op("sp", ld_consts, writes=["consts"], slot="consts", ndma=7)
    for k in range(8):
        sc.op("sp", lambda e, k=k: e.dma_start(out=xT[:, k, :], in_=xT_d[k * 128:(k + 1) * 128, :]),
              writes=[("x", k, t) for t in range(4)], slot=f"xld{k}")
    sc.op("dve", lambda e: e.tensor_scalar(out=qg8[:], in0=qkg[:, 0, :], scalar1=0.125, scalar2=None, op0=ALU.mult),
          reads=["consts"], writes=["qg8"])

    ring_cnt = {"wgu": 0, "wdn": 0, "w8": 0}

    def load_w(kind, src_ap, nbuf, view):
        i = ring_cnt[kind]
        ring_cnt[kind] += 1
        b = i % nbuf
        key = (kind, b)
        sc.op("pool", lambda e: e.dma_start(out=view(b), in_=src_ap), writes=[key], slot=f"{kind}{b}")
        return b, key

    def load_wgu(l, which, f):
        return load_w("wgu", wgu_d[l, which, f].rearrange("p k c -> p (k c)"), 2, lambda b: wgu[:, b, :])

    def load_wdn(l, which, dc):
        i = ring_cnt["wdn"]
        ring_cnt["wdn"] += 1
        b = i % 2
        key = ("wdn", b)
        h = NF * 64

        def fn(e):
            return [e.dma_start(out=wdn[:, b, 0:h], in_=wdn_d[l, which, dc, :, 0:h]),
                    e.dma_start(out=wdn[:, b, h:2 * h], in_=wdn_d[l, which, dc, :, h:2 * h])]
        sc.op("pool", fn, writes=[key], slot=f"wdn{b}", ndma=2)
        return b, key

    def load_w8(src3):
        return load_w("w8", src3.rearrange("p k c -> p (k c)"), 3, lambda b: w8[:, b, :])

    O_HT = 0
    O_SQ, O_LN, O_RS, O_SG = 44032, 48128, 49152, 50176

    def mm_group(e, out, pairs, start=True, stop=True):
        r = None
        n = len(pairs)
        for i, (l, rh) in enumerate(pairs):
            r = e.matmul(out, lhsT=l, rhs=rh, start=(start and i == 0), stop=(stop and i == n - 1),
                         skip_group_check=True)
        return r

    def norm_half(half, kind, layer, ssbank=6, hoff=0, hkey="hT"):
        hT = av(hoff, [8, 1024])
        sq = av(O_SQ, [8, 512])
        lnv = av(O_LN, [512], F32)
        rstd = av(O_RS, [512], F32)
        for tt in range(2):
            gt = half * 2 + tt
            c0 = gt * 512
            sc.op("act", lambda e, c0=c0: e.activation(out=sq, in_=xT[:, :, c0:c0 + 512], func=AF.Square),
                  reads=[("x", k, gt) for k in range(8)], writes=["sq"])
            sc.op("pe", lambda e: mm_group(e, ps[:, ssbank, :], [(cmat[:, ONES_MS, :], sq[:, k, :]) for k in range(8)]),
                  reads=["sq", "consts"], writes=[("ps", ssbank)])
            sc.op("act", lambda e: e.activation(out=lnv, in_=ps[:, ssbank, :], func=AF.Ln, bias=EPS, scale=1.0),
                  reads=[("ps", ssbank)], writes=["lnv"])
            sc.op("act", lambda e: e.activation(out=rstd, in_=lnv, func=AF.Exp, scale=-0.5),
                  reads=["lnv"], writes=["rstd"])
            for k in range(8):
                sc.op("dve", lambda e, k=k, c0=c0, tt=tt: e.scalar_tensor_tensor(
                    out=hT[:, k, tt * 512:(tt + 1) * 512], in0=xT[:, k, c0:c0 + 512],
                    scalar=gn[:, kind, layer, k:k + 1], in1=rstd, op0=ALU.mult, op1=ALU.mult),
                    reads=[("x", k, gt), "rstd", "consts"], writes=[(hkey, k, tt)])
        return hT

    def mix_prenorm(layer, half, defer=False):
        j = layer // 2
        hT = norm_half(half, 1, layer)
        hv = hgin_a[j].rearrange("(h k p) t -> h p k t", h=2, p=128)
        sc.op("sp", lambda e: e.dma_start(out=hv[half], in_=hT),
              reads=[("hT", k, tt) for k in range(8) for tt in range(2)], writes=[("hgin_a", half)], slot="hgst")
        def emit_ag():
            sc.op("pool", lambda e: [e.collective_compute("AllGather", ALU.bypass, replica_groups=[[0, 1, 2, 3], [4, 5, 6, 7]],
                                                          ins=[hgin_a[j][(half * 4 + q) * 256:(half * 4 + q + 1) * 256, :]],
                                                          outs=[hgout_a[j][(half * 4 + q) * 1024:(half * 4 + q + 1) * 1024, :]])
                                     for q in range(4)],
                  reads=[("hgin_a", half)], writes=[("hgout_a", half)], slot="cc_a", ndma=4)
        if defer:
            return emit_ag
        emit_ag()

    def ffn(layer, which):
        sc.barrier(("pe", "act", "dve", "sp"))
        kind = 0 if which == 0 else 2
        aT = av(8192, [NF, 1024])
        sgt = av(O_SG, [2, 512], F32)
        hTs = {0: norm_half(0, kind, layer)}
        deferred = []
        want_pre = False
        for half in range(2):
            hT = hTs[half]
            hkey = "hT" if half == 0 else "hT2"
            cnt = 0
            for f in range(NF):
                if half == 0 and f == NF // 2:
                    hTs[1] = norm_half(1, kind, layer, hoff=30720, hkey="hT2")
                b, wkey = load_wgu(layer, which, f)
                if half == 1 and f == 2 and want_pre:
                    deferred.append(mix_prenorm(layer, 0, defer=True))
                if half == 1 and f == 5 and deferred:
                    deferred.pop()()
                for tt in range(2):
                    gb = cnt % 2
                    cnt += 1
                    wv_ = wgu[:, b, :].rearrange("p (k c) -> p k c", k=8)
                    sc.op("pe", lambda e, wv_=wv_, tt=tt, gb=gb, hT=hT: [
                        mm_group(e, ps[:, gb, :], [(wv_[:, k, 0:128], hT[:, k, tt * 512:(tt + 1) * 512]) for k in range(8)]),
                        mm_group(e, ps[:, 2 + gb, :], [(wv_[:, k, 128:256], hT[:, k, tt * 512:(tt + 1) * 512]) for k in range(8)])],
                        reads=[wkey] + [(hkey, k, tt) for k in range(8)], writes=[("ps", gb), ("ps", 2 + gb)])
                    sc.op("act", lambda e, gb=gb: e.activation(out=sgt[:, gb, :], in_=ps[:, gb, :], func=AF.Silu),
                          reads=[("ps", gb)], writes=[("sgt", gb)])
                    sc.op("dve", lambda e, gb=gb, f=f, tt=tt: e.tensor_tensor(
                        out=aT[:, f, tt * 512:(tt + 1) * 512], in0=sgt[:, gb, :], in1=ps[:, 2 + gb, :], op=ALU.mult),
                        reads=[("sgt", gb), ("ps", 2 + gb)], writes=[("aT", f, tt)])
            cnt = 0
            for dc in range(8):
                b, wkey = load_wdn(layer, which, dc)
                wv_ = wdn[:, b, :].rearrange("p (f c) -> p f c", f=NF)
                for tt in range(2):
                    db = 4 + cnt % 2
                    cnt += 1
                    gt = half * 2 + tt
                    sc.op("pe", lambda e, wv_=wv_, tt=tt, db=db: mm_group(
                        e, ps[:, db, :], [(wv_[:, f, :], aT[:, f, tt * 512:(tt + 1) * 512]) for f in range(NF)]),
                        reads=[wkey] + [("aT", f, tt) for f in range(NF)], writes=[("ps", db)])
                    sc.op("dve", lambda e, dc=dc, gt=gt, db=db: e.scalar_tensor_tensor(
                        out=xT[:, dc, gt * 512:(gt + 1) * 512], in0=ps[:, db, :], scalar=0.5,
                        in1=xT[:, dc, gt * 512:(gt + 1) * 512], op0=ALU.mult, op1=ALU.add),
                        reads=[("ps", db), ("x", dc, gt)], writes=[("x", dc, gt)])
            if half == 0 and which == 0 and layer % 2 == 0:
                want_pre = True

    def ple(layer):
        sc.barrier(("pe", "act", "dve", "sp"))
        pTb = av(8192, [2, T])
        wpp = av(12288, [2, 1024])
        sgm = av(14336, [2, 512], F32)
        tmp = av(16384, [2, 512], F32)
        sc.op("pool", lambda e: [e.dma_start(out=pTb, in_=pT_d[layer].rearrange("(k p) t -> p k t", p=128)),
                                 e.dma_start(out=wpp, in_=wpp_d[layer])],
              writes=["pTb", "wpp"], slot="plew", ndma=2, after_barrier=True)
        cnt = 0
        for half in range(2):
            hT = norm_half(half, 3, layer)
            for dc in range(8):
                b, wkey = load_w8(wpg_d[layer, dc])
                wv_ = w8[:, b, :].rearrange("p (k c) -> p k c", k=8)
                for tt in range(2):
                    gb = cnt % 2
                    cnt += 1
                    gt = half * 2 + tt
                    sc.op("pe", lambda e, wv_=wv_, tt=tt, gb=gb, dc=dc, gt=gt: [
                        mm_group(e, ps[:, gb, :], [(wv_[:, k, :], hT[:, k, tt * 512:(tt + 1) * 512]) for k in range(8)]),
                        mm_group(e, ps[:, 2 + gb, :], [(wpp[:, k, dc * 128:(dc + 1) * 128], pTb[:, k, gt * 512:(gt + 1) * 512]) for k in range(2)])],
                        reads=[wkey, "pTb", "wpp"] + [("hT", k, tt) for k in range(8)], writes=[("ps", gb), ("ps", 2 + gb)])
                    sc.op("act", lambda e, gb=gb: e.activation(out=sgm[:, gb, :], in_=ps[:, gb, :], func=AF.Sigmoid),
                          reads=[("ps", gb)], writes=[("sgm", gb)])
                    sc.op("dve", lambda e, gb=gb: e.tensor_tensor(out=tmp[:, gb, :], in0=sgm[:, gb, :], in1=ps[:, 2 + gb, :], op=ALU.mult),
                          reads=[("sgm", gb), ("ps", 2 + gb)], writes=[("tmp", gb)])
                    sc.op("dve", lambda e, gb=gb, dc=dc, gt=gt: e.tensor_tensor(
                        out=xT[:, dc, gt * 512:(gt + 1) * 512], in0=tmp[:, gb, :], in1=xT[:, dc, gt * 512:(gt + 1) * 512], op=ALU.add),
                        reads=[("tmp", gb), ("x", dc, gt)], writes=[("x", dc, gt)])

    def pool_mixer(layer):
        j = layer // 2
        sc.barrier(("pe", "act", "dve", "sp"))
        U = [av(0, [8, 1040], F32), av(16640, [8, 1040], F32)]
        O_H = 33280
        hT = av(O_H, [8, 1024])
        tmpS = av(41472, [2, 1040], F32)
        hal = av(45632, [4, 128], F32)
        sc.op("pool", lambda e: e.dma_start(out=wgrp[:].rearrange("p a b c -> p (a b c)"), in_=wgrp_d[j]),
              writes=["wgrp"], slot="wgrp")

        def do_norm(half):
            sq = av(46656, [8, 512])
            lnv = tmpS[:, 0, 0:512]
            rstd = tmpS[:, 1, 0:512]
            for tt in range(2):
                gt = half * 2 + tt
                c0 = gt * 512
                sc.op("act", lambda e, c0=c0: e.activation(out=sq, in_=xT[:, :, c0:c0 + 512], func=AF.Square),
                      reads=[("x", k, gt) for k in range(8)], writes=["sq"])
                sc.op("pe", lambda e: mm_group(e, ps[:, 6, :], [(cmat[:, ONES_MS, :], sq[:, k, :]) for k in range(8)]),
                      reads=["sq", "consts"], writes=[("ps", 6)])
                sc.op("act", lambda e: e.activation(out=lnv, in_=ps[:, 6, :], func=AF.Ln, bias=EPS, scale=1.0),
                      reads=[("ps", 6)], writes=[("tmpS", 0)])
                sc.op("act", lambda e: e.activation(out=rstd, in_=lnv, func=AF.Exp, scale=-0.5),
                      reads=[("tmpS", 0)], writes=[("tmpS", 1)])
                for k in range(8):
                    sc.op("dve", lambda e, k=k, c0=c0, tt=tt: e.scalar_tensor_tensor(
                        out=hT[:, k, tt * 512:(tt + 1) * 512], in0=xT[:, k, c0:c0 + 512],
                        scalar=gn[:, 1, layer, k:k + 1], in1=rstd, op0=ALU.mult, op1=ALU.mult),
                        reads=[("x", k, gt), ("tmpS", 1), "consts"], writes=[("hT", k, tt), ("pl", k)])

        def compute_u(half):
            cnt = 0
            for uc in range(8):
                b, wkey = load_w8(wpi_d[j, uc])
                wv_ = w8[:, b, :].rearrange("p (k c) -> p k c", k=8)
                for tt in range(2):
                    gb = cnt % 2
                    cnt += 1
                    sc.op("pe", lambda e, wv_=wv_, tt=tt, gb=gb: mm_group(
                        e, ps[:, gb, :], [(wv_[:, k, :], hT[:, k, tt * 512:(tt + 1) * 512]) for k in range(8)]),
                        reads=[wkey] + [("hT", k, tt) for k in range(8)], writes=[("ps", gb)])
                    sc.op("act", lambda e, gb=gb, uc=uc, tt=tt: e.activation(
                        out=U[half][:, uc, 16 + tt * 512:16 + (tt + 1) * 512], in_=ps[:, gb, :], func=AF.Copy),
                        reads=[("ps", gb)], writes=[("U", half, uc)])

        def pool_and_mix(half):
            pooled = hT
            for uc in range(8):
                g = uc // 2
                w = 2 << g
                cur = U[half][:, uc, :]
                lo = 0
                nsteps = g + 1
                srcbuf = cur
                for st in range(nsteps):
                    sh = 1 << st
                    lo2 = lo + sh
                    dst = tmpS[:, st % 2, :]
                    sc.op("dve", lambda e, dst=dst, srcbuf=srcbuf, lo2=lo2, sh=sh: e.tensor_tensor(
                        out=dst[:, lo2:1040], in0=srcbuf[:, lo2:1040], in1=srcbuf[:, lo2 - sh:1040 - sh], op=ALU.add),
                        reads=[("U", half, uc), ("tmpS", 0), ("tmpS", 1)], writes=[("tmpS", st % 2)])
                    srcbuf = dst
                    lo = lo2
                sfin = srcbuf
                sc.op("dve", lambda e, sfin=sfin, cur=cur, uc=uc, w=w: e.scalar_tensor_tensor(
                    out=pooled[:, uc, :], in0=sfin[:, 16:1040], scalar=1.0 / w, in1=cur[:, 16:1040],
                    op0=ALU.mult, op1=ALU.subtract),
                    reads=[("tmpS", 0), ("tmpS", 1), ("U", half, uc)],
                    writes=[("pl", uc), ("hT", uc, 0), ("hT", uc, 1)])
                if half == 0:
                    t16 = tmpS[:, (nsteps) % 2, 0:16]
                    sc.op("dve", lambda e, t16=t16, sfin=sfin, g=g: e.tensor_tensor(
                        out=t16, in0=sfin[:, 16:32], in1=icnt[:, g, :], op=ALU.mult),
                        reads=[("tmpS", 0), ("tmpS", 1), "consts"], writes=[("tmpS", nsteps % 2)])
                    sc.op("dve", lambda e, t16=t16, cur=cur, uc=uc: e.tensor_tensor(
                        out=pooled[:, uc, 0:16], in0=t16, in1=cur[:, 16:32], op=ALU.subtract),
                        reads=[("tmpS", 0), ("tmpS", 1), ("U", half, uc), ("pl", uc)], writes=[("pl", uc)])
            cnt = 0
            for g in range(4):
                for dd in range(2):
                    dc = 2 * g + dd
                    for tt in range(2):
                        db = 4 + cnt % 2
                        cnt += 1
                        gt = half * 2 + tt
                        sc.op("pe", lambda e, g=g, dd=dd, tt=tt, db=db: mm_group(
                            e, ps[:, db, :], [(wgrp[:, g, cc, dd * 128:(dd + 1) * 128], pooled[:, 2 * g + cc, tt * 512:(tt + 1) * 512]) for cc in range(2)]),
                            reads=["wgrp", ("pl", 2 * g), ("pl", 2 * g + 1)], writes=[("ps", db)])
                        sc.op("dve", lambda e, dc=dc, gt=gt, db=db: e.scalar_tensor_tensor(
                            out=xT[:, dc, gt * 512:(gt + 1) * 512], in0=ps[:, db, :], scalar=psc[:, j, dc:dc + 1],
                            in1=xT[:, dc, gt * 512:(gt + 1) * 512], op0=ALU.mult, op1=ALU.add),
                            reads=[("ps", db), ("x", dc, gt), "consts"], writes=[("x", dc, gt)])

        do_norm(1)
        compute_u(1)
        sc.op("sp", lambda e: e.dma_start(out=hgin[j].rearrange("p (k c) -> p k c", k=8), in_=U[1][:, :, 1024:1040]),
              reads=[("U", 1, uc) for uc in range(8)], writes=["hgin"], slot="hgin")
        sc.op("pool", lambda e: e.collective_compute("AllGather", ALU.bypass, replica_groups=[[0, 1, 2, 3], [4, 5, 6, 7]],
                                                     ins=[hgin[j]], outs=[hgout[j]]),
              reads=["hgin"], writes=["hgout"], slot="cc_h")
        do_norm(0)
        compute_u(0)
        for uc in range(8):
            sc.op("dve", lambda e, uc=uc: e.tensor_copy(out=U[1][:, uc, 0:16], in_=U[0][:, uc, 1024:1040]),
                  reads=[("U", 0, uc)], writes=[("U", 1, uc)])
        pool_and_mix(1)
        sc.op("sp", lambda e: e.dma_start(out=hal, in_=hgout[j].rearrange("(i p) c -> p i c", p=128)),
              reads=["hgout"], writes=["hal"], slot="hal")
        for uc in range(8):
            halv = hal.rearrange("p i (k c) -> p i k c", k=8)
            sc.op("dve", lambda e, uc=uc, halv=halv: e.tensor_scalar(
                out=U[0][:, uc, 0:16], in0=halv[:, 0, uc, :], scalar1=sel[:, 0:1], scalar2=None, op0=ALU.mult),
                reads=["hal", "consts"], writes=[("U", 0, uc)])
            for i in range(1, 4):
                sc.op("dve", lambda e, uc=uc, i=i, halv=halv: e.scalar_tensor_tensor(
                    out=U[0][:, uc, 0:16], in0=halv[:, i, uc, :], scalar=sel[:, i:i + 1], in1=U[0][:, uc, 0:16],
                    op0=ALU.mult, op1=ALU.add),
                    reads=["hal", "consts", ("U", 0, uc)], writes=[("U", 0, uc)])
        pool_and_mix(0)

    def attention(layer):
        j = layer // 2
        sc.barrier(("pe", "act", "dve", "sp"))
        hTi = [av(8192, [8, 1024]), av(16384, [8, 1024])]
        wqb = av(24576, [4, 8 * 128])
        wvb = av(28672, [8, 256])
        qst = av(30720, [2, 512])
        vst = av(31744, [4, 256])
        sqh = av(32768, [2, 512])
        lnv = av(O_LN, [512], F32)
        rstd = av(O_RS, [512], F32)
        sc.op("pool", lambda e: [e.dma_start(out=wqb[:, qc, :], in_=wqk_d[j, qc].rearrange("p k c -> p (k c)")) for qc in range(4)]
              + [e.dma_start(out=wvb, in_=wv_d[j])],
              writes=["wqb", "wvb"], slot="wqv", ndma=5, after_barrier=True)
        mix_prenorm(layer, 1)
        hgv = hgout_a[j].rearrange("(c i k2 p) t -> c i p k2 t", c=8, i=4, k2=2)
        minev = mine[j].rearrange("(i r) c -> i r c", i=4)
        cq = 0
        cv = 0
        it = 0

        def hload(it_):
            half_, i_ = it_ // 4, it_ % 4
            hb_ = it_ % 2
            sc.op("pool", lambda e: [e.dma_start(out=hTi[hb_][:, 2 * q:2 * q + 2, :], in_=hgv[half_ * 4 + q, i_]) for q in range(4)],
                  reads=[("hgout_a", half_)], writes=[("hTi", hb_)], slot=f"hld{hb_}", ndma=4)
        hload(0)
        for half in range(2):
            for i in range(4):
                hb = it % 2
                it += 1
                hX = hTi[hb]
                if it < 8:
                    hload(it)
                PB = (0, 1, 6, 7)
                groups = [(qc, tt) for qc in range(4) for tt in range(2)]

                def g_mm(qc, tt, pbk, hX=hX, hb=hb):
                    wv_ = wqb[:, qc, :].rearrange("p (k c) -> p k c", k=8)
                    sc.op("pe", lambda e: mm_group(
                        e, ps[:, pbk, :], [(wv_[:, k, :], hX[:, k, tt * 512:(tt + 1) * 512]) for k in range(8)]),
                        reads=["wqb", ("hTi", hb)], writes=[("ps", pbk)])

                def g_rest(qc, tt, pbk, gb, i=i, half=half):
                    isk = qc // 2
                    c2 = qc % 2
                    sc.op("act", lambda e: e.activation(out=sqh[:, gb, :], in_=ps[:, pbk, :], func=AF.Square),
                          reads=[("ps", pbk)], writes=[("sqh", gb)])
                    sc.op("pe", lambda e: mm_group(e, ps[:, 2 + gb, :], [(cmat[:, ONES_HD, :], sqh[:, gb, :])]),
                          reads=[("sqh", gb), "consts"], writes=[("ps", 2 + gb)])
                    sc.op("act", lambda e: e.activation(out=lnv, in_=ps[:, 2 + gb, :], func=AF.Ln, bias=EPS, scale=1.0),
                          reads=[("ps", 2 + gb)], writes=["lnv"])
                    sc.op("act", lambda e: e.activation(out=rstd, in_=lnv, func=AF.Exp, scale=-0.5),
                          reads=["lnv"], writes=["rstd"])
                    gsc = qkg[:, 1, j:j + 1] if isk else qg8[:, j:j + 1]
                    sc.op("dve", lambda e: e.scalar_tensor_tensor(
                        out=qst[:, gb, :], in0=ps[:, pbk, :], scalar=gsc, in1=rstd, op0=ALU.mult, op1=ALU.mult),
                        reads=[("ps", pbk), "rstd", "consts", "qg8"], writes=[("qst", gb)])
                    r0 = isk * 256 + c2 * 128
                    col = half * 1024 + tt * 512
                    sc.op("sp", lambda e: e.dma_start(out=minev[i, r0:r0 + 128, col:col + 512], in_=qst[:, gb, :]),
                          reads=[("qst", gb)], writes=["mine"], slot=f"qst{gb}")

                idx = [cq + n for n in range(len(groups))]
                cq += len(groups)
                g_mm(*groups[0], PB[idx[0] % 4])
                for gi in range(len(groups)):
                    if gi + 1 < len(groups):
                        g_mm(*groups[gi + 1], PB[idx[gi + 1] % 4])
                    g_rest(*groups[gi], PB[idx[gi] % 4], idx[gi] % 2)
                vreg = minev[i, 512:768, :].rearrange("r (t8 c) -> (r t8) c", c=256)
                for tb in range(8):
                    gb = cv % 2
                    vb = cv % 4
                    cv += 1
                    sc.op("pe", lambda e, tb=tb, gb=gb, hX=hX: mm_group(
                        e, ps[:, 4 + gb, 0:256], [(hX[:, k, tb * 128:(tb + 1) * 128], wvb[:, k, :]) for k in range(8)]),
                        reads=["wvb", ("hTi", hb)], writes=[("ps", 4 + gb)])
                    sc.op("act", lambda e, gb=gb, vb=vb: e.activation(out=vst[:, vb, :], in_=ps[:, 4 + gb, 0:256], func=AF.Copy),
                          reads=[("ps", 4 + gb)], writes=[("vst", vb)])
                    tok0 = half * 1024 + tb * 128
                    sc.op("sp", lambda e, vb=vb, tok0=tok0, vreg=vreg: e.dma_start(out=vreg[tok0:tok0 + 128, :], in_=vst[:, vb, :]),
                          reads=[("vst", vb)], writes=["mine"], slot=f"vst{vb}")
        sc.barrier(("pe", "act", "dve", "sp"))

        Kst = av(0, [2, S])
        Vp = [av(16384, [64, 128]), av(24576, [64, 128])]
        E = av(32768, [2, 1024], F32)
        P = av(36864, [3, 1024])
        A = av(39936, [2, 1024])
        Qd = av(41984, [2, 1024])
        Qz = av(44032, [2, 1024])
        Osb = av(46080, [2, 512])
        minev = mine[j].rearrange("(i r) c -> i r c", i=4)
        minevv = mine[j].rearrange("(i r) (t8 c) -> i (r t8) c", i=4, c=256)[:, 4096:6144, :].rearrange(
            "i (b s) c -> i s b c", s=128)

        def mysl(e):
            return e.partition_id() % 4

        def ag_o(hp_, tc):
            sc.op("pool", lambda e: e.collective_compute("AllGather", ALU.bypass, replica_groups=[[0, 1, 2, 3], [4, 5, 6, 7]],
                                                         ins=[ogin[j][(tc * 2 + hp_) * 128:(tc * 2 + hp_ + 1) * 128, :]],
                                                         outs=[ogout[j][(tc * 2 + hp_) * 512:(tc * 2 + hp_ + 1) * 512, :]]),
                  reads=[("ogin", hp_, tc)], writes=[("ogout", hp_, tc)], slot="cc_o", ndma=1)

        for hpi, hp in enumerate((0, 1)):
            if hpi >= 1:
                sc.barrier(("pe", "act", "dve", "sp"))
                for tc_ in range(4):
                    ag_o(0, tc_)
            sc.op("dve", lambda e: e.memset(arena[:, 16384:32768], 0.0), writes=["Vp"])
            sc.op("dve", lambda e: e.memset(arena[:, 44032:46080], 0.0), writes=["Q2z", ("Q2", 0), ("Q2", 1)])
            sc.op("dve", lambda e: e.memset(Kst[64:128, 0, S - 128:S], 0.0), writes=["Kst"])
            sc.op("dve", lambda e: e.memset(Kst[64:128, 1, S - 128:S], 0.0), writes=["Kst"])

            def ldk(e, hp=hp):
                r = []
                for i in range(4):
                    for h in range(2):
                        rr = 256 + hp * 128 + h * 64
                        src = minev[i, rr:rr + 64, :]
                        r.append(e.dma_start(out=Kst[0:64, h, i * 2048:(i + 1) * 2048], in_=src))
                        if i == 0:
                            r.append(e.dma_start(out=Kst[64:128, h, 0:1920], in_=src[:, 128:2048]))
                        else:
                            r.append(e.dma_start(out=Kst[64:128, h, i * 2048 - 128:(i + 1) * 2048 - 128], in_=src))
                return r
            sc.op("sp", ldk, reads=["mine"], writes=["Kst"], slot="kld", ndma=16)

            def ldv(e, hp=hp):
                r = []
                for i in range(4):
                    for h in range(2):
                        c0 = hp * 128 + h * 64
                        for q4 in range(4):
                            src = minevv[i, :, q4 * 4:q4 * 4 + 4, c0:c0 + 64]
                            r.append(e.dma_start(out=Vp[h][:, i * 16 + q4 * 4:i * 16 + q4 * 4 + 4, h * 64:(h + 1) * 64], in_=src))
                return r
            sc.op("sp", ldv, reads=["mine"], writes=["Vp"], slot="vld", ndma=32)
            sc.op("dve", lambda e: e.tensor_scalar(out=Kst[64:128, :, 0:S - 128], in0=Kst[64:128, :, 0:S - 128],
                                                   scalar1=-1.0, scalar2=None, op0=ALU.mult),
                  reads=["Kst"], writes=["Kst"])

            steps = [(qt, kb) for qt in range(16) for kb in range(4 * qt + 3, -1, -1)]
            ns = len(steps)

            def ldq(qt, hp=hp):
                qb = qt % 2

                def fn(e):
                    i = qt // 4
                    c0 = (qt % 4) * 512
                    rr = hp * 128
                    src = minev[i, rr:rr + 128, c0:c0 + 512].rearrange("(h d) c -> d h c", h=2)
                    return [e.dma_start(out=Qd[0:64, qb, :].rearrange("p (h c) -> p h c", h=2), in_=src),
                            e.dma_start(out=Qd[64:128, qb, :].rearrange("p (h c) -> p h c", h=2), in_=src),
                            e.dma_start(out=Qz[0:64, qb, :].rearrange("p (h c) -> p h c", h=2), in_=src)]
                sc.op("sp", fn, reads=["mine"], writes=[("Q2", qb)], slot=f"q2{qb}", ndma=3)

            def stA(s):
                qt, kb = steps[s]
                zb, qb = s % 2, qt % 2
                c0 = max(0, (kb - 4 * qt)) * 128

                def fn(e):
                    r = None
                    for h in range(2):
                        q = Qz[:, qb, h * 512 + c0:(h + 1) * 512]
                        r = mm_group(e, ps[:, 2 * zb + h, c0:512], [(Kst[:, h, kb * 128:(kb + 1) * 128], q)])
                    return r
                sc.op("pe", fn, reads=["Kst", ("Q2", qb), "Q2z"], writes=[("Z", zb)])

            def stS1(s):
                qt, kb = steps[s]
                zb = s % 2
                p3 = s % 3
                c0 = max(0, (kb - 4 * qt)) * 128
                Zv = ps[:, 2 * zb:2 * zb + 2, c0:512]
                Ev = E[:, zb, :].rearrange("p (h c) -> p h c", h=2)[:, :, c0:512]
                Pv = P[:, p3, :].rearrange("p (h c) -> p h c", h=2)
                if c0 > 0:
                    sc.op("dve", lambda e: e.memset(Pv[:, :, 0:c0], 0.0), writes=[("P", p3)])
                sc.op("act", lambda e: e.activation(out=Ev, in_=Zv, func=AF.Exp),
                      reads=[("Z", zb)], writes=[("E", zb)])
                sc.op("act", lambda e: e.activation(out=Pv[:, :, c0:512], in_=Ev, func=AF.Ln, bias=1.0, scale=1.0),
                      reads=[("E", zb)], writes=[("P", p3)])
                if kb >= 4 * qt:
                    sc.op("dve", lambda e: [e.tensor_tensor(out=Pv[:, h, c0:c0 + 128], in0=Pv[:, h, c0:c0 + 128],
                                                            in1=cmat[:, TRI01, :], op=ALU.mult) for h in range(2)],
                          reads=["consts", ("P", p3)], writes=[("P", p3)])

            def stB(s):
                qt, kb = steps[s]
                pb, qb = s % 3, qt % 2
                first = kb == 4 * qt + 3
                diag = kb >= 4 * qt
                i = kb - 4 * qt

                def fn(e):
                    r = None
                    for h in range(2):
                        q = (Qz if first else Qd)[:, qb, h * 512:(h + 1) * 512]
                        pairs = [(Kst[:, h, kb * 128:(kb + 1) * 128], q),
                                 (cmat[:, NEGTRI, :], P[:, pb, h * 512:(h + 1) * 512])]
                        r = mm_group(e, ps[:, 4 + h, :], pairs, start=first)
                    return r
                sc.op("pe", fn, reads=["Kst", ("Q2", qb), "Q2z", ("P", pb), "consts"], writes=["B"])

            def stS2(s):
                qt, kb = steps[s]
                ab = s % 2
                c0 = max(0, (kb - 4 * qt)) * 128
                Av = A[:, ab, :].rearrange("p (h c) -> p h c", h=2)
                if c0 > 0:
                    sc.op("dve", lambda e: e.memset(Av[:, :, 0:c0], 0.0), writes=[("A", ab)])
                sc.op("act", lambda e: e.activation(out=Av[:, :, c0:512], in_=ps[:, 4:6, c0:512], func=AF.Exp),
                      reads=["B"], writes=[("A", ab)])
                if kb >= 4 * qt:
                    sc.op("dve", lambda e: [e.tensor_tensor(out=Av[:, h, c0:c0 + 128], in0=Av[:, h, c0:c0 + 128],
                                                            in1=cmat[:, TRI01, :], op=ALU.mult) for h in range(2)],
                          reads=["consts", ("A", ab)], writes=[("A", ab)])

            def stC1(s):
                qt, kb = steps[s]
                pb = s % 3
                last = kb == 0
                diag = kb >= 4 * qt
                i = kb - 4 * qt
                if last:
                    return

                def fn(e):
                    r = None
                    for h in range(2):
                        pairs = [(cmat[:, NEGREST, :], P[:, pb, h * 512:(h + 1) * 512])]
                        r = mm_group(e, ps[:, 4 + h, :], pairs, start=False)
                    return r
                sc.op("pe", fn, reads=[("P", pb), "consts"], writes=["B"])

            def stPV(s, hp=hp):
                qt, kb = steps[s]
                ab, ob = s % 2, qt % 2
                first = kb == 4 * qt + 3
                last = kb == 0

                def fn(e):
                    pairs = [(Vp[h][:, kb, :], A[:, ab, h * 512:(h + 1) * 512]) for h in range(2)]
                    return mm_group(e, ps[:, 6 + ob, :], pairs, start=first, stop=last)
                sc.op("pe", fn, reads=[("A", ab), "Vp"], writes=[("O", ob)])
                if last:
                    sc.op("dve", lambda e: e.tensor_copy(out=Osb[:, ob, :], in_=ps[:, 6 + ob, :]),
                          reads=[("O", ob)], writes=[("Osb", ob)])
                    sc.op("sp", lambda e: e.dma_start(out=ogin[j][(qt // 4) * 256 + hp * 128:(qt // 4) * 256 + (hp + 1) * 128, (qt % 4) * 512:(qt % 4 + 1) * 512], in_=Osb[:, ob, :]),
                          reads=[("Osb", ob)], writes=[("ogin", hp, qt // 4)], slot=f"osb{ob}")
                    if qt % 4 == 3 and hp == 1:
                        ag_o(hp, qt // 4)

            def doA(s):
                if s > 0 and steps[s][0] != steps[s - 1][0]:
                    ldq(steps[s][0])
                stA(s)

            ldq(0)
            doA(0)
            doA(1)
            stS1(0)
            for s in range(ns):
                if s + 1 < ns:
                    stS1(s + 1)
                if s >= 1:
                    stC1(s - 1)
                stB(s)
                stS2(s)
                if s >= 1:
                    stPV(s - 1)
                if s + 2 < ns:
                    doA(s + 2)
            stPV(ns - 1)

        sc.barrier(("pe", "act", "dve", "sp"))
        oT = av(0, [8, T])
        if debug and layer == 0:
            sc.op("sp", lambda e: [e.dma_start(out=dbg_mine, in_=mine[0]), e.dma_start(out=dbg_ogin, in_=ogin[0])],
                  reads=["mine"] + [("ogin", h_, t_) for h_ in range(2) for t_ in range(4)], writes=["dbg"], slot="dbg", ndma=2)

        def ldo(e):
            ov = ogout[j].rearrange("(tc rh i p) t -> tc rh p i t", tc=4, rh=2, i=4)
            g = bass.ds(mysl(e), 1)
            o4 = oT.rearrange("p (i k2) t -> p i k2 t", k2=2)
            return [e.dma_start(out=o4[:, :, rh, :], in_=ov[g, rh, :, :, :].rearrange("o p i t -> (o p) i t")) for rh in range(2)]
        sc.op("pool", ldo, reads=[("ogout", h_, t_) for h_ in range(2) for t_ in range(4)], writes=["oT"], slot="oT", ndma=2)
        cnt = 0
        for dc in range(8):
            b, wkey = load_w8(wo_d[j, dc])
            wv_ = w8[:, b, :].rearrange("p (k c) -> p k c", k=8)
            for gt in range(4):
                db = 4 + cnt % 2
                cnt += 1
                sc.op("pe", lambda e, wv_=wv_, gt=gt, db=db: mm_group(
                    e, ps[:, db, :], [(wv_[:, k, :], oT[:, k, gt * 512:(gt + 1) * 512]) for k in range(8)]),
                    reads=[wkey, "oT"], writes=[("ps", db)])
                sc.op("dve", lambda e, dc=dc, gt=gt, db=db: e.tensor_tensor(
                    out=xT[:, dc, gt * 512:(gt + 1) * 512], in0=ps[:, db, :], in1=xT[:, dc, gt * 512:(gt + 1) * 512], op=ALU.add),
                    reads=[("ps", db), ("x", dc, gt)], writes=[("x", dc, gt)])

    stages = []
    for l in range(DEPTH):
        stages += [("ffn", l, 0), ("mix", l), ("ffn", l, 1), ("ple", l)]
    for st in stages:
        if st[0] == "ffn":
            ffn(st[1], st[2])
        elif st[0] == "mix":
            if st[1] % 2 == 0:
                attention(st[1])
            else:
                pool_mixer(st[1])
        else:
            ple(st[1])
        if stop_after is not None and st == stop_after:
            break

    sc.barrier(("sp",))
    for k in range(8):
        sc.op("sp", lambda e, k=k: e.dma_start(out=yT_d[k * 128:(k + 1) * 128, :], in_=xT[:, k, :]),
              reads=[("x", k, t) for t in range(4)], writes=["yT"], slot="yst")
    final_tok = sc.slot("yst")

    with nc.Block() as block:
        @block.sync
        def _(e):
            sc.replay("sp", e)
            e.wait_ge(final_tok[0], final_tok[1])
            if "dbg" in sc.slots:
                e.wait_ge(sc.slots["dbg"][0], sc.slots["dbg"][1])

        @block.gpsimd
        def _(e):
            sc.replay("pool", e)

        @block.tensor
        def _(e):
            sc.replay("pe", e)

        @block.scalar
        def _(e):
            sc.replay("act", e)

        @block.vector
        def _(e):
            sc.replay("dve", e)
    es.close()
    return nc


def _bf(a):
    return np.ascontiguousarray(a.astype(ml_dtypes.bfloat16))


def host_layout(inp, nl=DEPTH):
    f = lambda a: np.ascontiguousarray(np.asarray(a, dtype=np.float32))
    sh = {}
    gu = np.stack([f(inp["w_ffn1_gu"][:nl]), f(inp["w_ffn2_gu"][:nl])], 1)
    gate = gu[..., :DFF].reshape(nl, 2, 8, 128, NF, 128)
    up = gu[..., DFF:].reshape(nl, 2, 8, 128, NF, 128)
    g2 = np.concatenate([gate.transpose(0, 1, 4, 3, 2, 5), up.transpose(0, 1, 4, 3, 2, 5)], -1)
    sh["wgu"] = np.ascontiguousarray(g2)
    dn = np.stack([f(inp["w_ffn1_down"][:nl]), f(inp["w_ffn2_down"][:nl])], 1)
    dn = dn.reshape(nl, 2, NF, 128, 8, 128).transpose(0, 1, 4, 3, 2, 5)
    sh["wdn"] = np.ascontiguousarray(dn).reshape(nl, 2, 8, 128, NF * 128)
    wqkv = f(inp["w_qkv"])
    qk = wqkv[:, :, :2048].reshape(2, 8, 128, 16, 128).transpose(0, 3, 2, 1, 4)
    sh["wqk"] = np.ascontiguousarray(qk)
    sh["wv"] = np.ascontiguousarray(wqkv[:, :, 2048:].reshape(2, 8, 128, 1024).transpose(0, 2, 1, 3))
    c8 = lambda w: np.ascontiguousarray(w.reshape(w.shape[0], 8, 128, 8, 128).transpose(0, 3, 2, 1, 4))
    sh["wo"] = c8(f(inp["w_o"]))
    sh["wpi"] = c8(f(inp["w_pool_in"]))
    sh["wpg"] = c8(f(inp["w_ple_gate"]))
    wg = f(inp["w_pool_grp"]).reshape(2, 4, 2, 128, 256).transpose(0, 3, 1, 2, 4)
    sh["wgrp"] = np.ascontiguousarray(wg).reshape(2, 128, 2048)
    sh["wpp"] = np.ascontiguousarray(f(inp["w_ple_proj"]).reshape(DEPTH, 2, 128, 1024).transpose(0, 2, 1, 3))
    gn = np.stack([f(inp["norm_ffn1"]), f(inp["norm_mix"]), f(inp["norm_ffn2"]), f(inp["norm_ple"])], 0)
    sh["gn"] = np.ascontiguousarray(gn.reshape(4, DEPTH, 8, 128).transpose(3, 0, 1, 2)).reshape(128, 128)
    qk_g = np.stack([f(inp["q_norm"]), f(inp["k_norm"])], 0)
    qk_g = np.concatenate([qk_g, qk_g], -1)
    sh["qkg"] = np.ascontiguousarray(qk_g.transpose(2, 0, 1)).reshape(128, 4)
    sh["psc"] = np.ascontiguousarray(f(inp["pool_scale"]).reshape(2, 8, 128).transpose(2, 0, 1)).reshape(128, 16)
    cm = np.zeros((128, 7, 128), np.float32)
    cm[:, 0] = np.eye(128)
    cm[:, 1] = 1.0 / 1024
    cm[:64, 2, :64] = 1.0 / 64
    cm[64:, 2, 64:] = 1.0 / 64
    jj, ss = np.meshgrid(np.arange(128), np.arange(128), indexing="ij")
    cm[:, 3] = -1.0 * (jj >= ss)
    cm[:, 4] = -1.0 * (jj < ss)
    cm[:, 5] = -np.eye(128)
    cm[:, 6] = 1.0 * (jj < ss)
    sh["cmat"] = _bf(cm.reshape(128, 896))
    mk = np.zeros((128, 4, 512), np.float32)
    for i in range(4):
        kpos = 128 * i + np.arange(128)[:, None]
        mk[:, i] = np.where(kpos >= np.arange(512)[None, :], MASKV, 0.0)
    sh["mneg"] = _bf(mk.reshape(128, 2048))
    x = f(inp["x"])
    p = f(inp["p"])
    maps = []
    for r in range(8):
        b, c = r // 4, r % 4
        m = dict(sh)
        m["xT"] = np.ascontiguousarray(x[b, c * T:(c + 1) * T, :].T)
        m["wqk"] = np.ascontiguousarray(sh["wqk"][:, [2 * c, 2 * c + 1, 8 + 2 * c, 8 + 2 * c + 1]])
        m["wv"] = np.ascontiguousarray(sh["wv"][:, :, :, 256 * c:256 * (c + 1)])
        m["pT"] = np.ascontiguousarray(p[:, b, c * T:(c + 1) * T, :].transpose(0, 2, 1))
        s = np.zeros((128, 4), np.float32)
        if c > 0:
            s[:, c - 1] = 1.0
        m["sel"] = s
        ic = np.zeros((128, 4, 16), np.float32)
        for g in range(4):
            w = 2 << g
            if c == 0:
                ic[:, g] = 1.0 / np.minimum(np.arange(16) + 1, w)
            else:
                ic[:, g] = 1.0 / w
        m["icnt"] = ic.reshape(128, 64)
        maps.append(m)
    return maps


_NC_CACHE = {}


def kernel(_stop_after=None, _debug=False, **inputs):
    if _stop_after is not None:
        nl = _stop_after[1] + 1
        maps = host_layout(inputs, nl)
        nc = build(_stop_after, nl=nl, debug=_debug)
        res = run_bass_kernel_spmd(nc, maps, core_ids=list(range(8)))
        _NC_CACHE["res"] = res
        out = np.zeros((2, S, D), np.float32)
        for r in range(8):
            b, c = r // 4, r % 4
            out[b, c * T:(c + 1) * T, :] = np.asarray(res.results[r]["yT"]).T
        return out
    maps = host_layout(inputs)
    if False:
        _NC_CACHE["nc"] = build(_stop_after)
    if "nc" not in _NC_CACHE:
        _NC_CACHE["nc"] = build()
    res = run_bass_kernel_spmd(_NC_CACHE["nc"], maps, core_ids=list(range(8)))
    out = np.zeros((2, S, D), np.float32)
    for r in range(8):
        b, c = r // 4, r % 4
        out[b, c * T:(c + 1) * T, :] = np.asarray(res.results[r]["yT"]).T
    return out
```

```python
import contextlib
import numpy as np
import ml_dtypes
import concourse.bass as bass
import concourse.mybir as mybir
from concourse.bass_utils import run_bass_kernel_spmd

F32 = mybir.dt.float32
BF16 = mybir.dt.bfloat16
AF = mybir.ActivationFunctionType
ALU = mybir.AluOpType

D = 1024
T = 2048
S = 8192
DFF = 2816
NF = 22
DEPTH = 4
EPS = 1e-6
MASKV = -128.0
ARENA = 53248
COMPUTE = ("pe", "act", "dve")


class Sched:
    def __init__(self, nc, es):
        self.nc = nc
        self.es = es
        self.prog = {e: [] for e in ("pe", "act", "dve", "pool", "sp")}
        self.esem = {e: es.enter_context(nc.semaphore("s_" + e)) for e in COMPUTE}
        self.ecnt = {e: 0 for e in COMPUTE}
        self.slots = {}
        self.waited = {e: {} for e in self.prog}
        self.lastw = {}
        self.readers = {}
        self.barrier_toks = []
        self.pending = {e: [] for e in self.prog}
        self.last_barrier = []

    def slot(self, name):
        if name not in self.slots:
            self.slots[name] = [self.es.enter_context(self.nc.semaphore("d_" + name)), 0]
        return self.slots[name]

    def barrier(self, engines=("pe", "act", "dve", "sp", "pool")):
        toks = [(self.esem[e], self.ecnt[e], e) for e in COMPUTE if self.ecnt[e] > 0]
        toks += [(s[0], s[1], None) for s in self.slots.values() if s[1] > 0]
        self.last_barrier = list(toks)
        for e in engines:
            self.pending[e] = list(toks)

    def op(self, eng, fn, reads=(), writes=(), slot=None, ndma=1, after_barrier=False):
        toks = list(self.pending[eng])
        self.pending[eng] = []
        if after_barrier:
            toks += self.last_barrier
        for k in reads:
            if k in self.lastw:
                toks.append(self.lastw[k])
        for k in writes:
            if k in self.lastw:
                toks.append(self.lastw[k])
            toks.extend(self.readers.get(k, ()))
        waits = {}
        for (sem, val, src) in toks:
            if eng == "pe" and src == "pe":
                continue
            key = id(sem)
            if key not in waits or waits[key][1] < val:
                waits[key] = (sem, val)
        wl = []
        for key, (sem, val) in waits.items():
            if self.waited[eng].get(key, 0) >= val:
                continue
            self.waited[eng][key] = val
            wl.append((sem, val))
        if eng in COMPUTE:
            self.ecnt[eng] += 1
            tok = (self.esem[eng], self.ecnt[eng], eng)
            inc = (self.esem[eng], 1)
        else:
            sl = self.slot(slot)
            step = 1 if slot.startswith("cc_") else 16
            sl[1] += step * ndma
            tok = (sl[0], sl[1], None)
            inc = (sl[0], step)
        self.prog[eng].append((wl, fn, inc))
        for k in writes:
            self.lastw[k] = tok
            self.readers[k] = []
        for k in reads:
            self.readers.setdefault(k, []).append(tok)
        return tok

    def replay(self, eng_name, eng):
        compute = eng_name in COMPUTE
        for (wl, fn, inc) in self.prog[eng_name]:
            for (sem, val) in wl:
                eng.wait_ge(sem, val)
            r = fn(eng)
            if r is None:
                continue
            if not isinstance(r, (list, tuple)):
                r = [r]
            if compute:
                r[-1].then_inc(inc[0], inc[1])
            else:
                for ins in r:
                    ins.then_inc(inc[0], inc[1])


def build(stop_after=None, nl=DEPTH, debug=False):
    nc = bass.Bass("TRN2", target_bir_lowering=False)
    es = contextlib.ExitStack()

    def din(name, shape, dt=F32):
        return nc.dram_tensor(name, list(shape), dt, kind="ExternalInput").ap()

    xT_d = din("xT", [D, T])
    pT_d = din("pT", [DEPTH, 256, T])
    wgu_d = din("wgu", [nl, 2, NF, 128, 8, 256])
    wdn_d = din("wdn", [nl, 2, 8, 128, NF * 128])
    wqk_d = din("wqk", [2, 4, 128, 8, 128])
    wv_d = din("wv", [2, 128, 8, 256])
    wo_d = din("wo", [2, 8, 128, 8, 128])
    wpi_d = din("wpi", [2, 8, 128, 8, 128])
    wpg_d = din("wpg", [DEPTH, 8, 128, 8, 128])
    wgrp_d = din("wgrp", [2, 128, 4 * 2 * 256])
    wpp_d = din("wpp", [DEPTH, 128, 2, 1024])
    gn_d = din("gn", [128, 4 * DEPTH * 8])
    qkg_d = din("qkg", [128, 4])
    psc_d = din("psc", [128, 16])
    sel_d = din("sel", [128, 4])
    icnt_d = din("icnt", [128, 64])
    cmat_d = din("cmat", [128, 7 * 128], BF16)
    mneg_d = din("mneg", [128, 4 * 512], BF16)
    yT_d = nc.dram_tensor("yT", [D, T], F32, kind="ExternalOutput").ap()
    if debug:
        dbg_mine = nc.dram_tensor("dbg_mine", [4 * 768, 2048], BF16, kind="ExternalOutput").ap()
        dbg_ogin = nc.dram_tensor("dbg_ogin", [1024, 2048], BF16, kind="ExternalOutput").ap()

    hgin_a = [nc.dram_tensor(f"hgina{j}", [2048, 1024], BF16, kind="Internal").ap() for j in range(2)]
    hgout_a = [nc.dram_tensor(f"hgouta{j}", [4 * 2048, 1024], BF16, kind="Internal").ap() for j in range(2)]
    ogin = [nc.dram_tensor(f"ogin{j}", [1024, 2048], BF16, kind="Internal").ap() for j in range(2)]
    ogout = [nc.dram_tensor(f"ogout{j}", [4096, 2048], BF16, kind="Internal").ap() for j in range(2)]
    mine = [nc.dram_tensor(f"mine{j}", [4 * 768, 2048], BF16, kind="Internal").ap() for j in range(2)]
    hgin = [nc.dram_tensor(f"hgin{j}", [128, 128], F32, kind="Internal").ap() for j in range(2)]
    hgout = [nc.dram_tensor(f"hgout{j}", [4 * 128, 128], F32, kind="Internal").ap() for j in range(2)]

    def sb(name, shape, dt):
        return es.enter_context(nc.sbuf_tensor(name, list(shape), dt))

    xT = sb("xT_sb", [128, 8, T], F32)
    arena = sb("arena", [128, ARENA], BF16)
    wgu = sb("wgu_sb", [128, 2, 8 * 256], BF16)
    wdn = sb("wdn_sb", [128, 2, NF * 128], BF16)
    w8 = sb("w8_sb", [128, 3, 8 * 128], BF16)
    cmat = sb("cmat_sb", [128, 7, 128], BF16)
    mneg = sb("mneg_sb", [128, 4, 512], BF16)
    gn = sb("gn_sb", [128, 4, DEPTH, 8], F32)
    qkg = sb("qkg_sb", [128, 2, 2], F32)
    qg8 = sb("qg8_sb", [128, 2], F32)
    psc = sb("psc_sb", [128, 2, 8], F32)
    sel = sb("sel_sb", [128, 4], F32)
    icnt = sb("icnt_sb", [128, 4, 16], F32)
    wgrp = sb("wgrp_sb", [128, 4, 2, 256], BF16)
    ps = es.enter_context(nc.psum_tensor("ps", [128, 8, 512], F32))

    IDENT, ONES_MS, ONES_HD, NEGTRI, NEGREST, NEGIDENT, TRI01 = range(7)

    def av(off, shape, dt=BF16):
        n = int(np.prod(shape))
        if dt == F32:
            v = arena[:, off:off + 2 * n].bitcast(F32)
        else:
            v = arena[:, off:off + n]
        if len(shape) == 1:
            return v
        if len(shape) == 2:
            return v.rearrange("p (a b) -> p a b", a=shape[0])
        return v.rearrange("p (a b c) -> p a b c", a=shape[0], b=shape[1])

    sc = Sched(nc, es)
    me4 = {}

    def ld_consts(e):
        return [
            e.dma_start(out=cmat[:].rearrange("p a b -> p (a b)"), in_=cmat_d),
            e.dma_start(out=mneg[:].rearrange("p a b -> p (a b)"), in_=mneg_d),
            e.dma_start(out=gn[:].rearrange("p a b c -> p (a b c)"), in_=gn_d),
            e.dma_start(out=qkg[:].rearrange("p a b -> p (a b)"), in_=qkg_d),
            e.dma_start(out=psc[:].rearrange("p a b -> p (a b)"), in_=psc_d),
            e.dma_start(out=sel[:], in_=sel_d),
            e.dma_start(out=icnt[:].rearrange("p a b -> p (a b)"), in_=icnt_d),
        ]
    sc.op("sp", ld_consts, writes=["consts"], slot="consts", ndma=7)
    for k in range(8):
        sc.op("sp", lambda e, k=k: e.dma_start(out=xT[:, k, :], in_=xT_d[k * 128:(k + 1) * 128, :]),
              writes=[("x", k, t) for t in range(4)], slot=f"xld{k}")
    sc.op("dve", lambda e: e.tensor_scalar(out=qg8[:], in0=qkg[:, 0, :], scalar1=0.125, scalar2=None, op0=ALU.mult),
          reads=["consts"], writes=["qg8"])

    ring_cnt = {"wgu": 0, "wdn": 0, "w8": 0}

    def load_w(kind, src_ap, nbuf, view):
        i = ring_cnt[kind]
        ring_cnt[kind] += 1
        b = i % nbuf
        key = (kind, b)
        sc.op("pool", lambda e: e.dma_start(out=view(b), in_=src_ap), writes=[key], slot=f"{kind}{b}")
        return b, key

    def load_wgu(l, which, f):
        return load_w("wgu", wgu_d[l, which, f].rearrange("p k c -> p (k c)"), 2, lambda b: wgu[:, b, :])

    def load_wdn(l, which, dc):
        i = ring_cnt["wdn"]
        ring_cnt["wdn"] += 1
        b = i % 2
        key = ("wdn", b)
        h = NF * 64

        def fn(e):
            return [e.dma_start(out=wdn[:, b, 0:h], in_=wdn_d[l, which, dc, :, 0:h]),
                    e.dma_start(out=wdn[:, b, h:2 * h], in_=wdn_d[l, which, dc, :, h:2 * h])]
        sc.op("pool", fn, writes=[key], slot=f"wdn{b}", ndma=2)
        return b, key

    def load_w8(src3):
        return load_w("w8", src3.rearrange("p k c -> p (k c)"), 3, lambda b: w8[:, b, :])

    O_HT = 0
    O_SQ, O_LN, O_RS, O_SG = 44032, 48128, 49152, 50176

    def mm_group(e, out, pairs, start=True, stop=True):
        r = None
        n = len(pairs)
        for i, (l, rh) in enumerate(pairs):
            r = e.matmul(out, lhsT=l, rhs=rh, start=(start and i == 0), stop=(stop and i == n - 1),
                         skip_group_check=True)
        return r

    def norm_half(half, kind, layer, ssbank=6, hoff=0, hkey="hT"):
        hT = av(hoff, [8, 1024])
        sq = av(O_SQ, [8, 512])
        lnv = av(O_LN, [512], F32)
        rstd = av(O_RS, [512], F32)
        for tt in range(2):
            gt = half * 2 + tt
            c0 = gt * 512
            sc.op("act", lambda e, c0=c0: e.activation(out=sq, in_=xT[:, :, c0:c0 + 512], func=AF.Square),
                  reads=[("x", k, gt) for k in range(8)], writes=["sq"])
            sc.op("pe", lambda e: mm_group(e, ps[:, ssbank, :], [(cmat[:, ONES_MS, :], sq[:, k, :]) for k in range(8)]),
                  reads=["sq", "consts"], writes=[("ps", ssbank)])
            sc.op("act", lambda e: e.activation(out=lnv, in_=ps[:, ssbank, :], func=AF.Ln, bias=EPS, scale=1.0),
                  reads=[("ps", ssbank)], writes=["lnv"])
            sc.op("act", lambda e: e.activation(out=rstd, in_=lnv, func=AF.Exp, scale=-0.5),
                  reads=["lnv"], writes=["rstd"])
            for k in range(8):
                sc.op("dve", lambda e, k=k, c0=c0, tt=tt: e.scalar_tensor_tensor(
                    out=hT[:, k, tt * 512:(tt + 1) * 512], in0=xT[:, k, c0:c0 + 512],
                    scalar=gn[:, kind, layer, k:k + 1], in1=rstd, op0=ALU.mult, op1=ALU.mult),
                    reads=[("x", k, gt), "rstd", "consts"], writes=[(hkey, k, tt)])
        return hT

    def mix_prenorm(layer, half, defer=False):
        j = layer // 2
        hT = norm_half(half, 1, layer)
        hv = hgin_a[j].rearrange("(h k p) t -> h p k t", h=2, p=128)
        sc.op("sp", lambda e: e.dma_start(out=hv[half], in_=hT),
              reads=[("hT", k, tt) for k in range(8) for tt in range(2)], writes=[("hgin_a", half)], slot="hgst")
        def emit_ag():
            sc.op("pool", lambda e: [e.collective_compute("AllGather", ALU.bypass, replica_groups=[[0, 1, 2, 3], [4, 5, 6, 7]],
                                                          ins=[hgin_a[j][(half * 4 + q) * 256:(half * 4 + q + 1) * 256, :]],
                                                          outs=[hgout_a[j][(half * 4 + q) * 1024:(half * 4 + q + 1) * 1024, :]])
                                     for q in range(4)],
                  reads=[("hgin_a", half)], writes=[("hgout_a", half)], slot="cc_a", ndma=4)
        if defer:
            return emit_ag
        emit_ag()

    def ffn(layer, which):
        sc.barrier(("pe", "act", "dve", "sp"))
        kind = 0 if which == 0 else 2
        aT = av(8192, [NF, 1024])
        sgt = av(O_SG, [2, 512], F32)
        hTs = {0: norm_half(0, kind, layer)}
        deferred = []
        want_pre = False
        for half in range(2):
            hT = hTs[half]
            hkey = "hT" if half == 0 else "hT2"
            cnt = 0
            for f in range(NF):
                if half == 0 and f == NF // 2:
                    hTs[1] = norm_half(1, kind, layer, hoff=30720, hkey="hT2")
                b, wkey = load_wgu(layer, which, f)
                if half == 1 and f == 2 and want_pre:
                    deferred.append(mix_prenorm(layer, 0, defer=True))
                if half == 1 and f == 5 and deferred:
                    deferred.pop()()
                for tt in range(2):
                    gb = cnt % 2
                    cnt += 1
                    wv_ = wgu[:, b, :].rearrange("p (k c) -> p k c", k=8)
                    sc.op("pe", lambda e, wv_=wv_, tt=tt, gb=gb, hT=hT: [
                        mm_group(e, ps[:, gb, :], [(wv_[:, k, 0:128], hT[:, k, tt * 512:(tt + 1) * 512]) for k in range(8)]),
                        mm_group(e, ps[:, 2 + gb, :], [(wv_[:, k, 128:256], hT[:, k, tt * 512:(tt + 1) * 512]) for k in range(8)])],
                        reads=[wkey] + [(hkey, k, tt) for k in range(8)], writes=[("ps", gb), ("ps", 2 + gb)])
                    sc.op("act", lambda e, gb=gb: e.activation(out=sgt[:, gb, :], in_=ps[:, gb, :], func=AF.Silu),
                          reads=[("ps", gb)], writes=[("sgt", gb)])
                    sc.op("dve", lambda e, gb=gb, f=f, tt=tt: e.tensor_tensor(
                        out=aT[:, f, tt * 512:(tt + 1) * 512], in0=sgt[:, gb, :], in1=ps[:, 2 + gb, :], op=ALU.mult),
                        reads=[("sgt", gb), ("ps", 2 + gb)], writes=[("aT", f, tt)])
            cnt = 0
            for dc in range(8):
                b, wkey = load_wdn(layer, which, dc)
                wv_ = wdn[:, b, :].rearrange("p (f c) -> p f c", f=NF)
                for tt in range(2):
                    db = 4 + cnt % 2
                    cnt += 1
                    gt = half * 2 + tt
                    sc.op("pe", lambda e, wv_=wv_, tt=tt, db=db: mm_group(
                        e, ps[:, db, :], [(wv_[:, f, :], aT[:, f, tt * 512:(tt + 1) * 512]) for f in range(NF)]),
                        reads=[wkey] + [("aT", f, tt) for f in range(NF)], writes=[("ps", db)])
                    sc.op("dve", lambda e, dc=dc, gt=gt, db=db: e.scalar_tensor_tensor(
                        out=xT[:, dc, gt * 512:(gt + 1) * 512], in0=ps[:, db, :], scalar=0.5,
                        in1=xT[:, dc, gt * 512:(gt + 1) * 512], op0=ALU.mult, op1=ALU.add),
                        reads=[("ps", db), ("x", dc, gt)], writes=[("x", dc, gt)])
            if half == 0 and which == 0 and layer % 2 == 0:
                want_pre = True

    def ple(layer):
        sc.barrier(("pe", "act", "dve", "sp"))
        pTb = av(8192, [2, T])
        wpp = av(12288, [2, 1024])
        sgm = av(14336, [2, 512], F32)
        tmp = av(16384, [2, 512], F32)
        sc.op("pool", lambda e: [e.dma_start(out=pTb, in_=pT_d[layer].rearrange("(k p) t -> p k t", p=128)),
                                 e.dma_start(out=wpp, in_=wpp_d[layer])],
              writes=["pTb", "wpp"], slot="plew", ndma=2, after_barrier=True)
        cnt = 0
        for half in range(2):
            hT = norm_half(half, 3, layer)
            for dc in range(8):
                b, wkey = load_w8(wpg_d[layer, dc])
                wv_ = w8[:, b, :].rearrange("p (k c) -> p k c", k=8)
                for tt in range(2):
                    gb = cnt % 2
                    cnt += 1
                    gt = half * 2 + tt
                    sc.op("pe", lambda e, wv_=wv_, tt=tt, gb=gb, dc=dc, gt=gt: [
                        mm_group(e, ps[:, gb, :], [(wv_[:, k, :], hT[:, k, tt * 512:(tt + 1) * 512]) for k in range(8)]),
                        mm_group(e, ps[:, 2 + gb, :], [(wpp[:, k, dc * 128:(dc + 1) * 128], pTb[:, k, gt * 512:(gt + 1) * 512]) for k in range(2)])],
                        reads=[wkey, "pTb", "wpp"] + [("hT", k, tt) for k in range(8)], writes=[("ps", gb), ("ps", 2 + gb)])
                    sc.op("act", lambda e, gb=gb: e.activation(out=sgm[:, gb, :], in_=ps[:, gb, :], func=AF.Sigmoid),
                          reads=[("ps", gb)], writes=[("sgm", gb)])
                    sc.op("dve", lambda e, gb=gb: e.tensor_tensor(out=tmp[:, gb, :], in0=sgm[:, gb, :], in1=ps[:, 2 + gb, :], op=ALU.mult),
                          reads=[("sgm", gb), ("ps", 2 + gb)], writes=[("tmp", gb)])
                    sc.op("dve", lambda e, gb=gb, dc=dc, gt=gt: e.tensor_tensor(
                        out=xT[:, dc, gt * 512:(gt + 1) * 512], in0=tmp[:, gb, :], in1=xT[:, dc, gt * 512:(gt + 1) * 512], op=ALU.add),
                        reads=[("tmp", gb), ("x", dc, gt)], writes=[("x", dc, gt)])

    def pool_mixer(layer):
        j = layer // 2
        sc.barrier(("pe", "act", "dve", "sp"))
        U = [av(0, [8, 1040], F32), av(16640, [8, 1040], F32)]
        O_H = 33280
        hT = av(O_H, [8, 1024])
        tmpS = av(41472, [2, 1040], F32)
        hal = av(45632, [4, 128], F32)
        sc.op("pool", lambda e: e.dma_start(out=wgrp[:].rearrange("p a b c -> p (a b c)"), in_=wgrp_d[j]),
              writes=["wgrp"], slot="wgrp")

        def do_norm(half):
            sq = av(46656, [8, 512])
            lnv = tmpS[:, 0, 0:512]
            rstd = tmpS[:, 1, 0:512]
            for tt in range(2):
                gt = half * 2 + tt
                c0 = gt * 512
                sc.op("act", lambda e, c0=c0: e.activation(out=sq, in_=xT[:, :, c0:c0 + 512], func=AF.Square),
                      reads=[("x", k, gt) for k in range(8)], writes=["sq"])
                sc.op("pe", lambda e: mm_group(e, ps[:, 6, :], [(cmat[:, ONES_MS, :], sq[:, k, :]) for k in range(8)]),
                      reads=["sq", "consts"], writes=[("ps", 6)])
                sc.op("act", lambda e: e.activation(out=lnv, in_=ps[:, 6, :], func=AF.Ln, bias=EPS, scale=1.0),
                      reads=[("ps", 6)], writes=[("tmpS", 0)])
                sc.op("act", lambda e: e.activation(out=rstd, in_=lnv, func=AF.Exp, scale=-0.5),
                      reads=[("tmpS", 0)], writes=[("tmpS", 1)])
                for k in range(8):
                    sc.op("dve", lambda e, k=k, c0=c0, tt=tt: e.scalar_tensor_tensor(
                        out=hT[:, k, tt * 512:(tt + 1) * 512], in0=xT[:, k, c0:c0 + 512],
                        scalar=gn[:, 1, layer, k:k + 1], in1=rstd, op0=ALU.mult, op1=ALU.mult),
                        reads=[("x", k, gt), ("tmpS", 1), "consts"], writes=[("hT", k, tt), ("pl", k)])

        def compute_u(half):
            cnt = 0
            for uc in range(8):
                b, wkey = load_w8(wpi_d[j, uc])
                wv_ = w8[:, b, :].rearrange("p (k c) -> p k c", k=8)
                for tt in range(2):
                    gb = cnt % 2
                    cnt += 1
                    sc.op("pe", lambda e, wv_=wv_, tt=tt, gb=gb: mm_group(
                        e, ps[:, gb, :], [(wv_[:, k, :], hT[:, k, tt * 512:(tt + 1) * 512]) for k in range(8)]),
                        reads=[wkey] + [("hT", k, tt) for k in range(8)], writes=[("ps", gb)])
                    sc.op("act", lambda e, gb=gb, uc=uc, tt=tt: e.activation(
                        out=U[half][:, uc, 16 + tt * 512:16 + (tt + 1) * 512], in_=ps[:, gb, :], func=AF.Copy),
                        reads=[("ps", gb)], writes=[("U", half, uc)])

        def pool_and_mix(half):
            pooled = hT
            for uc in range(8):
                g = uc // 2
                w = 2 << g
                cur = U[half][:, uc, :]
                lo = 0
                nsteps = g + 1
                srcbuf = cur
                for st in range(nsteps):
                    sh = 1 << st
                    lo2 = lo + sh
                    dst = tmpS[:, st % 2, :]
                    sc.op("dve", lambda e, dst=dst, srcbuf=srcbuf, lo2=lo2, sh=sh: e.tensor_tensor(
                        out=dst[:, lo2:1040], in0=srcbuf[:, lo2:1040], in1=srcbuf[:, lo2 - sh:1040 - sh], op=ALU.add),
                        reads=[("U", half, uc), ("tmpS", 0), ("tmpS", 1)], writes=[("tmpS", st % 2)])
                    srcbuf = dst
                    lo = lo2
                sfin = srcbuf
                sc.op("dve", lambda e, sfin=sfin, cur=cur, uc=uc, w=w: e.scalar_tensor_tensor(
                    out=pooled[:, uc, :], in0=sfin[:, 16:1040], scalar=1.0 / w, in1=cur[:, 16:1040],
                    op0=ALU.mult, op1=ALU.subtract),
                    reads=[("tmpS", 0), ("tmpS", 1), ("U", half, uc)],
                    writes=[("pl", uc), ("hT", uc, 0), ("hT", uc, 1)])
                if half == 0:
                    t16 = tmpS[:, (nsteps) % 2, 0:16]
                    sc.op("dve", lambda e, t16=t16, sfin=sfin, g=g: e.tensor_tensor(
                        out=t16, in0=sfin[:, 16:32], in1=icnt[:, g, :], op=ALU.mult),
                        reads=[("tmpS", 0), ("tmpS", 1), "consts"], writes=[("tmpS", nsteps % 2)])
                    sc.op("dve", lambda e, t16=t16, cur=cur, uc=uc: e.tensor_tensor(
                        out=pooled[:, uc, 0:16], in0=t16, in1=cur[:, 16:32], op=ALU.subtract),
                        reads=[("tmpS", 0), ("tmpS", 1), ("U", half, uc), ("pl", uc)], writes=[("pl", uc)])
            cnt = 0
            for g in range(4):
                for dd in range(2):
                    dc = 2 * g + dd
                    for tt in range(2):
                        db = 4 + cnt % 2
                        cnt += 1
                        gt = half * 2 + tt
                        sc.op("pe", lambda e, g=g, dd=dd, tt=tt, db=db: mm_group(
                            e, ps[:, db, :], [(wgrp[:, g, cc, dd * 128:(dd + 1) * 128], pooled[:, 2 * g + cc, tt * 512:(tt + 1) * 512]) for cc in range(2)]),
                            reads=["wgrp", ("pl", 2 * g), ("pl", 2 * g + 1)], writes=[("ps", db)])
                        sc.op("dve", lambda e, dc=dc, gt=gt, db=db: e.scalar_tensor_tensor(
                            out=xT[:, dc, gt * 512:(gt + 1) * 512], in0=ps[:, db, :], scalar=psc[:, j, dc:dc + 1],
                            in1=xT[:, dc, gt * 512:(gt + 1) * 512], op0=ALU.mult, op1=ALU.add),
                            reads=[("ps", db), ("x", dc, gt), "consts"], writes=[("x", dc, gt)])

        do_norm(1)
        compute_u(1)
        sc.op("sp", lambda e: e.dma_start(out=hgin[j].rearrange("p (k c) -> p k c", k=8), in_=U[1][:, :, 1024:1040]),
              reads=[("U", 1, uc) for uc in range(8)], writes=["hgin"], slot="hgin")
        sc.op("pool", lambda e: e.collective_compute("AllGather", ALU.bypass, replica_groups=[[0, 1, 2, 3], [4, 5, 6, 7]],
                                                     ins=[hgin[j]], outs=[hgout[j]]),
              reads=["hgin"], writes=["hgout"], slot="cc_h")
        do_norm(0)
        compute_u(0)
        for uc in range(8):
            sc.op("dve", lambda e, uc=uc: e.tensor_copy(out=U[1][:, uc, 0:16], in_=U[0][:, uc, 1024:1040]),
                  reads=[("U", 0, uc)], writes=[("U", 1, uc)])
        pool_and_mix(1)
        sc.op("sp", lambda e: e.dma_start(out=hal, in_=hgout[j].rearrange("(i p) c -> p i c", p=128)),
              reads=["hgout"], writes=["hal"], slot="hal")
        for uc in range(8):
            halv = hal.rearrange("p i (k c) -> p i k c", k=8)
            sc.op("dve", lambda e, uc=uc, halv=halv: e.tensor_scalar(
                out=U[0][:, uc, 0:16], in0=halv[:, 0, uc, :], scalar1=sel[:, 0:1], scalar2=None, op0=ALU.mult),
                reads=["hal", "consts"], writes=[("U", 0, uc)])
            for i in range(1, 4):
                sc.op("dve", lambda e, uc=uc, i=i, halv=halv: e.scalar_tensor_tensor(
                    out=U[0][:, uc, 0:16], in0=halv[:, i, uc, :], scalar=sel[:, i:i + 1], in1=U[0][:, uc, 0:16],
                    op0=ALU.mult, op1=ALU.add),
                    reads=["hal", "consts", ("U", 0, uc)], writes=[("U", 0, uc)])
        pool_and_mix(0)

    def attention(layer):
        j = layer // 2
        sc.barrier(("pe", "act", "dve", "sp"))
        hTi = [av(8192, [8, 1024]), av(16384, [8, 1024])]
        wqb = av(24576, [4, 8 * 128])
        wvb = av(28672, [8, 256])
        qst = av(30720, [2, 512])
        vst = av(31744, [4, 256])
        sqh = av(32768, [2, 512])
        lnv = av(O_LN, [512], F32)
        rstd = av(O_RS, [512], F32)
        sc.op("pool", lambda e: [e.dma_start(out=wqb[:, qc, :], in_=wqk_d[j, qc].rearrange("p k c -> p (k c)")) for qc in range(4)]
              + [e.dma_start(out=wvb, in_=wv_d[j])],
              writes=["wqb", "wvb"], slot="wqv", ndma=5, after_barrier=True)
        mix_prenorm(layer, 1)
        hgv = hgout_a[j].rearrange("(c i k2 p) t -> c i p k2 t", c=8, i=4, k2=2)
        minev = mine[j].rearrange("(i r) c -> i r c", i=4)
        cq = 0
        cv = 0
        it = 0

        def hload(it_):
            half_, i_ = it_ // 4, it_ % 4
            hb_ = it_ % 2
            sc.op("pool", lambda e: [e.dma_start(out=hTi[hb_][:, 2 * q:2 * q + 2, :], in_=hgv[half_ * 4 + q, i_]) for q in range(4)],
                  reads=[("hgout_a", half_)], writes=[("hTi", hb_)], slot=f"hld{hb_}", ndma=4)
        hload(0)
        for half in range(2):
            for i in range(4):
                hb = it % 2
                it += 1
                hX = hTi[hb]
                if it < 8:
                    hload(it)
                PB = (0, 1, 6, 7)
                groups = [(qc, tt) for qc in range(4) for tt in range(2)]

                def g_mm(qc, tt, pbk, hX=hX, hb=hb):
                    wv_ = wqb[:, qc, :].rearrange("p (k c) -> p k c", k=8)
                    sc.op("pe", lambda e: mm_group(
                        e, ps[:, pbk, :], [(wv_[:, k, :], hX[:, k, tt * 512:(tt + 1) * 512]) for k in range(8)]),
                        reads=["wqb", ("hTi", hb)], writes=[("ps", pbk)])

                def g_rest(qc, tt, pbk, gb, i=i, half=half):
                    isk = qc // 2
                    c2 = qc % 2
                    sc.op("act", lambda e: e.activation(out=sqh[:, gb, :], in_=ps[:, pbk, :], func=AF.Square),
                          reads=[("ps", pbk)], writes=[("sqh", gb)])
                    sc.op("pe", lambda e: mm_group(e, ps[:, 2 + gb, :], [(cmat[:, ONES_HD, :], sqh[:, gb, :])]),
                          reads=[("sqh", gb), "consts"], writes=[("ps", 2 + gb)])
                    sc.op("act", lambda e: e.activation(out=lnv, in_=ps[:, 2 + gb, :], func=AF.Ln, bias=EPS, scale=1.0),
                          reads=[("ps", 2 + gb)], writes=["lnv"])
                    sc.op("act", lambda e: e.activation(out=rstd, in_=lnv, func=AF.Exp, scale=-0.5),
                          reads=["lnv"], writes=["rstd"])
                    gsc = qkg[:, 1, j:j + 1] if isk else qg8[:, j:j + 1]
                    sc.op("dve", lambda e: e.scalar_tensor_tensor(
                        out=qst[:, gb, :], in0=ps[:, pbk, :], scalar=gsc, in1=rstd, op0=ALU.mult, op1=ALU.mult),
                        reads=[("ps", pbk), "rstd", "consts", "qg8"], writes=[("qst", gb)])
                    r0 = isk * 256 + c2 * 128
                    col = half * 1024 + tt * 512
                    sc.op("sp", lambda e: e.dma_start(out=minev[i, r0:r0 + 128, col:col + 512], in_=qst[:, gb, :]),
                          reads=[("qst", gb)], writes=["mine"], slot=f"qst{gb}")

                idx = [cq + n for n in range(len(groups))]
                cq += len(groups)
                g_mm(*groups[0], PB[idx[0] % 4])
                for gi in range(len(groups)):
                    if gi + 1 < len(groups):
                        g_mm(*groups[gi + 1], PB[idx[gi + 1] % 4])
                    g_rest(*groups[gi], PB[idx[gi] % 4], idx[gi] % 2)
                vreg = minev[i, 512:768, :].rearrange("r (t8 c) -> (r t8) c", c=256)
                for tb in range(8):
                    gb = cv % 2
                    vb = cv % 4
                    cv += 1
                    sc.op("pe", lambda e, tb=tb, gb=gb, hX=hX: mm_group(
                        e, ps[:, 4 + gb, 0:256], [(hX[:, k, tb * 128:(tb + 1) * 128], wvb[:, k, :]) for k in range(8)]),
                        reads=["wvb", ("hTi", hb)], writes=[("ps", 4 + gb)])
                    sc.op("act", lambda e, gb=gb, vb=vb: e.activation(out=vst[:, vb, :], in_=ps[:, 4 + gb, 0:256], func=AF.Copy),
                          reads=[("ps", 4 + gb)], writes=[("vst", vb)])
                    tok0 = half * 1024 + tb * 128
                    sc.op("sp", lambda e, vb=vb, tok0=tok0, vreg=vreg: e.dma_start(out=vreg[tok0:tok0 + 128, :], in_=vst[:, vb, :]),
                          reads=[("vst", vb)], writes=["mine"], slot=f"vst{vb}")
        sc.barrier(("pe", "act", "dve", "sp"))

        Kst = av(0, [2, S])
        Vp = [av(16384, [64, 128]), av(24576, [64, 128])]
        E = av(32768, [2, 1024], F32)
        P = av(36864, [3, 1024])
        A = av(39936, [2, 1024])
        Qd = av(41984, [2, 1024])
        Qz = av(44032, [2, 1024])
        Osb = av(46080, [2, 512])
        Pd = av(47104, [3, 1024])
        Ad = av(50176, [3, 1024])
        minev = mine[j].rearrange("(i r) c -> i r c", i=4)
        minevv = mine[j].rearrange("(i r) (t8 c) -> i (r t8) c", i=4, c=256)[:, 4096:6144, :].rearrange(
            "i (b s) c -> i s b c", s=128)

        def mysl(e):
            return e.partition_id() % 4

        def ag_o(hp_, tc):
            sc.op("pool", lambda e: e.collective_compute("AllGather", ALU.bypass, replica_groups=[[0, 1, 2, 3], [4, 5, 6, 7]],
                                                         ins=[ogin[j][(tc * 2 + hp_) * 128:(tc * 2 + hp_ + 1) * 128, :]],
                                                         outs=[ogout[j][(tc * 2 + hp_) * 512:(tc * 2 + hp_ + 1) * 512, :]]),
                  reads=[("ogin", hp_, tc)], writes=[("ogout", hp_, tc)], slot="cc_o", ndma=1)

        for hpi, hp in enumerate((0, 1)):
            if hpi >= 1:
                sc.barrier(("pe", "act", "dve", "sp"))
                for tc_ in range(4):
                    ag_o(0, tc_)
            sc.op("dve", lambda e: e.memset(arena[:, 16384:32768], 0.0), writes=["Vp"])
            sc.op("dve", lambda e: e.memset(arena[:, 44032:46080], 0.0), writes=["Q2z", ("Q2", 0), ("Q2", 1)])
            sc.op("dve", lambda e: e.memset(arena[:, 47104:53248], 0.0), writes=[("Pd", n_) for n_ in range(3)] + [("Ad", n_) for n_ in range(3)])
            sc.op("dve", lambda e: e.memset(Kst[64:128, 0, S - 128:S], 0.0), writes=["Kst"])
            sc.op("dve", lambda e: e.memset(Kst[64:128, 1, S - 128:S], 0.0), writes=["Kst"])

            def ldk(e, hp=hp):
                r = []
                for i in range(4):
                    for h in range(2):
                        rr = 256 + hp * 128 + h * 64
                        src = minev[i, rr:rr + 64, :]
                        r.append(e.dma_start(out=Kst[0:64, h, i * 2048:(i + 1) * 2048], in_=src))
                        if i == 0:
                            r.append(e.dma_start(out=Kst[64:128, h, 0:1920], in_=src[:, 128:2048]))
                        else:
                            r.append(e.dma_start(out=Kst[64:128, h, i * 2048 - 128:(i + 1) * 2048 - 128], in_=src))
                return r
            sc.op("sp", ldk, reads=["mine"], writes=["Kst"], slot="kld", ndma=16)

            def ldv(e, hp=hp):
                r = []
                for i in range(4):
                    for h in range(2):
                        c0 = hp * 128 + h * 64
                        for q4 in range(4):
                            src = minevv[i, :, q4 * 4:q4 * 4 + 4, c0:c0 + 64]
                            r.append(e.dma_start(out=Vp[h][:, i * 16 + q4 * 4:i * 16 + q4 * 4 + 4, h * 64:(h + 1) * 64], in_=src))
                return r
            sc.op("sp", ldv, reads=["mine"], writes=["Vp"], slot="vld", ndma=32)
            sc.op("dve", lambda e: e.tensor_scalar(out=Kst[64:128, :, 0:S - 128], in0=Kst[64:128, :, 0:S - 128],
                                                   scalar1=-1.0, scalar2=None, op0=ALU.mult),
                  reads=["Kst"], writes=["Kst"])

            steps = [(qt, kb) for qt in range(16) for kb in range(4 * qt + 3, -1, -1)]
            ns = len(steps)

            def ldq(qt, hp=hp):
                qb = qt % 2

                def fn(e):
                    i = qt // 4
                    c0 = (qt % 4) * 512
                    rr = hp * 128
                    src = minev[i, rr:rr + 128, c0:c0 + 512].rearrange("(h d) c -> d h c", h=2)
                    return [e.dma_start(out=Qd[0:64, qb, :].rearrange("p (h c) -> p h c", h=2), in_=src),
                            e.dma_start(out=Qd[64:128, qb, :].rearrange("p (h c) -> p h c", h=2), in_=src),
                            e.dma_start(out=Qz[0:64, qb, :].rearrange("p (h c) -> p h c", h=2), in_=src)]
                sc.op("sp", fn, reads=["mine"], writes=[("Q2", qb)], slot=f"q2{qb}", ndma=3)

            def pbuf(s):
                qt, kb = steps[s]
                i = kb - 4 * qt
                if i >= 1:
                    return Pd[:, i - 1, :], ("Pd", i - 1)
                return P[:, s % 3, :], ("P", s % 3)

            def abuf(s):
                qt, kb = steps[s]
                i = kb - 4 * qt
                if i >= 1:
                    return Ad[:, i - 1, :], ("Ad", i - 1)
                return A[:, s % 2, :], ("A", s % 2)

            def stA(s):
                qt, kb = steps[s]
                zb, qb = s % 2, qt % 2
                c0 = max(0, (kb - 4 * qt)) * 128

                def fn(e):
                    r = None
                    for h in range(2):
                        q = Qz[:, qb, h * 512 + c0:(h + 1) * 512]
                        r = mm_group(e, ps[:, 2 * zb + h, c0:512], [(Kst[:, h, kb * 128:(kb + 1) * 128], q)])
                    return r
                sc.op("pe", fn, reads=["Kst", ("Q2", qb), "Q2z"], writes=[("Z", zb)])

            def stS1(s):
                qt, kb = steps[s]
                zb = s % 2
                c0 = max(0, (kb - 4 * qt)) * 128
                Zv = ps[:, 2 * zb:2 * zb + 2, c0:512]
                Ev = E[:, zb, :].rearrange("p (h c) -> p h c", h=2)[:, :, c0:512]
                pt, pkey = pbuf(s)
                Pv = pt.rearrange("p (h c) -> p h c", h=2)
                sc.op("act", lambda e: e.activation(out=Ev, in_=Zv, func=AF.Exp),
                      reads=[("Z", zb)], writes=[("E", zb)])
                sc.op("act", lambda e: e.activation(out=Pv[:, :, c0:512], in_=Ev, func=AF.Ln, bias=1.0, scale=1.0),
                      reads=[("E", zb)], writes=[pkey])
                if kb >= 4 * qt:
                    sc.op("dve", lambda e: [e.tensor_tensor(out=Pv[:, h, c0:c0 + 128], in0=Pv[:, h, c0:c0 + 128],
                                                            in1=cmat[:, TRI01, :], op=ALU.mult) for h in range(2)],
                          reads=["consts", pkey], writes=[pkey])

            def stB(s):
                qt, kb = steps[s]
                pb, qb = s % 3, qt % 2
                first = kb == 4 * qt + 3
                diag = kb >= 4 * qt
                i = kb - 4 * qt

                pt, pkey = pbuf(s)

                def fn(e):
                    r = None
                    for h in range(2):
                        q = (Qz if first else Qd)[:, qb, h * 512:(h + 1) * 512]
                        pairs = [(Kst[:, h, kb * 128:(kb + 1) * 128], q),
                                 (cmat[:, NEGTRI, :], pt[:, h * 512:(h + 1) * 512])]
                        r = mm_group(e, ps[:, 4 + h, :], pairs, start=first)
                    return r
                sc.op("pe", fn, reads=["Kst", ("Q2", qb), "Q2z", pkey, "consts"], writes=["B"])

            def stS2(s):
                qt, kb = steps[s]
                c0 = max(0, (kb - 4 * qt)) * 128
                at, akey = abuf(s)
                Av = at.rearrange("p (h c) -> p h c", h=2)
                sc.op("act", lambda e: e.activation(out=Av[:, :, c0:512], in_=ps[:, 4:6, c0:512], func=AF.Exp),
                      reads=["B"], writes=[akey])
                if kb >= 4 * qt:
                    sc.op("dve", lambda e: [e.tensor_tensor(out=Av[:, h, c0:c0 + 128], in0=Av[:, h, c0:c0 + 128],
                                                            in1=cmat[:, TRI01, :], op=ALU.mult) for h in range(2)],
                          reads=["consts", akey], writes=[akey])

            def stC1(s):
                qt, kb = steps[s]
                pb = s % 3
                last = kb == 0
                diag = kb >= 4 * qt
                i = kb - 4 * qt
                if last:
                    return

                pt, pkey = pbuf(s)

                def fn(e):
                    r = None
                    for h in range(2):
                        pairs = [(cmat[:, NEGREST, :], pt[:, h * 512:(h + 1) * 512])]
                        r = mm_group(e, ps[:, 4 + h, :], pairs, start=False)
                    return r
                sc.op("pe", fn, reads=[pkey, "consts"], writes=["B"])

            def stPV(s, hp=hp):
                qt, kb = steps[s]
                ab, ob = s % 2, qt % 2
                first = kb == 4 * qt + 3
                last = kb == 0

                at, akey = abuf(s)

                def fn(e):
                    pairs = [(Vp[h][:, kb, :], at[:, h * 512:(h + 1) * 512]) for h in range(2)]
                    return mm_group(e, ps[:, 6 + ob, :], pairs, start=first, stop=last)
                sc.op("pe", fn, reads=[akey, "Vp"], writes=[("O", ob)])
                if last:
                    sc.op("dve", lambda e: e.tensor_copy(out=Osb[:, ob, :], in_=ps[:, 6 + ob, :]),
                          reads=[("O", ob)], writes=[("Osb", ob)])
                    sc.op("sp", lambda e: e.dma_start(out=ogin[j][(qt // 4) * 256 + hp * 128:(qt // 4) * 256 + (hp + 1) * 128, (qt % 4) * 512:(qt % 4 + 1) * 512], in_=Osb[:, ob, :]),
                          reads=[("Osb", ob)], writes=[("ogin", hp, qt // 4)], slot=f"osb{ob}")
                    if qt % 4 == 3 and hp == 1:
                        ag_o(hp, qt // 4)

            def doA(s):
                if s > 0 and steps[s][0] != steps[s - 1][0]:
                    ldq(steps[s][0])
                stA(s)

            ldq(0)
            doA(0)
            doA(1)
            stS1(0)
            for s in range(ns):
                if s + 1 < ns:
                    stS1(s + 1)
                if s >= 1:
                    stC1(s - 1)
                stB(s)
                stS2(s)
                if s >= 1:
                    stPV(s - 1)
                if s + 2 < ns:
                    doA(s + 2)
            stPV(ns - 1)

        sc.barrier(("pe", "act", "dve", "sp"))
        oT = av(0, [8, T])
        if debug and layer == 0:
            sc.op("sp", lambda e: [e.dma_start(out=dbg_mine, in_=mine[0]), e.dma_start(out=dbg_ogin, in_=ogin[0])],
                  reads=["mine"] + [("ogin", h_, t_) for h_ in range(2) for t_ in range(4)], writes=["dbg"], slot="dbg", ndma=2)

        def ldo(e):
            ov = ogout[j].rearrange("(tc rh i p) t -> tc rh p i t", tc=4, rh=2, i=4)
            g = bass.ds(mysl(e), 1)
            o4 = oT.rearrange("p (i k2) t -> p i k2 t", k2=2)
            return [e.dma_start(out=o4[:, :, rh, :], in_=ov[g, rh, :, :, :].rearrange("o p i t -> (o p) i t")) for rh in range(2)]
        sc.op("pool", ldo, reads=[("ogout", h_, t_) for h_ in range(2) for t_ in range(4)], writes=["oT"], slot="oT", ndma=2)
        cnt = 0
        for dc in range(8):
            b, wkey = load_w8(wo_d[j, dc])
            wv_ = w8[:, b, :].rearrange("p (k c) -> p k c", k=8)
            for gt in range(4):
                db = 4 + cnt % 2
                cnt += 1
                sc.op("pe", lambda e, wv_=wv_, gt=gt, db=db: mm_group(
                    e, ps[:, db, :], [(wv_[:, k, :], oT[:, k, gt * 512:(gt + 1) * 512]) for k in range(8)]),
                    reads=[wkey, "oT"], writes=[("ps", db)])
                sc.op("dve", lambda e, dc=dc, gt=gt, db=db: e.tensor_tensor(
                    out=xT[:, dc, gt * 512:(gt + 1) * 512], in0=ps[:, db, :], in1=xT[:, dc, gt * 512:(gt + 1) * 512], op=ALU.add),
                    reads=[("ps", db), ("x", dc, gt)], writes=[("x", dc, gt)])

    stages = []
    for l in range(DEPTH):
        stages += [("ffn", l, 0), ("mix", l), ("ffn", l, 1), ("ple", l)]
    for st in stages:
        if st[0] == "ffn":
            ffn(st[1], st[2])
        elif st[0] == "mix":
            if st[1] % 2 == 0:
                attention(st[1])
            else:
                pool_mixer(st[1])
        else:
            ple(st[1])
        if stop_after is not None and st == stop_after:
            break

    sc.barrier(("sp",))
    for k in range(8):
        sc.op("sp", lambda e, k=k: e.dma_start(out=yT_d[k * 128:(k + 1) * 128, :], in_=xT[:, k, :]),
              reads=[("x", k, t) for t in range(4)], writes=["yT"], slot="yst")
    final_tok = sc.slot("yst")

    with nc.Block() as block:
        @block.sync
        def _(e):
            sc.replay("sp", e)
            e.wait_ge(final_tok[0], final_tok[1])
            if "dbg" in sc.slots:
                e.wait_ge(sc.slots["dbg"][0], sc.slots["dbg"][1])

        @block.gpsimd
        def _(e):
            sc.replay("pool", e)

        @block.tensor
        def _(e):
            sc.replay("pe", e)

        @block.scalar
        def _(e):
            sc.replay("act", e)

        @block.vector
        def _(e):
            sc.replay("dve", e)
    es.close()
    return nc


def _bf(a):
    return np.ascontiguousarray(a.astype(ml_dtypes.bfloat16))


def host_layout(inp, nl=DEPTH):
    f = lambda a: np.ascontiguousarray(np.asarray(a, dtype=np.float32))
    sh = {}
    gu = np.stack([f(inp["w_ffn1_gu"][:nl]), f(inp["w_ffn2_gu"][:nl])], 1)
    gate = gu[..., :DFF].reshape(nl, 2, 8, 128, NF, 128)
    up = gu[..., DFF:].reshape(nl, 2, 8, 128, NF, 128)
    g2 = np.concatenate([gate.transpose(0, 1, 4, 3, 2, 5), up.transpose(0, 1, 4, 3, 2, 5)], -1)
    sh["wgu"] = np.ascontiguousarray(g2)
    dn = np.stack([f(inp["w_ffn1_down"][:nl]), f(inp["w_ffn2_down"][:nl])], 1)
    dn = dn.reshape(nl, 2, NF, 128, 8, 128).transpose(0, 1, 4, 3, 2, 5)
    sh["wdn"] = np.ascontiguousarray(dn).reshape(nl, 2, 8, 128, NF * 128)
    wqkv = f(inp["w_qkv"])
    qk = wqkv[:, :, :2048].reshape(2, 8, 128, 16, 128).transpose(0, 3, 2, 1, 4)
    sh["wqk"] = np.ascontiguousarray(qk)
    sh["wv"] = np.ascontiguousarray(wqkv[:, :, 2048:].reshape(2, 8, 128, 1024).transpose(0, 2, 1, 3))
    c8 = lambda w: np.ascontiguousarray(w.reshape(w.shape[0], 8, 128, 8, 128).transpose(0, 3, 2, 1, 4))
    sh["wo"] = c8(f(inp["w_o"]))
    sh["wpi"] = c8(f(inp["w_pool_in"]))
    sh["wpg"] = c8(f(inp["w_ple_gate"]))
    wg = f(inp["w_pool_grp"]).reshape(2, 4, 2, 128, 256).transpose(0, 3, 1, 2, 4)
    sh["wgrp"] = np.ascontiguousarray(wg).reshape(2, 128, 2048)
    sh["wpp"] = np.ascontiguousarray(f(inp["w_ple_proj"]).reshape(DEPTH, 2, 128, 1024).transpose(0, 2, 1, 3))
    gn = np.stack([f(inp["norm_ffn1"]), f(inp["norm_mix"]), f(inp["norm_ffn2"]), f(inp["norm_ple"])], 0)
    sh["gn"] = np.ascontiguousarray(gn.reshape(4, DEPTH, 8, 128).transpose(3, 0, 1, 2)).reshape(128, 128)
    qk_g = np.stack([f(inp["q_norm"]), f(inp["k_norm"])], 0)
    qk_g = np.concatenate([qk_g, qk_g], -1)
    sh["qkg"] = np.ascontiguousarray(qk_g.transpose(2, 0, 1)).reshape(128, 4)
    sh["psc"] = np.ascontiguousarray(f(inp["pool_scale"]).reshape(2, 8, 128).transpose(2, 0, 1)).reshape(128, 16)
    cm = np.zeros((128, 7, 128), np.float32)
    cm[:, 0] = np.eye(128)
    cm[:, 1] = 1.0 / 1024
    cm[:64, 2, :64] = 1.0 / 64
    cm[64:, 2, 64:] = 1.0 / 64
    jj, ss = np.meshgrid(np.arange(128), np.arange(128), indexing="ij")
    cm[:, 3] = -1.0 * (jj >= ss)
    cm[:, 4] = -1.0 * (jj < ss)
    cm[:, 5] = -np.eye(128)
    cm[:, 6] = 1.0 * (jj < ss)
    sh["cmat"] = _bf(cm.reshape(128, 896))
    mk = np.zeros((128, 4, 512), np.float32)
    for i in range(4):
        kpos = 128 * i + np.arange(128)[:, None]
        mk[:, i] = np.where(kpos >= np.arange(512)[None, :], MASKV, 0.0)
    sh["mneg"] = _bf(mk.reshape(128, 2048))
    x = f(inp["x"])
    p = f(inp["p"])
    maps = []
    for r in range(8):
        b, c = r // 4, r % 4
        m = dict(sh)
        m["xT"] = np.ascontiguousarray(x[b, c * T:(c + 1) * T, :].T)
        m["wqk"] = np.ascontiguousarray(sh["wqk"][:, [2 * c, 2 * c + 1, 8 + 2 * c, 8 + 2 * c + 1]])
        m["wv"] = np.ascontiguousarray(sh["wv"][:, :, :, 256 * c:256 * (c + 1)])
        m["pT"] = np.ascontiguousarray(p[:, b, c * T:(c + 1) * T, :].transpose(0, 2, 1))
        s = np.zeros((128, 4), np.float32)
        if c > 0:
            s[:, c - 1] = 1.0
        m["sel"] = s
        ic = np.zeros((128, 4, 16), np.float32)
        for g in range(4):
            w = 2 << g
            if c == 0:
                ic[:, g] = 1.0 / np.minimum(np.arange(16) + 1, w)
            else:
                ic[:, g] = 1.0 / w
        m["icnt"] = ic.reshape(128, 64)
        maps.append(m)
    return maps


_NC_CACHE = {}


def kernel(_stop_after=None, _debug=False, **inputs):
    if _stop_after is not None:
        nl = _stop_after[1] + 1
        maps = host_layout(inputs, nl)
        nc = build(_stop_after, nl=nl, debug=_debug)
        res = run_bass_kernel_spmd(nc, maps, core_ids=list(range(8)))
        _NC_CACHE["res"] = res
        out = np.zeros((2, S, D), np.float32)
        for r in range(8):
            b, c = r // 4, r % 4
            out[b, c * T:(c + 1) * T, :] = np.asarray(res.results[r]["yT"]).T
        return out
    maps = host_layout(inputs)
    if False:
        _NC_CACHE["nc"] = build(_stop_after)
    if "nc" not in _NC_CACHE:
        _NC_CACHE["nc"] = build()
    res = run_bass_kernel_spmd(_NC_CACHE["nc"], maps, core_ids=list(range(8)))
    out = np.zeros((2, S, D), np.float32)
    for r in range(8):
        b, c = r // 4, r % 4
        out[b, c * T:(c + 1) * T, :] = np.asarray(res.results[r]["yT"]).T
    return out
```

```python
import contextlib
import numpy as np
import ml_dtypes
import concourse.bass as bass
import concourse.mybir as mybir
from concourse.bass_utils import run_bass_kernel_spmd

F32 = mybir.dt.float32
BF16 = mybir.dt.bfloat16
AF = mybir.ActivationFunctionType
ALU = mybir.AluOpType

D = 1024
T = 2048
S = 8192
DFF = 2816
NF = 22
DEPTH = 4
EPS = 1e-6
MASKV = -128.0
ARENA = 53248
COMPUTE = ("pe", "act", "dve")


class Sched:
    def __init__(self, nc, es):
        self.nc = nc
        self.es = es
        self.prog = {e: [] for e in ("pe", "act", "dve", "pool", "sp")}
        self.esem = {e: es.enter_context(nc.semaphore("s_" + e)) for e in COMPUTE}
        self.ecnt = {e: 0 for e in COMPUTE}
        self.slots = {}
        self.waited = {e: {} for e in self.prog}
        self.lastw = {}
        self.readers = {}
        self.barrier_toks = []
        self.pending = {e: [] for e in self.prog}
        self.last_barrier = []

    def slot(self, name):
        if name not in self.slots:
            self.slots[name] = [self.es.enter_context(self.nc.semaphore("d_" + name)), 0]
        return self.slots[name]

    def barrier(self, engines=("pe", "act", "dve", "sp", "pool")):
        toks = [(self.esem[e], self.ecnt[e], e) for e in COMPUTE if self.ecnt[e] > 0]
        toks += [(s[0], s[1], None) for s in self.slots.values() if s[1] > 0]
        self.last_barrier = list(toks)
        for e in engines:
            self.pending[e] = list(toks)

    def op(self, eng, fn, reads=(), writes=(), slot=None, ndma=1, after_barrier=False):
        toks = list(self.pending[eng])
        self.pending[eng] = []
        if after_barrier:
            toks += self.last_barrier
        for k in reads:
            if k in self.lastw:
                toks.append(self.lastw[k])
        for k in writes:
            if k in self.lastw:
                toks.append(self.lastw[k])
            toks.extend(self.readers.get(k, ()))
        waits = {}
        for (sem, val, src) in toks:
            if eng == "pe" and src == "pe":
                continue
            key = id(sem)
            if key not in waits or waits[key][1] < val:
                waits[key] = (sem, val)
        wl = []
        for key, (sem, val) in waits.items():
            if self.waited[eng].get(key, 0) >= val:
                continue
            self.waited[eng][key] = val
            wl.append((sem, val))
        if eng in COMPUTE:
            self.ecnt[eng] += 1
            tok = (self.esem[eng], self.ecnt[eng], eng)
            inc = (self.esem[eng], 1)
        else:
            sl = self.slot(slot)
            step = 1 if slot.startswith("cc_") else 16
            sl[1] += step * ndma
            tok = (sl[0], sl[1], None)
            inc = (sl[0], step)
        self.prog[eng].append((wl, fn, inc))
        for k in writes:
            self.lastw[k] = tok
            self.readers[k] = []
        for k in reads:
            self.readers.setdefault(k, []).append(tok)
        return tok

    def replay(self, eng_name, eng):
        compute = eng_name in COMPUTE
        for (wl, fn, inc) in self.prog[eng_name]:
            for (sem, val) in wl:
                eng.wait_ge(sem, val)
            r = fn(eng)
            if r is None:
                continue
            if not isinstance(r, (list, tuple)):
                r = [r]
            if compute:
                r[-1].then_inc(inc[0], inc[1])
            else:
                for ins in r:
                    ins.then_inc(inc[0], inc[1])


def build(stop_after=None, nl=DEPTH, debug=False):
    nc = bass.Bass("TRN2", target_bir_lowering=False)
    es = contextlib.ExitStack()

    def din(name, shape, dt=F32):
        return nc.dram_tensor(name, list(shape), dt, kind="ExternalInput").ap()

    xT_d = din("xT", [D, T])
    pT_d = din("pT", [DEPTH, 256, T])
    wgu_d = din("wgu", [nl, 2, NF, 128, 8, 256])
    wdn_d = din("wdn", [nl, 2, 8, 128, NF * 128])
    wqk_d = din("wqk", [2, 4, 128, 8, 128])
    wv_d = din("wv", [2, 128, 8, 256])
    wo_d = din("wo", [2, 8, 128, 8, 128])
    wpi_d = din("wpi", [2, 8, 128, 8, 128])
    wpg_d = din("wpg", [DEPTH, 8, 128, 8, 128])
    wgrp_d = din("wgrp", [2, 128, 4 * 2 * 256])
    wpp_d = din("wpp", [DEPTH, 128, 2, 1024])
    gn_d = din("gn", [128, 4 * DEPTH * 8])
    qkg_d = din("qkg", [128, 4])
    psc_d = din("psc", [128, 16])
    sel_d = din("sel", [128, 4])
    icnt_d = din("icnt", [128, 64])
    cmat_d = din("cmat", [128, 7 * 128], BF16)
    mneg_d = din("mneg", [128, 4 * 512], BF16)
    yT_d = nc.dram_tensor("yT", [D, T], F32, kind="ExternalOutput").ap()
    if debug:
        dbg_mine = nc.dram_tensor("dbg_mine", [4 * 768, 2048], BF16, kind="ExternalOutput").ap()
        dbg_ogin = nc.dram_tensor("dbg_ogin", [1024, 2048], BF16, kind="ExternalOutput").ap()

    hgin_a = [nc.dram_tensor(f"hgina{j}", [2048, 1024], BF16, kind="Internal").ap() for j in range(2)]
    hgout_a = [nc.dram_tensor(f"hgouta{j}", [4 * 2048, 1024], BF16, kind="Internal").ap() for j in range(2)]
    ogin = [nc.dram_tensor(f"ogin{j}", [1024, 2048], BF16, kind="Internal").ap() for j in range(2)]
    ogout = [nc.dram_tensor(f"ogout{j}", [4096, 2048], BF16, kind="Internal").ap() for j in range(2)]
    mine = [nc.dram_tensor(f"mine{j}", [4 * 768, 2048], BF16, kind="Internal").ap() for j in range(2)]
    hgin = [nc.dram_tensor(f"hgin{j}", [128, 128], F32, kind="Internal").ap() for j in range(2)]
    hgout = [nc.dram_tensor(f"hgout{j}", [4 * 128, 128], F32, kind="Internal").ap() for j in range(2)]

    def sb(name, shape, dt):
        return es.enter_context(nc.sbuf_tensor(name, list(shape), dt))

    xT = sb("xT_sb", [128, 8, T], F32)
    arena = sb("arena", [128, ARENA], BF16)
    wgu = sb("wgu_sb", [128, 2, 8 * 256], BF16)
    wdn = sb("wdn_sb", [128, 2, NF * 128], BF16)
    w8 = sb("w8_sb", [128, 3, 8 * 128], BF16)
    cmat = sb("cmat_sb", [128, 7, 128], BF16)
    mneg = sb("mneg_sb", [128, 4, 512], BF16)
    gn = sb("gn_sb", [128, 4, DEPTH, 8], F32)
    qkg = sb("qkg_sb", [128, 2, 2], F32)
    qg8 = sb("qg8_sb", [128, 2], F32)
    psc = sb("psc_sb", [128, 2, 8], F32)
    sel = sb("sel_sb", [128, 4], F32)
    icnt = sb("icnt_sb", [128, 4, 16], F32)
    wgrp = sb("wgrp_sb", [128, 4, 2, 256], BF16)
    ps = es.enter_context(nc.psum_tensor("ps", [128, 8, 512], F32))

    IDENT, ONES_MS, ONES_HD, NEGTRI, NEGREST, NEGIDENT, TRI01 = range(7)

    def av(off, shape, dt=BF16):
        n = int(np.prod(shape))
        if dt == F32:
            v = arena[:, off:off + 2 * n].bitcast(F32)
        else:
            v = arena[:, off:off + n]
        if len(shape) == 1:
            return v
        if len(shape) == 2:
            return v.rearrange("p (a b) -> p a b", a=shape[0])
        return v.rearrange("p (a b c) -> p a b c", a=shape[0], b=shape[1])

    sc = Sched(nc, es)
    me4 = {}

    def ld_consts(e):
        return [
            e.dma_start(out=cmat[:].rearrange("p a b -> p (a b)"), in_=cmat_d),
            e.dma_start(out=mneg[:].rearrange("p a b -> p (a b)"), in_=mneg_d),
            e.dma_start(out=gn[:].rearrange("p a b c -> p (a b c)"), in_=gn_d),
            e.dma_start(out=qkg[:].rearrange("p a b -> p (a b)"), in_=qkg_d),
            e.dma_start(out=psc[:].rearrange("p a b -> p (a b)"), in_=psc_d),
            e.dma_start(out=sel[:], in_=sel_d),
            e.dma_start(out=icnt[:].rearrange("p a b -> p (a b)"), in_=icnt_d),
        ]
    sc.op("sp", ld_consts, writes=["consts"], slot="consts", ndma=7)
    for k in range(8):
        sc.op("sp", lambda e, k=k: e.dma_start(out=xT[:, k, :], in_=xT_d[k * 128:(k + 1) * 128, :]),
              writes=[("x", k, t) for t in range(4)], slot=f"xld{k}")
    sc.op("dve", lambda e: e.tensor_scalar(out=qg8[:], in0=qkg[:, 0, :], scalar1=0.125, scalar2=None, op0=ALU.mult),
          reads=["consts"], writes=["qg8"])

    ring_cnt = {"wgu": 0, "wdn": 0, "w8": 0}

    def load_w(kind, src_ap, nbuf, view):
        i = ring_cnt[kind]
        ring_cnt[kind] += 1
        b = i % nbuf
        key = (kind, b)
        sc.op("pool", lambda e: e.dma_start(out=view(b), in_=src_ap), writes=[key], slot=f"{kind}{b}")
        return b, key

    def load_wgu(l, which, f):
        return load_w("wgu", wgu_d[l, which, f].rearrange("p k c -> p (k c)"), 2, lambda b: wgu[:, b, :])

    def load_wdn(l, which, dc):
        i = ring_cnt["wdn"]
        ring_cnt["wdn"] += 1
        b = i % 2
        key = ("wdn", b)
        h = NF * 64

        def fn(e):
            return [e.dma_start(out=wdn[:, b, 0:h], in_=wdn_d[l, which, dc, :, 0:h]),
                    e.dma_start(out=wdn[:, b, h:2 * h], in_=wdn_d[l, which, dc, :, h:2 * h])]
        sc.op("pool", fn, writes=[key], slot=f"wdn{b}", ndma=2)
        return b, key

    def load_w8(src3):
        return load_w("w8", src3.rearrange("p k c -> p (k c)"), 3, lambda b: w8[:, b, :])

    O_HT = 0
    O_SQ, O_LN, O_RS, O_SG = 44032, 48128, 49152, 50176

    def mm_group(e, out, pairs, start=True, stop=True):
        r = None
        n = len(pairs)
        for i, (l, rh) in enumerate(pairs):
            r = e.matmul(out, lhsT=l, rhs=rh, start=(start and i == 0), stop=(stop and i == n - 1),
                         skip_group_check=True)
        return r

    def norm_half(half, kind, layer, ssbank=6, hoff=0, hkey="hT"):
        hT = av(hoff, [8, 1024])
        sq = av(O_SQ, [8, 512])
        lnv = av(O_LN, [512], F32)
        rstd = av(O_RS, [512], F32)
        for tt in range(2):
            gt = half * 2 + tt
            c0 = gt * 512
            sc.op("act", lambda e, c0=c0: e.activation(out=sq, in_=xT[:, :, c0:c0 + 512], func=AF.Square),
                  reads=[("x", k, gt) for k in range(8)], writes=["sq"])
            sc.op("pe", lambda e: mm_group(e, ps[:, ssbank, :], [(cmat[:, ONES_MS, :], sq[:, k, :]) for k in range(8)]),
                  reads=["sq", "consts"], writes=[("ps", ssbank)])
            sc.op("act", lambda e: e.activation(out=lnv, in_=ps[:, ssbank, :], func=AF.Ln, bias=EPS, scale=1.0),
                  reads=[("ps", ssbank)], writes=["lnv"])
            sc.op("act", lambda e: e.activation(out=rstd, in_=lnv, func=AF.Exp, scale=-0.5),
                  reads=["lnv"], writes=["rstd"])
            for k in range(8):
                sc.op("dve", lambda e, k=k, c0=c0, tt=tt: e.scalar_tensor_tensor(
                    out=hT[:, k, tt * 512:(tt + 1) * 512], in0=xT[:, k, c0:c0 + 512],
                    scalar=gn[:, kind, layer, k:k + 1], in1=rstd, op0=ALU.mult, op1=ALU.mult),
                    reads=[("x", k, gt), "rstd", "consts"], writes=[(hkey, k, tt)])
        return hT

    def mix_prenorm(layer, half, defer=False):
        j = layer // 2
        hT = norm_half(half, 1, layer)
        hv = hgin_a[j].rearrange("(h k p) t -> h p k t", h=2, p=128)
        sc.op("sp", lambda e: e.dma_start(out=hv[half], in_=hT),
              reads=[("hT", k, tt) for k in range(8) for tt in range(2)], writes=[("hgin_a", half)], slot="hgst")
        def emit_ag():
            sc.op("pool", lambda e: [e.collective_compute("AllGather", ALU.bypass, replica_groups=[[0, 1, 2, 3], [4, 5, 6, 7]],
                                                          ins=[hgin_a[j][(half * 4 + q) * 256:(half * 4 + q + 1) * 256, :]],
                                                          outs=[hgout_a[j][(half * 4 + q) * 1024:(half * 4 + q + 1) * 1024, :]])
                                     for q in range(4)],
                  reads=[("hgin_a", half)], writes=[("hgout_a", half)], slot="cc_a", ndma=4)
        if defer:
            return emit_ag
        emit_ag()

    def ffn(layer, which):
        sc.barrier(("pe", "act", "dve", "sp"))
        kind = 0 if which == 0 else 2
        aT = av(8192, [NF, 1024])
        sgt = av(O_SG, [2, 512], F32)
        hTs = {0: norm_half(0, kind, layer)}
        deferred = []
        want_pre = False
        for half in range(2):
            hT = hTs[half]
            hkey = "hT" if half == 0 else "hT2"
            cnt = 0
            for f in range(NF):
                if half == 0 and f == NF // 2:
                    hTs[1] = norm_half(1, kind, layer, hoff=30720, hkey="hT2")
                b, wkey = load_wgu(layer, which, f)
                if half == 1 and f == 2 and want_pre:
                    deferred.append(mix_prenorm(layer, 0, defer=True))
                if half == 1 and f == 10 and deferred:
                    deferred.pop()()
                for tt in range(2):
                    gb = cnt % 2
                    cnt += 1
                    wv_ = wgu[:, b, :].rearrange("p (k c) -> p k c", k=8)
                    sc.op("pe", lambda e, wv_=wv_, tt=tt, gb=gb, hT=hT: [
                        mm_group(e, ps[:, gb, :], [(wv_[:, k, 0:128], hT[:, k, tt * 512:(tt + 1) * 512]) for k in range(8)]),
                        mm_group(e, ps[:, 2 + gb, :], [(wv_[:, k, 128:256], hT[:, k, tt * 512:(tt + 1) * 512]) for k in range(8)])],
                        reads=[wkey] + [(hkey, k, tt) for k in range(8)], writes=[("ps", gb), ("ps", 2 + gb)])
                    sc.op("act", lambda e, gb=gb: e.activation(out=sgt[:, gb, :], in_=ps[:, gb, :], func=AF.Silu),
                          reads=[("ps", gb)], writes=[("sgt", gb)])
                    sc.op("dve", lambda e, gb=gb, f=f, tt=tt: e.tensor_tensor(
                        out=aT[:, f, tt * 512:(tt + 1) * 512], in0=sgt[:, gb, :], in1=ps[:, 2 + gb, :], op=ALU.mult),
                        reads=[("sgt", gb), ("ps", 2 + gb)], writes=[("aT", f, tt)])
            cnt = 0
            for dc in range(8):
                b, wkey = load_wdn(layer, which, dc)
                wv_ = wdn[:, b, :].rearrange("p (f c) -> p f c", f=NF)
                for tt in range(2):
                    db = 4 + cnt % 2
                    cnt += 1
                    gt = half * 2 + tt
                    sc.op("pe", lambda e, wv_=wv_, tt=tt, db=db: mm_group(
                        e, ps[:, db, :], [(wv_[:, f, :], aT[:, f, tt * 512:(tt + 1) * 512]) for f in range(NF)]),
                        reads=[wkey] + [("aT", f, tt) for f in range(NF)], writes=[("ps", db)])
                    sc.op("dve", lambda e, dc=dc, gt=gt, db=db: e.scalar_tensor_tensor(
                        out=xT[:, dc, gt * 512:(gt + 1) * 512], in0=ps[:, db, :], scalar=0.5,
                        in1=xT[:, dc, gt * 512:(gt + 1) * 512], op0=ALU.mult, op1=ALU.add),
                        reads=[("ps", db), ("x", dc, gt)], writes=[("x", dc, gt)])
            if half == 0 and which == 0 and layer % 2 == 0:
                want_pre = True

    def ple(layer):
        sc.barrier(("pe", "act", "dve", "sp"))
        pTb = av(8192, [2, T])
        wpp = av(12288, [2, 1024])
        sgm = av(14336, [2, 512], F32)
        tmp = av(16384, [2, 512], F32)
        sc.op("pool", lambda e: [e.dma_start(out=pTb, in_=pT_d[layer].rearrange("(k p) t -> p k t", p=128)),
                                 e.dma_start(out=wpp, in_=wpp_d[layer])],
              writes=["pTb", "wpp"], slot="plew", ndma=2, after_barrier=True)
        cnt = 0
        for half in range(2):
            hT = norm_half(half, 3, layer)
            for dc in range(8):
                b, wkey = load_w8(wpg_d[layer, dc])
                wv_ = w8[:, b, :].rearrange("p (k c) -> p k c", k=8)
                for tt in range(2):
                    gb = cnt % 2
                    cnt += 1
                    gt = half * 2 + tt
                    sc.op("pe", lambda e, wv_=wv_, tt=tt, gb=gb, dc=dc, gt=gt: [
                        mm_group(e, ps[:, gb, :], [(wv_[:, k, :], hT[:, k, tt * 512:(tt + 1) * 512]) for k in range(8)]),
                        mm_group(e, ps[:, 2 + gb, :], [(wpp[:, k, dc * 128:(dc + 1) * 128], pTb[:, k, gt * 512:(gt + 1) * 512]) for k in range(2)])],
                        reads=[wkey, "pTb", "wpp"] + [("hT", k, tt) for k in range(8)], writes=[("ps", gb), ("ps", 2 + gb)])
                    sc.op("act", lambda e, gb=gb: e.activation(out=sgm[:, gb, :], in_=ps[:, gb, :], func=AF.Sigmoid),
                          reads=[("ps", gb)], writes=[("sgm", gb)])
                    sc.op("dve", lambda e, gb=gb: e.tensor_tensor(out=tmp[:, gb, :], in0=sgm[:, gb, :], in1=ps[:, 2 + gb, :], op=ALU.mult),
                          reads=[("sgm", gb), ("ps", 2 + gb)], writes=[("tmp", gb)])
                    sc.op("dve", lambda e, gb=gb, dc=dc, gt=gt: e.tensor_tensor(
                        out=xT[:, dc, gt * 512:(gt + 1) * 512], in0=tmp[:, gb, :], in1=xT[:, dc, gt * 512:(gt + 1) * 512], op=ALU.add),
                        reads=[("tmp", gb), ("x", dc, gt)], writes=[("x", dc, gt)])

    def pool_mixer(layer):
        j = layer // 2
        sc.barrier(("pe", "act", "dve", "sp"))
        U = [av(0, [8, 1040], F32), av(16640, [8, 1040], F32)]
        O_H = 33280
        hT = av(O_H, [8, 1024])
        tmpS = av(41472, [2, 1040], F32)
        hal = av(45632, [4, 128], F32)
        sc.op("pool", lambda e: e.dma_start(out=wgrp[:].rearrange("p a b c -> p (a b c)"), in_=wgrp_d[j]),
              writes=["wgrp"], slot="wgrp")

        def do_norm(half):
            sq = av(46656, [8, 512])
            lnv = tmpS[:, 0, 0:512]
            rstd = tmpS[:, 1, 0:512]
            for tt in range(2):
                gt = half * 2 + tt
                c0 = gt * 512
                sc.op("act", lambda e, c0=c0: e.activation(out=sq, in_=xT[:, :, c0:c0 + 512], func=AF.Square),
                      reads=[("x", k, gt) for k in range(8)], writes=["sq"])
                sc.op("pe", lambda e: mm_group(e, ps[:, 6, :], [(cmat[:, ONES_MS, :], sq[:, k, :]) for k in range(8)]),
                      reads=["sq", "consts"], writes=[("ps", 6)])
                sc.op("act", lambda e: e.activation(out=lnv, in_=ps[:, 6, :], func=AF.Ln, bias=EPS, scale=1.0),
                      reads=[("ps", 6)], writes=[("tmpS", 0)])
                sc.op("act", lambda e: e.activation(out=rstd, in_=lnv, func=AF.Exp, scale=-0.5),
                      reads=[("tmpS", 0)], writes=[("tmpS", 1)])
                for k in range(8):
                    sc.op("dve", lambda e, k=k, c0=c0, tt=tt: e.scalar_tensor_tensor(
                        out=hT[:, k, tt * 512:(tt + 1) * 512], in0=xT[:, k, c0:c0 + 512],
                        scalar=gn[:, 1, layer, k:k + 1], in1=rstd, op0=ALU.mult, op1=ALU.mult),
                        reads=[("x", k, gt), ("tmpS", 1), "consts"], writes=[("hT", k, tt), ("pl", k)])

        def compute_u(half):
            cnt = 0
            for uc in range(8):
                b, wkey = load_w8(wpi_d[j, uc])
                wv_ = w8[:, b, :].rearrange("p (k c) -> p k c", k=8)
                for tt in range(2):
                    gb = cnt % 2
                    cnt += 1
                    sc.op("pe", lambda e, wv_=wv_, tt=tt, gb=gb: mm_group(
                        e, ps[:, gb, :], [(wv_[:, k, :], hT[:, k, tt * 512:(tt + 1) * 512]) for k in range(8)]),
                        reads=[wkey] + [("hT", k, tt) for k in range(8)], writes=[("ps", gb)])
                    sc.op("act", lambda e, gb=gb, uc=uc, tt=tt: e.activation(
                        out=U[half][:, uc, 16 + tt * 512:16 + (tt + 1) * 512], in_=ps[:, gb, :], func=AF.Copy),
                        reads=[("ps", gb)], writes=[("U", half, uc)])

        def pool_and_mix(half):
            pooled = hT
            for uc in range(8):
                g = uc // 2
                w = 2 << g
                cur = U[half][:, uc, :]
                lo = 0
                nsteps = g + 1
                srcbuf = cur
                for st in range(nsteps):
                    sh = 1 << st
                    lo2 = lo + sh
                    dst = tmpS[:, st % 2, :]
                    sc.op("dve", lambda e, dst=dst, srcbuf=srcbuf, lo2=lo2, sh=sh: e.tensor_tensor(
                        out=dst[:, lo2:1040], in0=srcbuf[:, lo2:1040], in1=srcbuf[:, lo2 - sh:1040 - sh], op=ALU.add),
                        reads=[("U", half, uc), ("tmpS", 0), ("tmpS", 1)], writes=[("tmpS", st % 2)])
                    srcbuf = dst
                    lo = lo2
                sfin = srcbuf
                sc.op("dve", lambda e, sfin=sfin, cur=cur, uc=uc, w=w: e.scalar_tensor_tensor(
                    out=pooled[:, uc, :], in0=sfin[:, 16:1040], scalar=1.0 / w, in1=cur[:, 16:1040],
                    op0=ALU.mult, op1=ALU.subtract),
                    reads=[("tmpS", 0), ("tmpS", 1), ("U", half, uc)],
                    writes=[("pl", uc), ("hT", uc, 0), ("hT", uc, 1)])
                if half == 0:
                    t16 = tmpS[:, (nsteps) % 2, 0:16]
                    sc.op("dve", lambda e, t16=t16, sfin=sfin, g=g: e.tensor_tensor(
                        out=t16, in0=sfin[:, 16:32], in1=icnt[:, g, :], op=ALU.mult),
                        reads=[("tmpS", 0), ("tmpS", 1), "consts"], writes=[("tmpS", nsteps % 2)])
                    sc.op("dve", lambda e, t16=t16, cur=cur, uc=uc: e.tensor_tensor(
                        out=pooled[:, uc, 0:16], in0=t16, in1=cur[:, 16:32], op=ALU.subtract),
                        reads=[("tmpS", 0), ("tmpS", 1), ("U", half, uc), ("pl", uc)], writes=[("pl", uc)])
            cnt = 0
            for g in range(4):
                for dd in range(2):
                    dc = 2 * g + dd
                    for tt in range(2):
                        db = 4 + cnt % 2
                        cnt += 1
                        gt = half * 2 + tt
                        sc.op("pe", lambda e, g=g, dd=dd, tt=tt, db=db: mm_group(
                            e, ps[:, db, :], [(wgrp[:, g, cc, dd * 128:(dd + 1) * 128], pooled[:, 2 * g + cc, tt * 512:(tt + 1) * 512]) for cc in range(2)]),
                            reads=["wgrp", ("pl", 2 * g), ("pl", 2 * g + 1)], writes=[("ps", db)])
                        sc.op("dve", lambda e, dc=dc, gt=gt, db=db: e.scalar_tensor_tensor(
                            out=xT[:, dc, gt * 512:(gt + 1) * 512], in0=ps[:, db, :], scalar=psc[:, j, dc:dc + 1],
                            in1=xT[:, dc, gt * 512:(gt + 1) * 512], op0=ALU.mult, op1=ALU.add),
                            reads=[("ps", db), ("x", dc, gt), "consts"], writes=[("x", dc, gt)])

        do_norm(1)
        compute_u(1)
        sc.op("sp", lambda e: e.dma_start(out=hgin[j].rearrange("p (k c) -> p k c", k=8), in_=U[1][:, :, 1024:1040]),
              reads=[("U", 1, uc) for uc in range(8)], writes=["hgin"], slot="hgin")
        sc.op("pool", lambda e: e.collective_compute("AllGather", ALU.bypass, replica_groups=[[0, 1, 2, 3], [4, 5, 6, 7]],
                                                     ins=[hgin[j]], outs=[hgout[j]]),
              reads=["hgin"], writes=["hgout"], slot="cc_h")
        do_norm(0)
        compute_u(0)
        for uc in range(8):
            sc.op("dve", lambda e, uc=uc: e.tensor_copy(out=U[1][:, uc, 0:16], in_=U[0][:, uc, 1024:1040]),
                  reads=[("U", 0, uc)], writes=[("U", 1, uc)])
        pool_and_mix(1)
        sc.op("sp", lambda e: e.dma_start(out=hal, in_=hgout[j].rearrange("(i p) c -> p i c", p=128)),
              reads=["hgout"], writes=["hal"], slot="hal")
        for uc in range(8):
            halv = hal.rearrange("p i (k c) -> p i k c", k=8)
            sc.op("dve", lambda e, uc=uc, halv=halv: e.tensor_scalar(
                out=U[0][:, uc, 0:16], in0=halv[:, 0, uc, :], scalar1=sel[:, 0:1], scalar2=None, op0=ALU.mult),
                reads=["hal", "consts"], writes=[("U", 0, uc)])
            for i in range(1, 4):
                sc.op("dve", lambda e, uc=uc, i=i, halv=halv: e.scalar_tensor_tensor(
                    out=U[0][:, uc, 0:16], in0=halv[:, i, uc, :], scalar=sel[:, i:i + 1], in1=U[0][:, uc, 0:16],
                    op0=ALU.mult, op1=ALU.add),
                    reads=["hal", "consts", ("U", 0, uc)], writes=[("U", 0, uc)])
        pool_and_mix(0)

    def attention(layer):
        j = layer // 2
        sc.barrier(("pe", "act", "dve", "sp"))
        hTi = [av(8192, [8, 1024]), av(16384, [8, 1024])]
        wqb = av(24576, [4, 8 * 128])
        wvb = av(28672, [8, 256])
        qst = av(30720, [2, 512])
        vst = av(31744, [4, 256])
        sqh = av(32768, [2, 512])
        lnv = av(O_LN, [512], F32)
        rstd = av(O_RS, [512], F32)
        sc.op("pool", lambda e: [e.dma_start(out=wqb[:, qc, :], in_=wqk_d[j, qc].rearrange("p k c -> p (k c)")) for qc in range(4)]
              + [e.dma_start(out=wvb, in_=wv_d[j])],
              writes=["wqb", "wvb"], slot="wqv", ndma=5, after_barrier=True)
        mix_prenorm(layer, 1)
        hgv = hgout_a[j].rearrange("(c i k2 p) t -> c i p k2 t", c=8, i=4, k2=2)
        minev = mine[j].rearrange("(i r) c -> i r c", i=4)
        cq = 0
        cv = 0
        it = 0

        def hload(it_):
            half_, i_ = it_ // 4, it_ % 4
            hb_ = it_ % 2
            sc.op("pool", lambda e: [e.dma_start(out=hTi[hb_][:, 2 * q:2 * q + 2, :], in_=hgv[half_ * 4 + q, i_]) for q in range(4)],
                  reads=[("hgout_a", half_)], writes=[("hTi", hb_)], slot=f"hld{hb_}", ndma=4)
        hload(0)
        for half in range(2):
            for i in range(4):
                hb = it % 2
                it += 1
                hX = hTi[hb]
                if it < 8:
                    hload(it)
                PB = (0, 1, 6, 7)
                groups = [(qc, tt) for qc in range(4) for tt in range(2)]

                def g_mm(qc, tt, pbk, hX=hX, hb=hb):
                    wv_ = wqb[:, qc, :].rearrange("p (k c) -> p k c", k=8)
                    sc.op("pe", lambda e: mm_group(
                        e, ps[:, pbk, :], [(wv_[:, k, :], hX[:, k, tt * 512:(tt + 1) * 512]) for k in range(8)]),
                        reads=["wqb", ("hTi", hb)], writes=[("ps", pbk)])

                def g_rest(qc, tt, pbk, gb, i=i, half=half):
                    isk = qc // 2
                    c2 = qc % 2
                    sc.op("act", lambda e: e.activation(out=sqh[:, gb, :], in_=ps[:, pbk, :], func=AF.Square),
                          reads=[("ps", pbk)], writes=[("sqh", gb)])
                    sc.op("pe", lambda e: mm_group(e, ps[:, 2 + gb, :], [(cmat[:, ONES_HD, :], sqh[:, gb, :])]),
                          reads=[("sqh", gb), "consts"], writes=[("ps", 2 + gb)])
                    sc.op("act", lambda e: e.activation(out=lnv, in_=ps[:, 2 + gb, :], func=AF.Ln, bias=EPS, scale=1.0),
                          reads=[("ps", 2 + gb)], writes=["lnv"])
                    sc.op("act", lambda e: e.activation(out=rstd, in_=lnv, func=AF.Exp, scale=-0.5),
                          reads=["lnv"], writes=["rstd"])
                    gsc = qkg[:, 1, j:j + 1] if isk else qg8[:, j:j + 1]
                    sc.op("dve", lambda e: e.scalar_tensor_tensor(
                        out=qst[:, gb, :], in0=ps[:, pbk, :], scalar=gsc, in1=rstd, op0=ALU.mult, op1=ALU.mult),
                        reads=[("ps", pbk), "rstd", "consts", "qg8"], writes=[("qst", gb)])
                    r0 = isk * 256 + c2 * 128
                    col = half * 1024 + tt * 512
                    sc.op("sp", lambda e: e.dma_start(out=minev[i, r0:r0 + 128, col:col + 512], in_=qst[:, gb, :]),
                          reads=[("qst", gb)], writes=["mine"], slot=f"qst{gb}")

                idx = [cq + n for n in range(len(groups))]
                cq += len(groups)
                g_mm(*groups[0], PB[idx[0] % 4])
                for gi in range(len(groups)):
                    if gi + 1 < len(groups):
                        g_mm(*groups[gi + 1], PB[idx[gi + 1] % 4])
                    g_rest(*groups[gi], PB[idx[gi] % 4], idx[gi] % 2)
                vreg = minev[i, 512:768, :].rearrange("r (t8 c) -> (r t8) c", c=256)
                for tb in range(8):
                    gb = cv % 2
                    vb = cv % 4
                    cv += 1
                    sc.op("pe", lambda e, tb=tb, gb=gb, hX=hX: mm_group(
                        e, ps[:, 4 + gb, 0:256], [(hX[:, k, tb * 128:(tb + 1) * 128], wvb[:, k, :]) for k in range(8)]),
                        reads=["wvb", ("hTi", hb)], writes=[("ps", 4 + gb)])
                    sc.op("act", lambda e, gb=gb, vb=vb: e.activation(out=vst[:, vb, :], in_=ps[:, 4 + gb, 0:256], func=AF.Copy),
                          reads=[("ps", 4 + gb)], writes=[("vst", vb)])
                    tok0 = half * 1024 + tb * 128
                    sc.op("sp", lambda e, vb=vb, tok0=tok0, vreg=vreg: e.dma_start(out=vreg[tok0:tok0 + 128, :], in_=vst[:, vb, :]),
                          reads=[("vst", vb)], writes=["mine"], slot=f"vst{vb}")
        sc.barrier(("pe", "act", "dve", "sp"))

        Kst = av(0, [2, S])
        Vp = [av(16384, [64, 128]), av(24576, [64, 128])]
        E = av(32768, [2, 1024], F32)
        P = av(36864, [3, 1024])
        A = av(39936, [2, 1024])
        Qd = av(41984, [2, 1024])
        Qz = av(44032, [2, 1024])
        Osb = av(46080, [2, 512])
        Pd = av(47104, [3, 1024])
        Ad = av(50176, [3, 1024])
        minev = mine[j].rearrange("(i r) c -> i r c", i=4)
        minevv = mine[j].rearrange("(i r) (t8 c) -> i (r t8) c", i=4, c=256)[:, 4096:6144, :].rearrange(
            "i (b s) c -> i s b c", s=128)

        def mysl(e):
            return e.partition_id() % 4

        def ag_o(hp_, tc):
            sc.op("pool", lambda e: e.collective_compute("AllGather", ALU.bypass, replica_groups=[[0, 1, 2, 3], [4, 5, 6, 7]],
                                                         ins=[ogin[j][(tc * 2 + hp_) * 128:(tc * 2 + hp_ + 1) * 128, :]],
                                                         outs=[ogout[j][(tc * 2 + hp_) * 512:(tc * 2 + hp_ + 1) * 512, :]]),
                  reads=[("ogin", hp_, tc)], writes=[("ogout", hp_, tc)], slot="cc_o", ndma=1)

        for hpi, hp in enumerate((0, 1)):
            if hpi >= 1:
                sc.barrier(("pe", "act", "dve", "sp"))
                for tc_ in range(4):
                    ag_o(0, tc_)
            sc.op("dve", lambda e: e.memset(arena[:, 16384:32768], 0.0), writes=["Vp"])
            sc.op("dve", lambda e: e.memset(arena[:, 44032:46080], 0.0), writes=["Q2z", ("Q2", 0), ("Q2", 1)])
            sc.op("dve", lambda e: e.memset(arena[:, 47104:53248], 0.0), writes=[("Pd", n_) for n_ in range(3)] + [("Ad", n_) for n_ in range(3)])
            sc.op("dve", lambda e: e.memset(Kst[64:128, 0, S - 128:S], 0.0), writes=["Kst"])
            sc.op("dve", lambda e: e.memset(Kst[64:128, 1, S - 128:S], 0.0), writes=["Kst"])

            def ldk(e, hp=hp):
                r = []
                for i in range(4):
                    for h in range(2):
                        rr = 256 + hp * 128 + h * 64
                        src = minev[i, rr:rr + 64, :]
                        r.append(e.dma_start(out=Kst[0:64, h, i * 2048:(i + 1) * 2048], in_=src))
                        if i == 0:
                            r.append(e.dma_start(out=Kst[64:128, h, 0:1920], in_=src[:, 128:2048]))
                        else:
                            r.append(e.dma_start(out=Kst[64:128, h, i * 2048 - 128:(i + 1) * 2048 - 128], in_=src))
                return r
            sc.op("sp", ldk, reads=["mine"], writes=["Kst"], slot="kld", ndma=16)

            def ldv(e, hp=hp):
                r = []
                for i in range(4):
                    for h in range(2):
                        c0 = hp * 128 + h * 64
                        for q4 in range(4):
                            src = minevv[i, :, q4 * 4:q4 * 4 + 4, c0:c0 + 64]
                            r.append(e.dma_start(out=Vp[h][:, i * 16 + q4 * 4:i * 16 + q4 * 4 + 4, h * 64:(h + 1) * 64], in_=src))
                return r
            sc.op("sp", ldv, reads=["mine"], writes=["Vp"], slot="vld", ndma=32)
            sc.op("dve", lambda e: e.tensor_scalar(out=Kst[64:128, :, 0:S - 128], in0=Kst[64:128, :, 0:S - 128],
                                                   scalar1=-1.0, scalar2=None, op0=ALU.mult),
                  reads=["Kst"], writes=["Kst"])

            steps = [(qt, kb) for qt in range(16) for kb in range(4 * qt + 3, -1, -1)]
            ns = len(steps)

            def ldq(qt, hp=hp):
                qb = qt % 2

                def fn(e):
                    i = qt // 4
                    c0 = (qt % 4) * 512
                    rr = hp * 128
                    src = minev[i, rr:rr + 128, c0:c0 + 512].rearrange("(h d) c -> d h c", h=2)
                    return [e.dma_start(out=Qd[0:64, qb, :].rearrange("p (h c) -> p h c", h=2), in_=src),
                            e.dma_start(out=Qd[64:128, qb, :].rearrange("p (h c) -> p h c", h=2), in_=src),
                            e.dma_start(out=Qz[0:64, qb, :].rearrange("p (h c) -> p h c", h=2), in_=src)]
                sc.op("sp", fn, reads=["mine"], writes=[("Q2", qb)], slot=f"q2{qb}", ndma=3)

            def pbuf(s):
                qt, kb = steps[s]
                i = kb - 4 * qt
                if i >= 1:
                    return Pd[:, i - 1, :], ("Pd", i - 1)
                return P[:, s % 3, :], ("P", s % 3)

            def abuf(s):
                qt, kb = steps[s]
                i = kb - 4 * qt
                if i >= 1:
                    return Ad[:, i - 1, :], ("Ad", i - 1)
                return A[:, s % 2, :], ("A", s % 2)

            def stA(s):
                qt, kb = steps[s]
                zb, qb = s % 2, qt % 2
                c0 = max(0, (kb - 4 * qt)) * 128

                def fn(e):
                    r = None
                    for h in range(2):
                        q = Qz[:, qb, h * 512 + c0:(h + 1) * 512]
                        r = mm_group(e, ps[:, 2 * zb + h, c0:512], [(Kst[:, h, kb * 128:(kb + 1) * 128], q)])
                    return r
                sc.op("pe", fn, reads=["Kst", ("Q2", qb), "Q2z"], writes=[("Z", zb)])

            def stS1(s):
                qt, kb = steps[s]
                zb = s % 2
                c0 = max(0, (kb - 4 * qt)) * 128
                Zv = ps[:, 2 * zb:2 * zb + 2, c0:512]
                Ev = E[:, zb, :].rearrange("p (h c) -> p h c", h=2)[:, :, c0:512]
                pt, pkey = pbuf(s)
                Pv = pt.rearrange("p (h c) -> p h c", h=2)
                sc.op("act", lambda e: e.activation(out=Ev, in_=Zv, func=AF.Exp),
                      reads=[("Z", zb)], writes=[("E", zb)])
                sc.op("act", lambda e: e.activation(out=Pv[:, :, c0:512], in_=Ev, func=AF.Ln, bias=1.0, scale=1.0),
                      reads=[("E", zb)], writes=[pkey])
                if kb >= 4 * qt:
                    sc.op("dve", lambda e: [e.tensor_tensor(out=Pv[:, h, c0:c0 + 128], in0=Pv[:, h, c0:c0 + 128],
                                                            in1=cmat[:, TRI01, :], op=ALU.mult) for h in range(2)],
                          reads=["consts", pkey], writes=[pkey])

            def stB(s):
                qt, kb = steps[s]
                pb, qb = s % 3, qt % 2
                first = kb == 4 * qt + 3
                diag = kb >= 4 * qt
                i = kb - 4 * qt

                pt, pkey = pbuf(s)

                def fn(e):
                    r = None
                    for h in range(2):
                        q = (Qz if first else Qd)[:, qb, h * 512:(h + 1) * 512]
                        pairs = [(Kst[:, h, kb * 128:(kb + 1) * 128], q),
                                 (cmat[:, NEGTRI, :], pt[:, h * 512:(h + 1) * 512])]
                        r = mm_group(e, ps[:, 4 + h, :], pairs, start=first)
                    return r
                sc.op("pe", fn, reads=["Kst", ("Q2", qb), "Q2z", pkey, "consts"], writes=["B"])

            def stS2(s):
                qt, kb = steps[s]
                c0 = max(0, (kb - 4 * qt)) * 128
                at, akey = abuf(s)
                Av = at.rearrange("p (h c) -> p h c", h=2)
                sc.op("act", lambda e: e.activation(out=Av[:, :, c0:512], in_=ps[:, 4:6, c0:512], func=AF.Exp),
                      reads=["B"], writes=[akey])
                if kb >= 4 * qt:
                    sc.op("dve", lambda e: [e.tensor_tensor(out=Av[:, h, c0:c0 + 128], in0=Av[:, h, c0:c0 + 128],
                                                            in1=cmat[:, TRI01, :], op=ALU.mult) for h in range(2)],
                          reads=["consts", akey], writes=[akey])

            def stC1(s):
                qt, kb = steps[s]
                pb = s % 3
                last = kb == 0
                diag = kb >= 4 * qt
                i = kb - 4 * qt
                if last:
                    return

                pt, pkey = pbuf(s)

                def fn(e):
                    r = None
                    for h in range(2):
                        pairs = [(cmat[:, NEGREST, :], pt[:, h * 512:(h + 1) * 512])]
                        r = mm_group(e, ps[:, 4 + h, :], pairs, start=False)
                    return r
                sc.op("pe", fn, reads=[pkey, "consts"], writes=["B"])

            def stPV(s, hp=hp):
                qt, kb = steps[s]
                ab, ob = s % 2, qt % 2
                first = kb == 4 * qt + 3
                last = kb == 0

                at, akey = abuf(s)

                def fn(e):
                    pairs = [(Vp[h][:, kb, :], at[:, h * 512:(h + 1) * 512]) for h in range(2)]
                    return mm_group(e, ps[:, 6 + ob, :], pairs, start=first, stop=last)
                sc.op("pe", fn, reads=[akey, "Vp"], writes=[("O", ob)])
                if last:
                    sc.op("dve", lambda e: e.tensor_copy(out=Osb[:, ob, :], in_=ps[:, 6 + ob, :]),
                          reads=[("O", ob)], writes=[("Osb", ob)])
                    sc.op("sp", lambda e: e.dma_start(out=ogin[j][(qt // 4) * 256 + hp * 128:(qt // 4) * 256 + (hp + 1) * 128, (qt % 4) * 512:(qt % 4 + 1) * 512], in_=Osb[:, ob, :]),
                          reads=[("Osb", ob)], writes=[("ogin", hp, qt // 4)], slot=f"osb{ob}")
                    if qt % 4 == 3 and hp == 1:
                        ag_o(hp, qt // 4)

            def doA(s):
                if s > 0 and steps[s][0] != steps[s - 1][0]:
                    ldq(steps[s][0])
                stA(s)

            ldq(0)
            doA(0)
            doA(1)
            stS1(0)
            for s in range(ns):
                if s + 1 < ns:
                    stS1(s + 1)
                if s >= 1:
                    stC1(s - 1)
                stB(s)
                stS2(s)
                if s >= 1:
                    stPV(s - 1)
                if s + 2 < ns:
                    doA(s + 2)
            stPV(ns - 1)

        sc.barrier(("pe", "act", "dve", "sp"))
        oT = av(0, [8, T])
        if debug and layer == 0:
            sc.op("sp", lambda e: [e.dma_start(out=dbg_mine, in_=mine[0]), e.dma_start(out=dbg_ogin, in_=ogin[0])],
                  reads=["mine"] + [("ogin", h_, t_) for h_ in range(2) for t_ in range(4)], writes=["dbg"], slot="dbg", ndma=2)

        def ldo(e):
            ov = ogout[j].rearrange("(tc rh i p) t -> tc rh p i t", tc=4, rh=2, i=4)
            g = bass.ds(mysl(e), 1)
            o4 = oT.rearrange("p (i k2) t -> p i k2 t", k2=2)
            return [e.dma_start(out=o4[:, :, rh, :], in_=ov[g, rh, :, :, :].rearrange("o p i t -> (o p) i t")) for rh in range(2)]
        sc.op("pool", ldo, reads=[("ogout", h_, t_) for h_ in range(2) for t_ in range(4)], writes=["oT"], slot="oT", ndma=2)
        cnt = 0
        for dc in range(8):
            b, wkey = load_w8(wo_d[j, dc])
            wv_ = w8[:, b, :].rearrange("p (k c) -> p k c", k=8)
            for gt in range(4):
                db = 4 + cnt % 2
                cnt += 1
                sc.op("pe", lambda e, wv_=wv_, gt=gt, db=db: mm_group(
                    e, ps[:, db, :], [(wv_[:, k, :], oT[:, k, gt * 512:(gt + 1) * 512]) for k in range(8)]),
                    reads=[wkey, "oT"], writes=[("ps", db)])
                sc.op("dve", lambda e, dc=dc, gt=gt, db=db: e.tensor_tensor(
                    out=xT[:, dc, gt * 512:(gt + 1) * 512], in0=ps[:, db, :], in1=xT[:, dc, gt * 512:(gt + 1) * 512], op=ALU.add),
                    reads=[("ps", db), ("x", dc, gt)], writes=[("x", dc, gt)])

    stages = []
    for l in range(DEPTH):
        stages += [("ffn", l, 0), ("mix", l), ("ffn", l, 1), ("ple", l)]
    for st in stages:
        if st[0] == "ffn":
            ffn(st[1], st[2])
        elif st[0] == "mix":
            if st[1] % 2 == 0:
                attention(st[1])
            else:
                pool_mixer(st[1])
        else:
            ple(st[1])
        if stop_after is not None and st == stop_after:
            break

    sc.barrier(("sp",))
    for k in range(8):
        sc.op("sp", lambda e, k=k: e.dma_start(out=yT_d[k * 128:(k + 1) * 128, :], in_=xT[:, k, :]),
              reads=[("x", k, t) for t in range(4)], writes=["yT"], slot="yst")
    final_tok = sc.slot("yst")

    with nc.Block() as block:
        @block.sync
        def _(e):
            sc.replay("sp", e)
            e.wait_ge(final_tok[0], final_tok[1])
            if "dbg" in sc.slots:
                e.wait_ge(sc.slots["dbg"][0], sc.slots["dbg"][1])

        @block.gpsimd
        def _(e):
            sc.replay("pool", e)

        @block.tensor
        def _(e):
            sc.replay("pe", e)

        @block.scalar
        def _(e):
            sc.replay("act", e)

        @block.vector
        def _(e):
            sc.replay("dve", e)
    es.close()
    return nc


def _bf(a):
    return np.ascontiguousarray(a.astype(ml_dtypes.bfloat16))


def host_layout(inp, nl=DEPTH):
    f = lambda a: np.ascontiguousarray(np.asarray(a, dtype=np.float32))
    sh = {}
    gu = np.stack([f(inp["w_ffn1_gu"][:nl]), f(inp["w_ffn2_gu"][:nl])], 1)
    gate = gu[..., :DFF].reshape(nl, 2, 8, 128, NF, 128)
    up = gu[..., DFF:].reshape(nl, 2, 8, 128, NF, 128)
    g2 = np.concatenate([gate.transpose(0, 1, 4, 3, 2, 5), up.transpose(0, 1, 4, 3, 2, 5)], -1)
    sh["wgu"] = np.ascontiguousarray(g2)
    dn = np.stack([f(inp["w_ffn1_down"][:nl]), f(inp["w_ffn2_down"][:nl])], 1)
    dn = dn.reshape(nl, 2, NF, 128, 8, 128).transpose(0, 1, 4, 3, 2, 5)
    sh["wdn"] = np.ascontiguousarray(dn).reshape(nl, 2, 8, 128, NF * 128)
    wqkv = f(inp["w_qkv"])
    qk = wqkv[:, :, :2048].reshape(2, 8, 128, 16, 128).transpose(0, 3, 2, 1, 4)
    sh["wqk"] = np.ascontiguousarray(qk)
    sh["wv"] = np.ascontiguousarray(wqkv[:, :, 2048:].reshape(2, 8, 128, 1024).transpose(0, 2, 1, 3))
    c8 = lambda w: np.ascontiguousarray(w.reshape(w.shape[0], 8, 128, 8, 128).transpose(0, 3, 2, 1, 4))
    sh["wo"] = c8(f(inp["w_o"]))
    sh["wpi"] = c8(f(inp["w_pool_in"]))
    sh["wpg"] = c8(f(inp["w_ple_gate"]))
    wg = f(inp["w_pool_grp"]).reshape(2, 4, 2, 128, 256).transpose(0, 3, 1, 2, 4)
    sh["wgrp"] = np.ascontiguousarray(wg).reshape(2, 128, 2048)
    sh["wpp"] = np.ascontiguousarray(f(inp["w_ple_proj"]).reshape(DEPTH, 2, 128, 1024).transpose(0, 2, 1, 3))
    gn = np.stack([f(inp["norm_ffn1"]), f(inp["norm_mix"]), f(inp["norm_ffn2"]), f(inp["norm_ple"])], 0)
    sh["gn"] = np.ascontiguousarray(gn.reshape(4, DEPTH, 8, 128).transpose(3, 0, 1, 2)).reshape(128, 128)
    qk_g = np.stack([f(inp["q_norm"]), f(inp["k_norm"])], 0)
    qk_g = np.concatenate([qk_g, qk_g], -1)
    sh["qkg"] = np.ascontiguousarray(qk_g.transpose(2, 0, 1)).reshape(128, 4)
    sh["psc"] = np.ascontiguousarray(f(inp["pool_scale"]).reshape(2, 8, 128).transpose(2, 0, 1)).reshape(128, 16)
    cm = np.zeros((128, 7, 128), np.float32)
    cm[:, 0] = np.eye(128)
    cm[:, 1] = 1.0 / 1024
    cm[:64, 2, :64] = 1.0 / 64
    cm[64:, 2, 64:] = 1.0 / 64
    jj, ss = np.meshgrid(np.arange(128), np.arange(128), indexing="ij")
    cm[:, 3] = -1.0 * (jj >= ss)
    cm[:, 4] = -1.0 * (jj < ss)
    cm[:, 5] = -np.eye(128)
    cm[:, 6] = 1.0 * (jj < ss)
    sh["cmat"] = _bf(cm.reshape(128, 896))
    mk = np.zeros((128, 4, 512), np.float32)
    for i in range(4):
        kpos = 128 * i + np.arange(128)[:, None]
        mk[:, i] = np.where(kpos >= np.arange(512)[None, :], MASKV, 0.0)
    sh["mneg"] = _bf(mk.reshape(128, 2048))
    x = f(inp["x"])
    p = f(inp["p"])
    maps = []
    for r in range(8):
        b, c = r // 4, r % 4
        m = dict(sh)
        m["xT"] = np.ascontiguousarray(x[b, c * T:(c + 1) * T, :].T)
        m["wqk"] = np.ascontiguousarray(sh["wqk"][:, [2 * c, 2 * c + 1, 8 + 2 * c, 8 + 2 * c + 1]])
        m["wv"] = np.ascontiguousarray(sh["wv"][:, :, :, 256 * c:256 * (c + 1)])
        m["pT"] = np.ascontiguousarray(p[:, b, c * T:(c + 1) * T, :].transpose(0, 2, 1))
        s = np.zeros((128, 4), np.float32)
        if c > 0:
            s[:, c - 1] = 1.0
        m["sel"] = s
        ic = np.zeros((128, 4, 16), np.float32)
        for g in range(4):
            w = 2 << g
            if c == 0:
                ic[:, g] = 1.0 / np.minimum(np.arange(16) + 1, w)
            else:
                ic[:, g] = 1.0 / w
        m["icnt"] = ic.reshape(128, 64)
        maps.append(m)
    return maps


_NC_CACHE = {}


def kernel(_stop_after=None, _debug=False, **inputs):
    if _stop_after is not None:
        nl = _stop_after[1] + 1
        maps = host_layout(inputs, nl)
        nc = build(_stop_after, nl=nl, debug=_debug)
        res = run_bass_kernel_spmd(nc, maps, core_ids=list(range(8)))
        _NC_CACHE["res"] = res
        out = np.zeros((2, S, D), np.float32)
        for r in range(8):
            b, c = r // 4, r % 4
            out[b, c * T:(c + 1) * T, :] = np.asarray(res.results[r]["yT"]).T
        return out
    maps = host_layout(inputs)
    if False:
        _NC_CACHE["nc"] = build(_stop_after)
    if "nc" not in _NC_CACHE:
        _NC_CACHE["nc"] = build()
    res = run_bass_kernel_spmd(_NC_CACHE["nc"], maps, core_ids=list(range(8)))
    out = np.zeros((2, S, D), np.float32)
    for r in range(8):
        b, c = r // 4, r % 4
        out[b, c * T:(c + 1) * T, :] = np.asarray(res.results[r]["yT"]).T
    return out
```

```python
import contextlib
import numpy as np
import ml_dtypes
import concourse.bass as bass
import concourse.mybir as mybir
from concourse.bass_utils import run_bass_kernel_spmd

F32 = mybir.dt.float32
BF16 = mybir.dt.bfloat16
AF = mybir.ActivationFunctionType
ALU = mybir.AluOpType

D = 1024
T = 2048
S = 8192
DFF = 2816
NF = 22
DEPTH = 4
EPS = 1e-6
MASKV = -128.0
ARENA = 54272
COMPUTE = ("pe", "act", "dve")


class Sched:
    def __init__(self, nc, es):
        self.nc = nc
        self.es = es
        self.prog = {e: [] for e in ("pe", "act", "dve", "pool", "sp")}
        self.esem = {e: es.enter_context(nc.semaphore("s_" + e)) for e in COMPUTE}
        self.ecnt = {e: 0 for e in COMPUTE}
        self.slots = {}
        self.waited = {e: {} for e in self.prog}
        self.lastw = {}
        self.readers = {}
        self.barrier_toks = []
        self.pending = {e: [] for e in self.prog}
        self.last_barrier = []

    def slot(self, name):
        if name not in self.slots:
            self.slots[name] = [self.es.enter_context(self.nc.semaphore("d_" + name)), 0]
        return self.slots[name]

    def barrier(self, engines=("pe", "act", "dve", "sp", "pool")):
        toks = [(self.esem[e], self.ecnt[e], e) for e in COMPUTE if self.ecnt[e] > 0]
        toks += [(s[0], s[1], None) for s in self.slots.values() if s[1] > 0]
        self.last_barrier = list(toks)
        for e in engines:
            self.pending[e] = list(toks)

    def op(self, eng, fn, reads=(), writes=(), slot=None, ndma=1, after_barrier=False):
        toks = list(self.pending[eng])
        self.pending[eng] = []
        if after_barrier:
            toks += self.last_barrier
        for k in reads:
            if k in self.lastw:
                toks.append(self.lastw[k])
        for k in writes:
            if k in self.lastw:
                toks.append(self.lastw[k])
            toks.extend(self.readers.get(k, ()))
        waits = {}
        for (sem, val, src) in toks:
            if eng == "pe" and src == "pe":
                continue
            key = id(sem)
            if key not in waits or waits[key][1] < val:
                waits[key] = (sem, val)
        wl = []
        for key, (sem, val) in waits.items():
            if self.waited[eng].get(key, 0) >= val:
                continue
            self.waited[eng][key] = val
            wl.append((sem, val))
        if eng in COMPUTE:
            self.ecnt[eng] += 1
            tok = (self.esem[eng], self.ecnt[eng], eng)
            inc = (self.esem[eng], 1)
        else:
            sl = self.slot(slot)
            step = 1 if slot.startswith("cc_") else 16
            sl[1] += step * ndma
            tok = (sl[0], sl[1], None)
            inc = (sl[0], step)
        self.prog[eng].append((wl, fn, inc))
        for k in writes:
            self.lastw[k] = tok
            self.readers[k] = []
        for k in reads:
            self.readers.setdefault(k, []).append(tok)
        return tok

    def replay(self, eng_name, eng):
        compute = eng_name in COMPUTE
        for (wl, fn, inc) in self.prog[eng_name]:
            for (sem, val) in wl:
                eng.wait_ge(sem, val)
            r = fn(eng)
            if r is None:
                continue
            if not isinstance(r, (list, tuple)):
                r = [r]
            if compute:
                r[-1].then_inc(inc[0], inc[1])
            else:
                for ins in r:
                    ins.then_inc(inc[0], inc[1])


def build(stop_after=None, nl=DEPTH, debug=False):
    nc = bass.Bass("TRN2", target_bir_lowering=False)
    es = contextlib.ExitStack()

    def din(name, shape, dt=F32):
        return nc.dram_tensor(name, list(shape), dt, kind="ExternalInput").ap()

    xT_d = din("xT", [D, T])
    pT_d = din("pT", [DEPTH, 256, T])
    wgu_d = din("wgu", [nl, 2, NF, 128, 8, 256])
    wdn_d = din("wdn", [nl, 2, 8, 128, NF * 128])
    wqk_d = din("wqk", [2, 4, 128, 8, 128])
    wv_d = din("wv", [2, 128, 8, 256])
    wo_d = din("wo", [2, 8, 128, 8, 128])
    wpi_d = din("wpi", [2, 8, 128, 8, 128])
    wpg_d = din("wpg", [DEPTH, 8, 128, 8, 128])
    wgrp_d = din("wgrp", [2, 128, 4 * 2 * 256])
    wpp_d = din("wpp", [DEPTH, 128, 2, 1024])
    gn_d = din("gn", [128, 4 * DEPTH * 8])
    qkg_d = din("qkg", [128, 4])
    psc_d = din("psc", [128, 16])
    sel_d = din("sel", [128, 4])
    icnt_d = din("icnt", [128, 64])
    cmat_d = din("cmat", [128, 7 * 128], BF16)
    mneg_d = din("mneg", [128, 4 * 512], BF16)
    yT_d = nc.dram_tensor("yT", [D, T], F32, kind="ExternalOutput").ap()
    if debug:
        dbg_mine = nc.dram_tensor("dbg_mine", [4 * 768, 2048], BF16, kind="ExternalOutput").ap()
        dbg_ogin = nc.dram_tensor("dbg_ogin", [1024, 2048], BF16, kind="ExternalOutput").ap()

    hgin_a = [nc.dram_tensor(f"hgina{j}", [2048, 1024], BF16, kind="Internal").ap() for j in range(2)]
    hgout_a = [nc.dram_tensor(f"hgouta{j}", [4 * 2048, 1024], BF16, kind="Internal").ap() for j in range(2)]
    ogin = [nc.dram_tensor(f"ogin{j}", [1024, 2048], BF16, kind="Internal").ap() for j in range(2)]
    ogout = [nc.dram_tensor(f"ogout{j}", [4096, 2048], BF16, kind="Internal").ap() for j in range(2)]
    mine = [nc.dram_tensor(f"mine{j}", [4 * 768, 2048], BF16, kind="Internal").ap() for j in range(2)]
    hgin = [nc.dram_tensor(f"hgin{j}", [128, 128], F32, kind="Internal").ap() for j in range(2)]
    hgout = [nc.dram_tensor(f"hgout{j}", [4 * 128, 128], F32, kind="Internal").ap() for j in range(2)]

    def sb(name, shape, dt):
        return es.enter_context(nc.sbuf_tensor(name, list(shape), dt))

    xT = sb("xT_sb", [128, 8, T], F32)
    arena = sb("arena", [128, ARENA], BF16)
    wgu = sb("wgu_sb", [128, 2, 8 * 256], BF16)
    wdn = sb("wdn_sb", [128, 2, NF * 128], BF16)
    w8 = sb("w8_sb", [128, 3, 8 * 128], BF16)
    cmat = sb("cmat_sb", [128, 7, 128], BF16)
    mneg = sb("mneg_sb", [128, 4, 512], BF16)
    gn = sb("gn_sb", [128, 4, DEPTH, 8], F32)
    qkg = sb("qkg_sb", [128, 2, 2], F32)
    qg8 = sb("qg8_sb", [128, 2], F32)
    psc = sb("psc_sb", [128, 2, 8], F32)
    sel = sb("sel_sb", [128, 4], F32)
    icnt = sb("icnt_sb", [128, 4, 16], F32)
    wgrp = sb("wgrp_sb", [128, 4, 2, 256], BF16)
    ps = es.enter_context(nc.psum_tensor("ps", [128, 8, 512], F32))

    IDENT, ONES_MS, ONES_HD, NEGTRI, NEGREST, NEGIDENT, TRI01 = range(7)

    def av(off, shape, dt=BF16):
        n = int(np.prod(shape))
        if dt == F32:
            v = arena[:, off:off + 2 * n].bitcast(F32)
        else:
            v = arena[:, off:off + n]
        if len(shape) == 1:
            return v
        if len(shape) == 2:
            return v.rearrange("p (a b) -> p a b", a=shape[0])
        return v.rearrange("p (a b c) -> p a b c", a=shape[0], b=shape[1])

    sc = Sched(nc, es)
    me4 = {}

    def ld_consts(e):
        return [
            e.dma_start(out=cmat[:].rearrange("p a b -> p (a b)"), in_=cmat_d),
            e.dma_start(out=mneg[:].rearrange("p a b -> p (a b)"), in_=mneg_d),
            e.dma_start(out=gn[:].rearrange("p a b c -> p (a b c)"), in_=gn_d),
            e.dma_start(out=qkg[:].rearrange("p a b -> p (a b)"), in_=qkg_d),
            e.dma_start(out=psc[:].rearrange("p a b -> p (a b)"), in_=psc_d),
            e.dma_start(out=sel[:], in_=sel_d),
            e.dma_start(out=icnt[:].rearrange("p a b -> p (a b)"), in_=icnt_d),
        ]
    sc.op("sp", ld_consts, writes=["consts"], slot="consts", ndma=7)
    for k in range(8):
        sc.op("sp", lambda e, k=k: e.dma_start(out=xT[:, k, :], in_=xT_d[k * 128:(k + 1) * 128, :]),
              writes=[("x", k, t) for t in range(4)], slot=f"xld{k}")
    sc.op("dve", lambda e: e.tensor_scalar(out=qg8[:], in0=qkg[:, 0, :], scalar1=0.125, scalar2=None, op0=ALU.mult),
          reads=["consts"], writes=["qg8"])

    ring_cnt = {"wgu": 0, "wdn": 0, "w8": 0}

    def load_w(kind, src_ap, nbuf, view):
        i = ring_cnt[kind]
        ring_cnt[kind] += 1
        b = i % nbuf
        key = (kind, b)
        sc.op("pool", lambda e: e.dma_start(out=view(b), in_=src_ap), writes=[key], slot=f"{kind}{b}")
        return b, key

    def load_wgu(l, which, f):
        return load_w("wgu", wgu_d[l, which, f].rearrange("p k c -> p (k c)"), 2, lambda b: wgu[:, b, :])

    def load_wdn(l, which, dc):
        i = ring_cnt["wdn"]
        ring_cnt["wdn"] += 1
        b = i % 2
        key = ("wdn", b)
        h = NF * 64

        def fn(e):
            return [e.dma_start(out=wdn[:, b, 0:h], in_=wdn_d[l, which, dc, :, 0:h]),
                    e.dma_start(out=wdn[:, b, h:2 * h], in_=wdn_d[l, which, dc, :, h:2 * h])]
        sc.op("pool", fn, writes=[key], slot=f"wdn{b}", ndma=2)
        return b, key

    def load_w8(src3):
        return load_w("w8", src3.rearrange("p k c -> p (k c)"), 3, lambda b: w8[:, b, :])

    O_HT = 0
    O_SQ, O_LN, O_RS, O_SG = 44032, 48128, 49152, 50176

    def mm_group(e, out, pairs, start=True, stop=True):
        r = None
        n = len(pairs)
        for i, (l, rh) in enumerate(pairs):
            r = e.matmul(out, lhsT=l, rhs=rh, start=(start and i == 0), stop=(stop and i == n - 1),
                         skip_group_check=True)
        return r

    def norm_half(half, kind, layer, ssbank=6, hoff=0, hkey="hT"):
        hT = av(hoff, [8, 1024])
        sq = av(O_SQ, [8, 512])
        lnv = av(O_LN, [512], F32)
        rstd = av(O_RS, [512], F32)
        for tt in range(2):
            gt = half * 2 + tt
            c0 = gt * 512
            sc.op("act", lambda e, c0=c0: e.activation(out=sq, in_=xT[:, :, c0:c0 + 512], func=AF.Square),
                  reads=[("x", k, gt) for k in range(8)], writes=["sq"])
            sc.op("pe", lambda e: mm_group(e, ps[:, ssbank, :], [(cmat[:, ONES_MS, :], sq[:, k, :]) for k in range(8)]),
                  reads=["sq", "consts"], writes=[("ps", ssbank)])
            sc.op("act", lambda e: e.activation(out=lnv, in_=ps[:, ssbank, :], func=AF.Ln, bias=EPS, scale=1.0),
                  reads=[("ps", ssbank)], writes=["lnv"])
            sc.op("act", lambda e: e.activation(out=rstd, in_=lnv, func=AF.Exp, scale=-0.5),
                  reads=["lnv"], writes=["rstd"])
            for k in range(8):
                sc.op("dve", lambda e, k=k, c0=c0, tt=tt: e.scalar_tensor_tensor(
                    out=hT[:, k, tt * 512:(tt + 1) * 512], in0=xT[:, k, c0:c0 + 512],
                    scalar=gn[:, kind, layer, k:k + 1], in1=rstd, op0=ALU.mult, op1=ALU.mult),
                    reads=[("x", k, gt), "rstd", "consts"], writes=[(hkey, k, tt)])
        return hT

    def mix_prenorm(layer, half, defer=False):
        j = layer // 2
        hT = norm_half(half, 1, layer)
        hv = hgin_a[j].rearrange("(h k p) t -> h p k t", h=2, p=128)
        sc.op("sp", lambda e: e.dma_start(out=hv[half], in_=hT),
              reads=[("hT", k, tt) for k in range(8) for tt in range(2)], writes=[("hgin_a", half)], slot="hgst")
        def emit_ag():
            sc.op("pool", lambda e: [e.collective_compute("AllGather", ALU.bypass, replica_groups=[[0, 1, 2, 3], [4, 5, 6, 7]],
                                                          ins=[hgin_a[j][(half * 4 + q) * 256:(half * 4 + q + 1) * 256, :]],
                                                          outs=[hgout_a[j][(half * 4 + q) * 1024:(half * 4 + q + 1) * 1024, :]])
                                     for q in range(4)],
                  reads=[("hgin_a", half)], writes=[("hgout_a", half)], slot="cc_a", ndma=4)
        if defer:
            return emit_ag
        emit_ag()

    def ffn(layer, which):
        sc.barrier(("pe", "act", "dve", "sp"))
        kind = 0 if which == 0 else 2
        aT = av(8192, [NF, 1024])
        sgt = av(O_SG, [2, 512], F32)
        hTs = {0: norm_half(0, kind, layer)}
        deferred = []
        want_pre = False
        for half in range(2):
            hT = hTs[half]
            hkey = "hT" if half == 0 else "hT2"
            cnt = 0
            for f in range(NF):
                if half == 0 and f == NF // 2:
                    hTs[1] = norm_half(1, kind, layer, hoff=30720, hkey="hT2")
                b, wkey = load_wgu(layer, which, f)
                if half == 1 and f == 2 and want_pre:
                    deferred.append(mix_prenorm(layer, 0, defer=True))
                if half == 1 and f == 10 and deferred:
                    deferred.pop()()
                for tt in range(2):
                    gb = cnt % 2
                    cnt += 1
                    wv_ = wgu[:, b, :].rearrange("p (k c) -> p k c", k=8)
                    sc.op("pe", lambda e, wv_=wv_, tt=tt, gb=gb, hT=hT: [
                        mm_group(e, ps[:, gb, :], [(wv_[:, k, 0:128], hT[:, k, tt * 512:(tt + 1) * 512]) for k in range(8)]),
                        mm_group(e, ps[:, 2 + gb, :], [(wv_[:, k, 128:256], hT[:, k, tt * 512:(tt + 1) * 512]) for k in range(8)])],
                        reads=[wkey] + [(hkey, k, tt) for k in range(8)], writes=[("ps", gb), ("ps", 2 + gb)])
                    sc.op("act", lambda e, gb=gb: e.activation(out=sgt[:, gb, :], in_=ps[:, gb, :], func=AF.Silu),
                          reads=[("ps", gb)], writes=[("sgt", gb)])
                    sc.op("dve", lambda e, gb=gb, f=f, tt=tt: e.tensor_tensor(
                        out=aT[:, f, tt * 512:(tt + 1) * 512], in0=sgt[:, gb, :], in1=ps[:, 2 + gb, :], op=ALU.mult),
                        reads=[("sgt", gb), ("ps", 2 + gb)], writes=[("aT", f, tt)])
            cnt = 0
            for dc in range(8):
                b, wkey = load_wdn(layer, which, dc)
                wv_ = wdn[:, b, :].rearrange("p (f c) -> p f c", f=NF)
                for tt in range(2):
                    db = 4 + cnt % 2
                    cnt += 1
                    gt = half * 2 + tt
                    sc.op("pe", lambda e, wv_=wv_, tt=tt, db=db: mm_group(
                        e, ps[:, db, :], [(wv_[:, f, :], aT[:, f, tt * 512:(tt + 1) * 512]) for f in range(NF)]),
                        reads=[wkey] + [("aT", f, tt) for f in range(NF)], writes=[("ps", db)])
                    sc.op("dve", lambda e, dc=dc, gt=gt, db=db: e.scalar_tensor_tensor(
                        out=xT[:, dc, gt * 512:(gt + 1) * 512], in0=ps[:, db, :], scalar=0.5,
                        in1=xT[:, dc, gt * 512:(gt + 1) * 512], op0=ALU.mult, op1=ALU.add),
                        reads=[("ps", db), ("x", dc, gt)], writes=[("x", dc, gt)])
            if half == 0 and which == 0 and layer % 2 == 0:
                want_pre = True

    def ple(layer):
        sc.barrier(("pe", "act", "dve", "sp"))
        pTb = av(8192, [2, T])
        wpp = av(12288, [2, 1024])
        sgm = av(14336, [2, 512], F32)
        tmp = av(16384, [2, 512], F32)
        sc.op("pool", lambda e: [e.dma_start(out=pTb, in_=pT_d[layer].rearrange("(k p) t -> p k t", p=128)),
                                 e.dma_start(out=wpp, in_=wpp_d[layer])],
              writes=["pTb", "wpp"], slot="plew", ndma=2, after_barrier=True)
        cnt = 0
        for half in range(2):
            hT = norm_half(half, 3, layer)
            for dc in range(8):
                b, wkey = load_w8(wpg_d[layer, dc])
                wv_ = w8[:, b, :].rearrange("p (k c) -> p k c", k=8)
                for tt in range(2):
                    gb = cnt % 2
                    cnt += 1
                    gt = half * 2 + tt
                    sc.op("pe", lambda e, wv_=wv_, tt=tt, gb=gb, dc=dc, gt=gt: [
                        mm_group(e, ps[:, gb, :], [(wv_[:, k, :], hT[:, k, tt * 512:(tt + 1) * 512]) for k in range(8)]),
                        mm_group(e, ps[:, 2 + gb, :], [(wpp[:, k, dc * 128:(dc + 1) * 128], pTb[:, k, gt * 512:(gt + 1) * 512]) for k in range(2)])],
                        reads=[wkey, "pTb", "wpp"] + [("hT", k, tt) for k in range(8)], writes=[("ps", gb), ("ps", 2 + gb)])
                    sc.op("act", lambda e, gb=gb: e.activation(out=sgm[:, gb, :], in_=ps[:, gb, :], func=AF.Sigmoid),
                          reads=[("ps", gb)], writes=[("sgm", gb)])
                    sc.op("dve", lambda e, gb=gb: e.tensor_tensor(out=tmp[:, gb, :], in0=sgm[:, gb, :], in1=ps[:, 2 + gb, :], op=ALU.mult),
                          reads=[("sgm", gb), ("ps", 2 + gb)], writes=[("tmp", gb)])
                    sc.op("dve", lambda e, gb=gb, dc=dc, gt=gt: e.tensor_tensor(
                        out=xT[:, dc, gt * 512:(gt + 1) * 512], in0=tmp[:, gb, :], in1=xT[:, dc, gt * 512:(gt + 1) * 512], op=ALU.add),
                        reads=[("tmp", gb), ("x", dc, gt)], writes=[("x", dc, gt)])

    def pool_mixer(layer):
        j = layer // 2
        sc.barrier(("pe", "act", "dve", "sp"))
        U = [av(0, [8, 1040], F32), av(16640, [8, 1040], F32)]
        O_H = 33280
        hT = av(O_H, [8, 1024])
        tmpS = av(41472, [2, 1040], F32)
        hal = av(45632, [4, 128], F32)
        sc.op("pool", lambda e: e.dma_start(out=wgrp[:].rearrange("p a b c -> p (a b c)"), in_=wgrp_d[j]),
              writes=["wgrp"], slot="wgrp")

        def do_norm(half):
            sq = av(46656, [8, 512])
            lnv = tmpS[:, 0, 0:512]
            rstd = tmpS[:, 1, 0:512]
            for tt in range(2):
                gt = half * 2 + tt
                c0 = gt * 512
                sc.op("act", lambda e, c0=c0: e.activation(out=sq, in_=xT[:, :, c0:c0 + 512], func=AF.Square),
                      reads=[("x", k, gt) for k in range(8)], writes=["sq"])
                sc.op("pe", lambda e: mm_group(e, ps[:, 6, :], [(cmat[:, ONES_MS, :], sq[:, k, :]) for k in range(8)]),
                      reads=["sq", "consts"], writes=[("ps", 6)])
                sc.op("act", lambda e: e.activation(out=lnv, in_=ps[:, 6, :], func=AF.Ln, bias=EPS, scale=1.0),
                      reads=[("ps", 6)], writes=[("tmpS", 0)])
                sc.op("act", lambda e: e.activation(out=rstd, in_=lnv, func=AF.Exp, scale=-0.5),
                      reads=[("tmpS", 0)], writes=[("tmpS", 1)])
                for k in range(8):
                    sc.op("dve", lambda e, k=k, c0=c0, tt=tt: e.scalar_tensor_tensor(
                        out=hT[:, k, tt * 512:(tt + 1) * 512], in0=xT[:, k, c0:c0 + 512],
                        scalar=gn[:, 1, layer, k:k + 1], in1=rstd, op0=ALU.mult, op1=ALU.mult),
                        reads=[("x", k, gt), ("tmpS", 1), "consts"], writes=[("hT", k, tt), ("pl", k)])

        def compute_u(half):
            cnt = 0
            for uc in range(8):
                b, wkey = load_w8(wpi_d[j, uc])
                wv_ = w8[:, b, :].rearrange("p (k c) -> p k c", k=8)
                for tt in range(2):
                    gb = cnt % 2
                    cnt += 1
                    sc.op("pe", lambda e, wv_=wv_, tt=tt, gb=gb: mm_group(
                        e, ps[:, gb, :], [(wv_[:, k, :], hT[:, k, tt * 512:(tt + 1) * 512]) for k in range(8)]),
                        reads=[wkey] + [("hT", k, tt) for k in range(8)], writes=[("ps", gb)])
                    sc.op("act", lambda e, gb=gb, uc=uc, tt=tt: e.activation(
                        out=U[half][:, uc, 16 + tt * 512:16 + (tt + 1) * 512], in_=ps[:, gb, :], func=AF.Copy),
                        reads=[("ps", gb)], writes=[("U", half, uc)])

        def pool_and_mix(half):
            pooled = hT
            for uc in range(8):
                g = uc // 2
                w = 2 << g
                cur = U[half][:, uc, :]
                lo = 0
                nsteps = g + 1
                srcbuf = cur
                for st in range(nsteps):
                    sh = 1 << st
                    lo2 = lo + sh
                    dst = tmpS[:, st % 2, :]
                    sc.op("dve", lambda e, dst=dst, srcbuf=srcbuf, lo2=lo2, sh=sh: e.tensor_tensor(
                        out=dst[:, lo2:1040], in0=srcbuf[:, lo2:1040], in1=srcbuf[:, lo2 - sh:1040 - sh], op=ALU.add),
                        reads=[("U", half, uc), ("tmpS", 0), ("tmpS", 1)], writes=[("tmpS", st % 2)])
                    srcbuf = dst
                    lo = lo2
                sfin = srcbuf
                sc.op("dve", lambda e, sfin=sfin, cur=cur, uc=uc, w=w: e.scalar_tensor_tensor(
                    out=pooled[:, uc, :], in0=sfin[:, 16:1040], scalar=1.0 / w, in1=cur[:, 16:1040],
                    op0=ALU.mult, op1=ALU.subtract),
                    reads=[("tmpS", 0), ("tmpS", 1), ("U", half, uc)],
                    writes=[("pl", uc), ("hT", uc, 0), ("hT", uc, 1)])
                if half == 0:
                    t16 = tmpS[:, (nsteps) % 2, 0:16]
                    sc.op("dve", lambda e, t16=t16, sfin=sfin, g=g: e.tensor_tensor(
                        out=t16, in0=sfin[:, 16:32], in1=icnt[:, g, :], op=ALU.mult),
                        reads=[("tmpS", 0), ("tmpS", 1), "consts"], writes=[("tmpS", nsteps % 2)])
                    sc.op("dve", lambda e, t16=t16, cur=cur, uc=uc: e.tensor_tensor(
                        out=pooled[:, uc, 0:16], in0=t16, in1=cur[:, 16:32], op=ALU.subtract),
                        reads=[("tmpS", 0), ("tmpS", 1), ("U", half, uc), ("pl", uc)], writes=[("pl", uc)])
            cnt = 0
            for g in range(4):
                for dd in range(2):
                    dc = 2 * g + dd
                    for tt in range(2):
                        db = 4 + cnt % 2
                        cnt += 1
                        gt = half * 2 + tt
                        sc.op("pe", lambda e, g=g, dd=dd, tt=tt, db=db: mm_group(
                            e, ps[:, db, :], [(wgrp[:, g, cc, dd * 128:(dd + 1) * 128], pooled[:, 2 * g + cc, tt * 512:(tt + 1) * 512]) for cc in range(2)]),
                            reads=["wgrp", ("pl", 2 * g), ("pl", 2 * g + 1)], writes=[("ps", db)])
                        sc.op("dve", lambda e, dc=dc, gt=gt, db=db: e.scalar_tensor_tensor(
                            out=xT[:, dc, gt * 512:(gt + 1) * 512], in0=ps[:, db, :], scalar=psc[:, j, dc:dc + 1],
                            in1=xT[:, dc, gt * 512:(gt + 1) * 512], op0=ALU.mult, op1=ALU.add),
                            reads=[("ps", db), ("x", dc, gt), "consts"], writes=[("x", dc, gt)])

        do_norm(1)
        compute_u(1)
        sc.op("sp", lambda e: e.dma_start(out=hgin[j].rearrange("p (k c) -> p k c", k=8), in_=U[1][:, :, 1024:1040]),
              reads=[("U", 1, uc) for uc in range(8)], writes=["hgin"], slot="hgin")
        sc.op("pool", lambda e: e.collective_compute("AllGather", ALU.bypass, replica_groups=[[0, 1, 2, 3], [4, 5, 6, 7]],
                                                     ins=[hgin[j]], outs=[hgout[j]]),
              reads=["hgin"], writes=["hgout"], slot="cc_h")
        do_norm(0)
        compute_u(0)
        for uc in range(8):
            sc.op("dve", lambda e, uc=uc: e.tensor_copy(out=U[1][:, uc, 0:16], in_=U[0][:, uc, 1024:1040]),
                  reads=[("U", 0, uc)], writes=[("U", 1, uc)])
        pool_and_mix(1)
        sc.op("sp", lambda e: e.dma_start(out=hal, in_=hgout[j].rearrange("(i p) c -> p i c", p=128)),
              reads=["hgout"], writes=["hal"], slot="hal")
        for uc in range(8):
            halv = hal.rearrange("p i (k c) -> p i k c", k=8)
            sc.op("dve", lambda e, uc=uc, halv=halv: e.tensor_scalar(
                out=U[0][:, uc, 0:16], in0=halv[:, 0, uc, :], scalar1=sel[:, 0:1], scalar2=None, op0=ALU.mult),
                reads=["hal", "consts"], writes=[("U", 0, uc)])
            for i in range(1, 4):
                sc.op("dve", lambda e, uc=uc, i=i, halv=halv: e.scalar_tensor_tensor(
                    out=U[0][:, uc, 0:16], in0=halv[:, i, uc, :], scalar=sel[:, i:i + 1], in1=U[0][:, uc, 0:16],
                    op0=ALU.mult, op1=ALU.add),
                    reads=["hal", "consts", ("U", 0, uc)], writes=[("U", 0, uc)])
        pool_and_mix(0)

    def attention(layer):
        j = layer // 2
        sc.barrier(("pe", "act", "dve", "sp"))
        hTi = [av(8192, [8, 1024]), av(16384, [8, 1024])]
        wqb = av(24576, [4, 8 * 128])
        wvb = av(28672, [8, 256])
        qst = av(30720, [2, 512])
        vst = av(31744, [4, 256])
        sqh = av(32768, [2, 512])
        lnv = av(O_LN, [512], F32)
        rstd = av(O_RS, [512], F32)
        sc.op("pool", lambda e: [e.dma_start(out=wqb[:, qc, :], in_=wqk_d[j, qc].rearrange("p k c -> p (k c)")) for qc in range(4)]
              + [e.dma_start(out=wvb, in_=wv_d[j])],
              writes=["wqb", "wvb"], slot="wqv", ndma=5, after_barrier=True)
        mix_prenorm(layer, 1)
        hgv = hgout_a[j].rearrange("(c i k2 p) t -> c i p k2 t", c=8, i=4, k2=2)
        minev = mine[j].rearrange("(i r) c -> i r c", i=4)
        cq = 0
        cv = 0
        it = 0

        def hload(it_):
            half_, i_ = it_ // 4, it_ % 4
            hb_ = it_ % 2
            sc.op("pool", lambda e: [e.dma_start(out=hTi[hb_][:, 2 * q:2 * q + 2, :], in_=hgv[half_ * 4 + q, i_]) for q in range(4)],
                  reads=[("hgout_a", half_)], writes=[("hTi", hb_)], slot=f"hld{hb_}", ndma=4)
        hload(0)
        for half in range(2):
            for i in range(4):
                hb = it % 2
                it += 1
                hX = hTi[hb]
                if it < 8:
                    hload(it)
                PB = (0, 1, 6, 7)
                groups = [(qc, tt) for qc in range(4) for tt in range(2)]

                def g_mm(qc, tt, pbk, hX=hX, hb=hb):
                    wv_ = wqb[:, qc, :].rearrange("p (k c) -> p k c", k=8)
                    sc.op("pe", lambda e: mm_group(
                        e, ps[:, pbk, :], [(wv_[:, k, :], hX[:, k, tt * 512:(tt + 1) * 512]) for k in range(8)]),
                        reads=["wqb", ("hTi", hb)], writes=[("ps", pbk)])

                def g_rest(qc, tt, pbk, gb, i=i, half=half):
                    isk = qc // 2
                    c2 = qc % 2
                    sc.op("act", lambda e: e.activation(out=sqh[:, gb, :], in_=ps[:, pbk, :], func=AF.Square),
                          reads=[("ps", pbk)], writes=[("sqh", gb)])
                    sc.op("pe", lambda e: mm_group(e, ps[:, 2 + gb, :], [(cmat[:, ONES_HD, :], sqh[:, gb, :])]),
                          reads=[("sqh", gb), "consts"], writes=[("ps", 2 + gb)])
                    sc.op("act", lambda e: e.activation(out=lnv, in_=ps[:, 2 + gb, :], func=AF.Ln, bias=EPS, scale=1.0),
                          reads=[("ps", 2 + gb)], writes=["lnv"])
                    sc.op("act", lambda e: e.activation(out=rstd, in_=lnv, func=AF.Exp, scale=-0.5),
                          reads=["lnv"], writes=["rstd"])
                    gsc = qkg[:, 1, j:j + 1] if isk else qg8[:, j:j + 1]
                    sc.op("dve", lambda e: e.scalar_tensor_tensor(
                        out=qst[:, gb, :], in0=ps[:, pbk, :], scalar=gsc, in1=rstd, op0=ALU.mult, op1=ALU.mult),
                        reads=[("ps", pbk), "rstd", "consts", "qg8"], writes=[("qst", gb)])
                    r0 = isk * 256 + c2 * 128
                    col = half * 1024 + tt * 512
                    sc.op("sp", lambda e: e.dma_start(out=minev[i, r0:r0 + 128, col:col + 512], in_=qst[:, gb, :]),
                          reads=[("qst", gb)], writes=["mine"], slot=f"qst{gb}")

                idx = [cq + n for n in range(len(groups))]
                cq += len(groups)
                g_mm(*groups[0], PB[idx[0] % 4])
                for gi in range(len(groups)):
                    if gi + 1 < len(groups):
                        g_mm(*groups[gi + 1], PB[idx[gi + 1] % 4])
                    g_rest(*groups[gi], PB[idx[gi] % 4], idx[gi] % 2)
                vreg = minev[i, 512:768, :].rearrange("r (t8 c) -> (r t8) c", c=256)
                for tb in range(8):
                    gb = cv % 2
                    vb = cv % 4
                    cv += 1
                    sc.op("pe", lambda e, tb=tb, gb=gb, hX=hX: mm_group(
                        e, ps[:, 4 + gb, 0:256], [(hX[:, k, tb * 128:(tb + 1) * 128], wvb[:, k, :]) for k in range(8)]),
                        reads=["wvb", ("hTi", hb)], writes=[("ps", 4 + gb)])
                    sc.op("act", lambda e, gb=gb, vb=vb: e.activation(out=vst[:, vb, :], in_=ps[:, 4 + gb, 0:256], func=AF.Copy),
                          reads=[("ps", 4 + gb)], writes=[("vst", vb)])
                    tok0 = half * 1024 + tb * 128
                    sc.op("sp", lambda e, vb=vb, tok0=tok0, vreg=vreg: e.dma_start(out=vreg[tok0:tok0 + 128, :], in_=vst[:, vb, :]),
                          reads=[("vst", vb)], writes=["mine"], slot=f"vst{vb}")
        sc.barrier(("pe", "act", "dve", "sp"))

        Kst = av(0, [2, S])
        Vp = [av(16384, [64, 128]), av(24576, [64, 128])]
        E = av(32768, [2, 1024], F32)
        PP = av(36864, [2, 2 * 1024])
        A = av(40960, [2, 1024])
        Qd = av(43008, [2, 1024])
        Qz = av(45056, [2, 1024])
        Osb = av(47104, [2, 512])
        Pd = av(48128, [3, 1024])
        Ad = av(51200, [3, 1024])
        minev = mine[j].rearrange("(i r) c -> i r c", i=4)
        minevv = mine[j].rearrange("(i r) (t8 c) -> i (r t8) c", i=4, c=256)[:, 4096:6144, :].rearrange(
            "i (b s) c -> i s b c", s=128)

        def mysl(e):
            return e.partition_id() % 4

        def ag_o(hp_, tc):
            sc.op("pool", lambda e: e.collective_compute("AllGather", ALU.bypass, replica_groups=[[0, 1, 2, 3], [4, 5, 6, 7]],
                                                         ins=[ogin[j][(tc * 2 + hp_) * 128:(tc * 2 + hp_ + 1) * 128, :]],
                                                         outs=[ogout[j][(tc * 2 + hp_) * 512:(tc * 2 + hp_ + 1) * 512, :]]),
                  reads=[("ogin", hp_, tc)], writes=[("ogout", hp_, tc)], slot="cc_o", ndma=1)

        for hpi, hp in enumerate((0, 1)):
            if hpi >= 1:
                sc.barrier(("pe", "act", "dve", "sp"))
                for tc_ in range(4):
                    ag_o(0, tc_)
            sc.op("dve", lambda e: e.memset(arena[:, 16384:32768], 0.0), writes=["Vp"])
            sc.op("dve", lambda e: e.memset(arena[:, 45056:47104], 0.0), writes=["Q2z", ("Q2", 0), ("Q2", 1)])
            sc.op("dve", lambda e: e.memset(arena[:, 48128:54272], 0.0), writes=[("Pd", n_) for n_ in range(3)] + [("Ad", n_) for n_ in range(3)])
            sc.op("dve", lambda e: e.memset(Kst[64:128, 0, S - 128:S], 0.0), writes=["Kst"])
            sc.op("dve", lambda e: e.memset(Kst[64:128, 1, S - 128:S], 0.0), writes=["Kst"])

            def ldk(e, hp=hp):
                r = []
                for i in range(4):
                    for h in range(2):
                        rr = 256 + hp * 128 + h * 64
                        src = minev[i, rr:rr + 64, :]
                        r.append(e.dma_start(out=Kst[0:64, h, i * 2048:(i + 1) * 2048], in_=src))
                        if i == 0:
                            r.append(e.dma_start(out=Kst[64:128, h, 0:1920], in_=src[:, 128:2048]))
                        else:
                            r.append(e.dma_start(out=Kst[64:128, h, i * 2048 - 128:(i + 1) * 2048 - 128], in_=src))
                return r
            sc.op("sp", ldk, reads=["mine"], writes=["Kst"], slot="kld", ndma=16)

            def ldv(e, hp=hp):
                r = []
                for i in range(4):
                    for h in range(2):
                        c0 = hp * 128 + h * 64
                        for q4 in range(4):
                            src = minevv[i, :, q4 * 4:q4 * 4 + 4, c0:c0 + 64]
                            r.append(e.dma_start(out=Vp[h][:, i * 16 + q4 * 4:i * 16 + q4 * 4 + 4, h * 64:(h + 1) * 64], in_=src))
                return r
            sc.op("sp", ldv, reads=["mine"], writes=["Vp"], slot="vld", ndma=32)
            sc.op("dve", lambda e: e.tensor_scalar(out=Kst[64:128, :, 0:S - 128], in0=Kst[64:128, :, 0:S - 128],
                                                   scalar1=-1.0, scalar2=None, op0=ALU.mult),
                  reads=["Kst"], writes=["Kst"])

            steps = [(qt, kb) for qt in range(16) for kb in range(4 * qt + 3, -1, -1)]
            ns = len(steps)

            def ldq(qt, hp=hp):
                qb = qt % 2

                def fn(e):
                    i = qt // 4
                    c0 = (qt % 4) * 512
                    rr = hp * 128
                    src = minev[i, rr:rr + 128, c0:c0 + 512].rearrange("(h d) c -> d h c", h=2)
                    return [e.dma_start(out=Qd[0:64, qb, :].rearrange("p (h c) -> p h c", h=2), in_=src),
                            e.dma_start(out=Qd[64:128, qb, :].rearrange("p (h c) -> p h c", h=2), in_=src),
                            e.dma_start(out=Qz[0:64, qb, :].rearrange("p (h c) -> p h c", h=2), in_=src)]
                sc.op("sp", fn, reads=["mine"], writes=[("Q2", qb)], slot=f"q2{qb}", ndma=3)

            def pbuf(s):
                qt, kb = steps[s]
                i = kb - 4 * qt
                if i >= 1:
                    return Pd[:, i - 1, :], ("Pd", i - 1)
                pb_, st_ = (s // 2) % 2, s % 2
                return PP[:, pb_, st_ * 1024:(st_ + 1) * 1024], ("PP", pb_, st_)

            def abuf(s):
                qt, kb = steps[s]
                i = kb - 4 * qt
                if i >= 1:
                    return Ad[:, i - 1, :], ("Ad", i - 1)
                return A[:, s % 2, :], ("A", s % 2)

            def stA(s):
                qt, kb = steps[s]
                zb, qb = s % 2, qt % 2
                c0 = max(0, (kb - 4 * qt)) * 128

                def fn(e):
                    r = None
                    for h in range(2):
                        q = Qz[:, qb, h * 512 + c0:(h + 1) * 512]
                        r = mm_group(e, ps[:, 2 * zb + h, c0:512], [(Kst[:, h, kb * 128:(kb + 1) * 128], q)])
                    return r
                sc.op("pe", fn, reads=["Kst", ("Q2", qb), "Q2z"], writes=[("Z", zb)])

            def stS1(s):
                qt, kb = steps[s]
                zb = s % 2
                c0 = max(0, (kb - 4 * qt)) * 128
                Zv = ps[:, 2 * zb:2 * zb + 2, c0:512]
                Ev = E[:, zb, :].rearrange("p (h c) -> p h c", h=2)[:, :, c0:512]
                pt, pkey = pbuf(s)
                Pv = pt.rearrange("p (h c) -> p h c", h=2)
                sc.op("act", lambda e: e.activation(out=Ev, in_=Zv, func=AF.Exp),
                      reads=[("Z", zb)], writes=[("E", zb)])
                sc.op("act", lambda e: e.activation(out=Pv[:, :, c0:512], in_=Ev, func=AF.Ln, bias=1.0, scale=1.0),
                      reads=[("E", zb)], writes=[pkey])
                if kb >= 4 * qt:
                    sc.op("dve", lambda e: [e.tensor_tensor(out=Pv[:, h, c0:c0 + 128], in0=Pv[:, h, c0:c0 + 128],
                                                            in1=cmat[:, TRI01, :], op=ALU.mult) for h in range(2)],
                          reads=["consts", pkey], writes=[pkey])

            def stB(s):
                qt, kb = steps[s]
                pb, qb = s % 3, qt % 2
                first = kb == 4 * qt + 3
                diag = kb >= 4 * qt
                i = kb - 4 * qt

                pt, pkey = pbuf(s)

                def fn(e):
                    r = None
                    for h in range(2):
                        q = (Qz if first else Qd)[:, qb, h * 512:(h + 1) * 512]
                        pairs = [(Kst[:, h, kb * 128:(kb + 1) * 128], q),
                                 (cmat[:, NEGTRI, :], pt[:, h * 512:(h + 1) * 512])]
                        r = mm_group(e, ps[:, 4 + h, :], pairs, start=first)
                    return r
                sc.op("pe", fn, reads=["Kst", ("Q2", qb), "Q2z", pkey, "consts"], writes=["B"])

            def stS2(s):
                qt, kb = steps[s]
                c0 = max(0, (kb - 4 * qt)) * 128
                at, akey = abuf(s)
                Av = at.rearrange("p (h c) -> p h c", h=2)
                sc.op("act", lambda e: e.activation(out=Av[:, :, c0:512], in_=ps[:, 4:6, c0:512], func=AF.Exp),
                      reads=["B"], writes=[akey])
                if kb >= 4 * qt:
                    sc.op("dve", lambda e: [e.tensor_tensor(out=Av[:, h, c0:c0 + 128], in0=Av[:, h, c0:c0 + 128],
                                                            in1=cmat[:, TRI01, :], op=ALU.mult) for h in range(2)],
                          reads=["consts", akey], writes=[akey])

            def stC1(s):
                qt, kb = steps[s]
                pb = s % 3
                last = kb == 0
                diag = kb >= 4 * qt
                i = kb - 4 * qt
                if last:
                    return

                pt, pkey = pbuf(s)

                def fn(e):
                    r = None
                    for h in range(2):
                        pairs = [(cmat[:, NEGREST, :], pt[:, h * 512:(h + 1) * 512])]
                        r = mm_group(e, ps[:, 4 + h, :], pairs, start=False)
                    return r
                sc.op("pe", fn, reads=[pkey, "consts"], writes=["B"])

            def stPV(s, hp=hp):
                qt, kb = steps[s]
                ab, ob = s % 2, qt % 2
                first = kb == 4 * qt + 3
                last = kb == 0

                at, akey = abuf(s)

                def fn(e):
                    pairs = [(Vp[h][:, kb, :], at[:, h * 512:(h + 1) * 512]) for h in range(2)]
                    return mm_group(e, ps[:, 6 + ob, :], pairs, start=first, stop=last)
                sc.op("pe", fn, reads=[akey, "Vp"], writes=[("O", ob)])
                if last:
                    sc.op("dve", lambda e: e.tensor_copy(out=Osb[:, ob, :], in_=ps[:, 6 + ob, :]),
                          reads=[("O", ob)], writes=[("Osb", ob)])
                    sc.op("sp", lambda e: e.dma_start(out=ogin[j][(qt // 4) * 256 + hp * 128:(qt // 4) * 256 + (hp + 1) * 128, (qt % 4) * 512:(qt % 4 + 1) * 512], in_=Osb[:, ob, :]),
                          reads=[("Osb", ob)], writes=[("ogin", hp, qt // 4)], slot=f"osb{ob}")
                    if qt % 4 == 3 and hp == 1:
                        ag_o(hp, qt // 4)

            def doA(s):
                if s > 0 and steps[s][0] != steps[s - 1][0]:
                    ldq(steps[s][0])
                stA(s)

            def is_diag(s):
                qt, kb = steps[s]
                return kb >= 4 * qt

            def s1_part(m, part):
                a = 2 * m
                if is_diag(a):
                    stS1(a + part)
                    return
                pb_ = m % 2
                if part == 0:
                    sc.op("act", lambda e: e.activation(out=E[:, :, :], in_=ps[:, 0:4, :].rearrange("p a b -> p (a b)").rearrange("p (a b) -> p a b", a=2), func=AF.Exp),
                          reads=[("Z", 0), ("Z", 1)], writes=[("E", 0), ("E", 1)])
                else:
                    sc.op("act", lambda e: e.activation(out=PP[:, pb_, :], in_=E[:, :, :].rearrange("p a b -> p (a b)"), func=AF.Ln, bias=1.0, scale=1.0),
                          reads=[("E", 0), ("E", 1)], writes=[("PP", pb_, 0), ("PP", pb_, 1)])

            ldq(0)
            doA(0)
            doA(1)
            s1_part(0, 0)
            s1_part(0, 1)
            doA(2)
            doA(3)
            for m in range(ns // 2):
                a, b = 2 * m, 2 * m + 1
                nxt = (2 * m + 2) < ns
                if nxt:
                    s1_part(m + 1, 0)
                if a >= 1:
                    stC1(a - 1)
                stB(a)
                stS2(a)
                if a >= 1:
                    stPV(a - 1)
                if a + 4 < ns:
                    doA(a + 4)
                if nxt:
                    s1_part(m + 1, 1)
                stC1(a)
                stB(b)
                stS2(b)
                stPV(a)
                if b + 4 < ns:
                    doA(b + 4)
            stPV(ns - 1)

        sc.barrier(("pe", "act", "dve", "sp"))
        oT = av(0, [8, T])
        if debug and layer == 0:
            sc.op("sp", lambda e: [e.dma_start(out=dbg_mine, in_=mine[0]), e.dma_start(out=dbg_ogin, in_=ogin[0])],
                  reads=["mine"] + [("ogin", h_, t_) for h_ in range(2) for t_ in range(4)], writes=["dbg"], slot="dbg", ndma=2)

        def ldo(e):
            ov = ogout[j].rearrange("(tc rh i p) t -> tc rh p i t", tc=4, rh=2, i=4)
            g = bass.ds(mysl(e), 1)
            o4 = oT.rearrange("p (i k2) t -> p i k2 t", k2=2)
            return [e.dma_start(out=o4[:, :, rh, :], in_=ov[g, rh, :, :, :].rearrange("o p i t -> (o p) i t")) for rh in range(2)]
        sc.op("pool", ldo, reads=[("ogout", h_, t_) for h_ in range(2) for t_ in range(4)], writes=["oT"], slot="oT", ndma=2)
        cnt = 0
        for dc in range(8):
            b, wkey = load_w8(wo_d[j, dc])
            wv_ = w8[:, b, :].rearrange("p (k c) -> p k c", k=8)
            for gt in range(4):
                db = 4 + cnt % 2
                cnt += 1
                sc.op("pe", lambda e, wv_=wv_, gt=gt, db=db: mm_group(
                    e, ps[:, db, :], [(wv_[:, k, :], oT[:, k, gt * 512:(gt + 1) * 512]) for k in range(8)]),
                    reads=[wkey, "oT"], writes=[("ps", db)])
                sc.op("dve", lambda e, dc=dc, gt=gt, db=db: e.tensor_tensor(
                    out=xT[:, dc, gt * 512:(gt + 1) * 512], in0=ps[:, db, :], in1=xT[:, dc, gt * 512:(gt + 1) * 512], op=ALU.add),
                    reads=[("ps", db), ("x", dc, gt)], writes=[("x", dc, gt)])

    stages = []
    for l in range(DEPTH):
        stages += [("ffn", l, 0), ("mix", l), ("ffn", l, 1), ("ple", l)]
    for st in stages:
        if st[0] == "ffn":
            ffn(st[1], st[2])
        elif st[0] == "mix":
            if st[1] % 2 == 0:
                attention(st[1])
            else:
                pool_mixer(st[1])
        else:
            ple(st[1])
        if stop_after is not None and st == stop_after:
            break

    sc.barrier(("sp",))
    for k in range(8):
        sc.op("sp", lambda e, k=k: e.dma_start(out=yT_d[k * 128:(k + 1) * 128, :], in_=xT[:, k, :]),
              reads=[("x", k, t) for t in range(4)], writes=["yT"], slot="yst")
    final_tok = sc.slot("yst")

    with nc.Block() as block:
        @block.sync
        def _(e):
            sc.replay("sp", e)
            e.wait_ge(final_tok[0], final_tok[1])
            if "dbg" in sc.slots:
                e.wait_ge(sc.slots["dbg"][0], sc.slots["dbg"][1])

        @block.gpsimd
        def _(e):
            sc.replay("pool", e)

        @block.tensor
        def _(e):
            sc.replay("pe", e)

        @block.scalar
        def _(e):
            sc.replay("act", e)

        @block.vector
        def _(e):
            sc.replay("dve", e)
    es.close()
    return nc


def _bf(a):
    return np.ascontiguousarray(a.astype(ml_dtypes.bfloat16))


def host_layout(inp, nl=DEPTH):
    f = lambda a: np.ascontiguousarray(np.asarray(a, dtype=np.float32))
    sh = {}
    gu = np.stack([f(inp["w_ffn1_gu"][:nl]), f(inp["w_ffn2_gu"][:nl])], 1)
    gate = gu[..., :DFF].reshape(nl, 2, 8, 128, NF, 128)
    up = gu[..., DFF:].reshape(nl, 2, 8, 128, NF, 128)
    g2 = np.concatenate([gate.transpose(0, 1, 4, 3, 2, 5), up.transpose(0, 1, 4, 3, 2, 5)], -1)
    sh["wgu"] = np.ascontiguousarray(g2)
    dn = np.stack([f(inp["w_ffn1_down"][:nl]), f(inp["w_ffn2_down"][:nl])], 1)
    dn = dn.reshape(nl, 2, NF, 128, 8, 128).transpose(0, 1, 4, 3, 2, 5)
    sh["wdn"] = np.ascontiguousarray(dn).reshape(nl, 2, 8, 128, NF * 128)
    wqkv = f(inp["w_qkv"])
    qk = wqkv[:, :, :2048].reshape(2, 8, 128, 16, 128).transpose(0, 3, 2, 1, 4)
    sh["wqk"] = np.ascontiguousarray(qk)
    sh["wv"] = np.ascontiguousarray(wqkv[:, :, 2048:].reshape(2, 8, 128, 1024).transpose(0, 2, 1, 3))
    c8 = lambda w: np.ascontiguousarray(w.reshape(w.shape[0], 8, 128, 8, 128).transpose(0, 3, 2, 1, 4))
    sh["wo"] = c8(f(inp["w_o"]))
    sh["wpi"] = c8(f(inp["w_pool_in"]))
    sh["wpg"] = c8(f(inp["w_ple_gate"]))
    wg = f(inp["w_pool_grp"]).reshape(2, 4, 2, 128, 256).transpose(0, 3, 1, 2, 4)
    sh["wgrp"] = np.ascontiguousarray(wg).reshape(2, 128, 2048)
    sh["wpp"] = np.ascontiguousarray(f(inp["w_ple_proj"]).reshape(DEPTH, 2, 128, 1024).transpose(0, 2, 1, 3))
    gn = np.stack([f(inp["norm_ffn1"]), f(inp["norm_mix"]), f(inp["norm_ffn2"]), f(inp["norm_ple"])], 0)
    sh["gn"] = np.ascontiguousarray(gn.reshape(4, DEPTH, 8, 128).transpose(3, 0, 1, 2)).reshape(128, 128)
    qk_g = np.stack([f(inp["q_norm"]), f(inp["k_norm"])], 0)
    qk_g = np.concatenate([qk_g, qk_g], -1)
    sh["qkg"] = np.ascontiguousarray(qk_g.transpose(2, 0, 1)).reshape(128, 4)
    sh["psc"] = np.ascontiguousarray(f(inp["pool_scale"]).reshape(2, 8, 128).transpose(2, 0, 1)).reshape(128, 16)
    cm = np.zeros((128, 7, 128), np.float32)
    cm[:, 0] = np.eye(128)
    cm[:, 1] = 1.0 / 1024
    cm[:64, 2, :64] = 1.0 / 64
    cm[64:, 2, 64:] = 1.0 / 64
    jj, ss = np.meshgrid(np.arange(128), np.arange(128), indexing="ij")
    cm[:, 3] = -1.0 * (jj >= ss)
    cm[:, 4] = -1.0 * (jj < ss)
    cm[:, 5] = -np.eye(128)
    cm[:, 6] = 1.0 * (jj < ss)
    sh["cmat"] = _bf(cm.reshape(128, 896))
    mk = np.zeros((128, 4, 512), np.float32)
    for i in range(4):
        kpos = 128 * i + np.arange(128)[:, None]
        mk[:, i] = np.where(kpos >= np.arange(512)[None, :], MASKV, 0.0)
    sh["mneg"] = _bf(mk.reshape(128, 2048))
    x = f(inp["x"])
    p = f(inp["p"])
    maps = []
    for r in range(8):
        b, c = r // 4, r % 4
        m = dict(sh)
        m["xT"] = np.ascontiguousarray(x[b, c * T:(c + 1) * T, :].T)
        m["wqk"] = np.ascontiguousarray(sh["wqk"][:, [2 * c, 2 * c + 1, 8 + 2 * c, 8 + 2 * c + 1]])
        m["wv"] = np.ascontiguousarray(sh["wv"][:, :, :, 256 * c:256 * (c + 1)])
        m["pT"] = np.ascontiguousarray(p[:, b, c * T:(c + 1) * T, :].transpose(0, 2, 1))
        s = np.zeros((128, 4), np.float32)
        if c > 0:
            s[:, c - 1] = 1.0
        m["sel"] = s
        ic = np.zeros((128, 4, 16), np.float32)
        for g in range(4):
            w = 2 << g
            if c == 0:
                ic[:, g] = 1.0 / np.minimum(np.arange(16) + 1, w)
            else:
                ic[:, g] = 1.0 / w
        m["icnt"] = ic.reshape(128, 64)
        maps.append(m)
    return maps


_NC_CACHE = {}


def kernel(_stop_after=None, _debug=False, **inputs):
    if _stop_after is not None:
        nl = _stop_after[1] + 1
        maps = host_layout(inputs, nl)
        nc = build(_stop_after, nl=nl, debug=_debug)
        res = run_bass_kernel_spmd(nc, maps, core_ids=list(range(8)))
        _NC_CACHE["res"] = res
        out = np.zeros((2, S, D), np.float32)
        for r in range(8):
            b, c = r // 4, r % 4
            out[b, c * T:(c + 1) * T, :] = np.asarray(res.results[r]["yT"]).T
        return out
    maps = host_layout(inputs)
    if False:
        _NC_CACHE["nc"] = build(_stop_after)
    if "nc" not in _NC_CACHE:
        _NC_CACHE["nc"] = build()
    res = run_bass_kernel_spmd(_NC_CACHE["nc"], maps, core_ids=list(range(8)))
    out = np.zeros((2, S, D), np.float32)
    for r in range(8):
        b, c = r // 4, r % 4
        out[b, c * T:(c + 1) * T, :] = np.asarray(res.results[r]["yT"]).T
    return out
```

```python
import contextlib
import numpy as np
import ml_dtypes
import concourse.bass as bass
import concourse.mybir as mybir
from concourse.bass_utils import run_bass_kernel_spmd

F32 = mybir.dt.float32
BF16 = mybir.dt.bfloat16
AF = mybir.ActivationFunctionType
ALU = mybir.AluOpType

D = 1024
T = 2048
S = 8192
DFF = 2816
NF = 22
DEPTH = 4
EPS = 1e-6
MASKV = -128.0
ARENA = 54272
COMPUTE = ("pe", "act", "dve")


class Sched:
    def __init__(self, nc, es):
        self.nc = nc
        self.es = es
        self.prog = {e: [] for e in ("pe", "act", "dve", "pool", "sp")}
        self.esem = {e: es.enter_context(nc.semaphore("s_" + e)) for e in COMPUTE}
        self.ecnt = {e: 0 for e in COMPUTE}
        self.slots = {}
        self.waited = {e: {} for e in self.prog}
        self.lastw = {}
        self.readers = {}
        self.barrier_toks = []
        self.pending = {e: [] for e in self.prog}
        self.last_barrier = []

    def slot(self, name):
        if name not in self.slots:
            self.slots[name] = [self.es.enter_context(self.nc.semaphore("d_" + name)), 0]
        return self.slots[name]

    def barrier(self, engines=("pe", "act", "dve", "sp", "pool")):
        toks = [(self.esem[e], self.ecnt[e], e) for e in COMPUTE if self.ecnt[e] > 0]
        toks += [(s[0], s[1], None) for s in self.slots.values() if s[1] > 0]
        self.last_barrier = list(toks)
        for e in engines:
            self.pending[e] = list(toks)

    def op(self, eng, fn, reads=(), writes=(), slot=None, ndma=1, after_barrier=False):
        toks = list(self.pending[eng])
        self.pending[eng] = []
        if after_barrier:
            toks += self.last_barrier
        for k in reads:
            if k in self.lastw:
                toks.append(self.lastw[k])
        for k in writes:
            if k in self.lastw:
                toks.append(self.lastw[k])
            toks.extend(self.readers.get(k, ()))
        waits = {}
        for (sem, val, src) in toks:
            if eng == "pe" and src == "pe":
                continue
            key = id(sem)
            if key not in waits or waits[key][1] < val:
                waits[key] = (sem, val)
        wl = []
        for key, (sem, val) in waits.items():
            if self.waited[eng].get(key, 0) >= val:
                continue
            self.waited[eng][key] = val
            wl.append((sem, val))
        if eng in COMPUTE:
            self.ecnt[eng] += 1
            tok = (self.esem[eng], self.ecnt[eng], eng)
            inc = (self.esem[eng], 1)
        else:
            sl = self.slot(slot)
            step = 1 if slot.startswith("cc_") else 16
            sl[1] += step * ndma
            tok = (sl[0], sl[1], None)
            inc = (sl[0], step)
        self.prog[eng].append((wl, fn, inc))
        for k in writes:
            self.lastw[k] = tok
            self.readers[k] = []
        for k in reads:
            self.readers.setdefault(k, []).append(tok)
        return tok

    def replay(self, eng_name, eng):
        compute = eng_name in COMPUTE
        for (wl, fn, inc) in self.prog[eng_name]:
            for (sem, val) in wl:
                eng.wait_ge(sem, val)
            r = fn(eng)
            if r is None:
                continue
            if not isinstance(r, (list, tuple)):
                r = [r]
            if compute:
                r[-1].then_inc(inc[0], inc[1])
            else:
                for ins in r:
                    ins.then_inc(inc[0], inc[1])


def build(stop_after=None, nl=DEPTH, debug=False):
    nc = bass.Bass("TRN2", target_bir_lowering=False)
    es = contextlib.ExitStack()

    def din(name, shape, dt=F32):
        return nc.dram_tensor(name, list(shape), dt, kind="ExternalInput").ap()

    xT_d = din("xT", [D, T])
    pT_d = din("pT", [DEPTH, 256, T])
    wgu_d = din("wgu", [nl, 2, NF, 128, 8, 256])
    wdn_d = din("wdn", [nl, 2, 8, 128, NF * 128])
    wqk_d = din("wqk", [2, 4, 128, 8, 128])
    wv_d = din("wv", [2, 128, 8, 256])
    wo_d = din("wo", [2, 8, 128, 8, 128])
    wpi_d = din("wpi", [2, 8, 128, 8, 128])
    wpg_d = din("wpg", [DEPTH, 8, 128, 8, 128])
    wgrp_d = din("wgrp", [2, 128, 4 * 2 * 256])
    wpp_d = din("wpp", [DEPTH, 128, 2, 1024])
    gn_d = din("gn", [128, 4 * DEPTH * 8])
    qkg_d = din("qkg", [128, 4])
    psc_d = din("psc", [128, 16])
    sel_d = din("sel", [128, 4])
    icnt_d = din("icnt", [128, 64])
    cmat_d = din("cmat", [128, 7 * 128], BF16)
    mneg_d = din("mneg", [128, 4 * 512], BF16)
    yT_d = nc.dram_tensor("yT", [D, T], F32, kind="ExternalOutput").ap()
    if debug:
        dbg_mine = nc.dram_tensor("dbg_mine", [4 * 768, 2048], BF16, kind="ExternalOutput").ap()
        dbg_ogin = nc.dram_tensor("dbg_ogin", [1024, 2048], BF16, kind="ExternalOutput").ap()

    hgin_a = [nc.dram_tensor(f"hgina{j}", [2048, 1024], BF16, kind="Internal").ap() for j in range(2)]
    hgout_a = [nc.dram_tensor(f"hgouta{j}", [4 * 2048, 1024], BF16, kind="Internal").ap() for j in range(2)]
    ogin = [nc.dram_tensor(f"ogin{j}", [1024, 2048], BF16, kind="Internal").ap() for j in range(2)]
    ogout = [nc.dram_tensor(f"ogout{j}", [4096, 2048], BF16, kind="Internal").ap() for j in range(2)]
    mine = [nc.dram_tensor(f"mine{j}", [4 * 768, 2048], BF16, kind="Internal").ap() for j in range(2)]
    hgin = [nc.dram_tensor(f"hgin{j}", [128, 128], F32, kind="Internal").ap() for j in range(2)]
    hgout = [nc.dram_tensor(f"hgout{j}", [4 * 128, 128], F32, kind="Internal").ap() for j in range(2)]

    def sb(name, shape, dt):
        return es.enter_context(nc.sbuf_tensor(name, list(shape), dt))

    xT = sb("xT_sb", [128, 8, T], F32)
    arena = sb("arena", [128, ARENA], BF16)
    wgu = sb("wgu_sb", [128, 2, 8 * 256], BF16)
    wdn = sb("wdn_sb", [128, 2, NF * 128], BF16)
    w8 = sb("w8_sb", [128, 3, 8 * 128], BF16)
    cmat = sb("cmat_sb", [128, 7, 128], BF16)
    mneg = sb("mneg_sb", [128, 4, 512], BF16)
    gn = sb("gn_sb", [128, 4, DEPTH, 8], F32)
    qkg = sb("qkg_sb", [128, 2, 2], F32)
    qg8 = sb("qg8_sb", [128, 2], F32)
    psc = sb("psc_sb", [128, 2, 8], F32)
    sel = sb("sel_sb", [128, 4], F32)
    icnt = sb("icnt_sb", [128, 4, 16], F32)
    wgrp = sb("wgrp_sb", [128, 4, 2, 256], BF16)
    ps = es.enter_context(nc.psum_tensor("ps", [128, 8, 512], F32))

    IDENT, ONES_MS, ONES_HD, NEGTRI, NEGREST, NEGIDENT, TRI01 = range(7)

    def av(off, shape, dt=BF16):
        n = int(np.prod(shape))
        if dt == F32:
            v = arena[:, off:off + 2 * n].bitcast(F32)
        else:
            v = arena[:, off:off + n]
        if len(shape) == 1:
            return v
        if len(shape) == 2:
            return v.rearrange("p (a b) -> p a b", a=shape[0])
        return v.rearrange("p (a b c) -> p a b c", a=shape[0], b=shape[1])

    sc = Sched(nc, es)
    me4 = {}

    def ld_consts(e):
        return [
            e.dma_start(out=cmat[:].rearrange("p a b -> p (a b)"), in_=cmat_d),
            e.dma_start(out=mneg[:].rearrange("p a b -> p (a b)"), in_=mneg_d),
            e.dma_start(out=gn[:].rearrange("p a b c -> p (a b c)"), in_=gn_d),
            e.dma_start(out=qkg[:].rearrange("p a b -> p (a b)"), in_=qkg_d),
            e.dma_start(out=psc[:].rearrange("p a b -> p (a b)"), in_=psc_d),
            e.dma_start(out=sel[:], in_=sel_d),
            e.dma_start(out=icnt[:].rearrange("p a b -> p (a b)"), in_=icnt_d),
        ]
    sc.op("sp", ld_consts, writes=["consts"], slot="consts", ndma=7)
    for k in range(8):
        sc.op("sp", lambda e, k=k: e.dma_start(out=xT[:, k, :], in_=xT_d[k * 128:(k + 1) * 128, :]),
              writes=[("x", k, t) for t in range(4)], slot=f"xld{k}")
    sc.op("dve", lambda e: e.tensor_scalar(out=qg8[:], in0=qkg[:, 0, :], scalar1=0.125, scalar2=None, op0=ALU.mult),
          reads=["consts"], writes=["qg8"])

    ring_cnt = {"wgu": 0, "wdn": 0, "w8": 0}

    def load_w(kind, src_ap, nbuf, view):
        i = ring_cnt[kind]
        ring_cnt[kind] += 1
        b = i % nbuf
        key = (kind, b)
        sc.op("pool", lambda e: e.dma_start(out=view(b), in_=src_ap), writes=[key], slot=f"{kind}{b}")
        return b, key

    def load_wgu(l, which, f):
        return load_w("wgu", wgu_d[l, which, f].rearrange("p k c -> p (k c)"), 2, lambda b: wgu[:, b, :])

    def load_wdn(l, which, dc):
        i = ring_cnt["wdn"]
        ring_cnt["wdn"] += 1
        b = i % 2
        key = ("wdn", b)
        h = NF * 64

        def fn(e):
            return [e.dma_start(out=wdn[:, b, 0:h], in_=wdn_d[l, which, dc, :, 0:h]),
                    e.dma_start(out=wdn[:, b, h:2 * h], in_=wdn_d[l, which, dc, :, h:2 * h])]
        sc.op("pool", fn, writes=[key], slot=f"wdn{b}", ndma=2)
        return b, key

    def load_w8(src3):
        return load_w("w8", src3.rearrange("p k c -> p (k c)"), 3, lambda b: w8[:, b, :])

    O_HT = 0
    O_SQ, O_LN, O_RS, O_SG = 44032, 48128, 49152, 50176

    def mm_group(e, out, pairs, start=True, stop=True):
        r = None
        n = len(pairs)
        for i, (l, rh) in enumerate(pairs):
            r = e.matmul(out, lhsT=l, rhs=rh, start=(start and i == 0), stop=(stop and i == n - 1),
                         skip_group_check=True)
        return r

    def norm_half(half, kind, layer, ssbank=6, hoff=0, hkey="hT"):
        hT = av(hoff, [8, 1024])
        sq = av(O_SQ, [8, 512])
        lnv = av(O_LN, [512], F32)
        rstd = av(O_RS, [512], F32)
        for tt in range(2):
            gt = half * 2 + tt
            c0 = gt * 512
            sc.op("act", lambda e, c0=c0: e.activation(out=sq, in_=xT[:, :, c0:c0 + 512], func=AF.Square),
                  reads=[("x", k, gt) for k in range(8)], writes=["sq"])
            sc.op("pe", lambda e: mm_group(e, ps[:, ssbank, :], [(cmat[:, ONES_MS, :], sq[:, k, :]) for k in range(8)]),
                  reads=["sq", "consts"], writes=[("ps", ssbank)])
            sc.op("act", lambda e: e.activation(out=lnv, in_=ps[:, ssbank, :], func=AF.Ln, bias=EPS, scale=1.0),
                  reads=[("ps", ssbank)], writes=["lnv"])
            sc.op("act", lambda e: e.activation(out=rstd, in_=lnv, func=AF.Exp, scale=-0.5),
                  reads=["lnv"], writes=["rstd"])
            for k in range(8):
                sc.op("dve", lambda e, k=k, c0=c0, tt=tt: e.scalar_tensor_tensor(
                    out=hT[:, k, tt * 512:(tt + 1) * 512], in0=xT[:, k, c0:c0 + 512],
                    scalar=gn[:, kind, layer, k:k + 1], in1=rstd, op0=ALU.mult, op1=ALU.mult),
                    reads=[("x", k, gt), "rstd", "consts"], writes=[(hkey, k, tt)])
        return hT

    def mix_prenorm(layer, half, defer=False):
        j = layer // 2
        hT = norm_half(half, 1, layer)
        hv = hgin_a[j].rearrange("(h k p) t -> h p k t", h=2, p=128)
        sc.op("sp", lambda e: e.dma_start(out=hv[half], in_=hT),
              reads=[("hT", k, tt) for k in range(8) for tt in range(2)], writes=[("hgin_a", half)], slot="hgst")
        def emit_ag():
            sc.op("pool", lambda e: [e.collective_compute("AllGather", ALU.bypass, replica_groups=[[0, 1, 2, 3], [4, 5, 6, 7]],
                                                          ins=[hgin_a[j][(half * 4 + q) * 256:(half * 4 + q + 1) * 256, :]],
                                                          outs=[hgout_a[j][(half * 4 + q) * 1024:(half * 4 + q + 1) * 1024, :]])
                                     for q in range(4)],
                  reads=[("hgin_a", half)], writes=[("hgout_a", half)], slot="cc_a", ndma=4)
        if defer:
            return emit_ag
        emit_ag()

    def ffn(layer, which):
        sc.barrier(("pe", "act", "dve", "sp"))
        kind = 0 if which == 0 else 2
        aT = av(8192, [NF, 1024])
        sgt = av(O_SG, [2, 512], F32)
        hTs = {0: norm_half(0, kind, layer)}
        deferred = []
        want_pre = False
        for half in range(2):
            hT = hTs[half]
            hkey = "hT" if half == 0 else "hT2"
            cnt = 0
            for f in range(NF):
                if half == 0 and f == NF // 2:
                    hTs[1] = norm_half(1, kind, layer, hoff=30720, hkey="hT2")
                b, wkey = load_wgu(layer, which, f)
                if half == 1 and f == 2 and want_pre:
                    deferred.append(mix_prenorm(layer, 0, defer=True))
                if half == 1 and f == 10 and deferred:
                    deferred.pop()()
                for tt in range(2):
                    gb = cnt % 2
                    cnt += 1
                    wv_ = wgu[:, b, :].rearrange("p (k c) -> p k c", k=8)
                    sc.op("pe", lambda e, wv_=wv_, tt=tt, gb=gb, hT=hT: [
                        mm_group(e, ps[:, gb, :], [(wv_[:, k, 0:128], hT[:, k, tt * 512:(tt + 1) * 512]) for k in range(8)]),
                        mm_group(e, ps[:, 2 + gb, :], [(wv_[:, k, 128:256], hT[:, k, tt * 512:(tt + 1) * 512]) for k in range(8)])],
                        reads=[wkey] + [(hkey, k, tt) for k in range(8)], writes=[("ps", gb), ("ps", 2 + gb)])
                    sc.op("act", lambda e, gb=gb: e.activation(out=sgt[:, gb, :], in_=ps[:, gb, :], func=AF.Silu),
                          reads=[("ps", gb)], writes=[("sgt", gb)])
                    sc.op("dve", lambda e, gb=gb, f=f, tt=tt: e.tensor_tensor(
                        out=aT[:, f, tt * 512:(tt + 1) * 512], in0=sgt[:, gb, :], in1=ps[:, 2 + gb, :], op=ALU.mult),
                        reads=[("sgt", gb), ("ps", 2 + gb)], writes=[("aT", f, tt)])
            cnt = 0
            for dc in range(8):
                b, wkey = load_wdn(layer, which, dc)
                wv_ = wdn[:, b, :].rearrange("p (f c) -> p f c", f=NF)
                for tt in range(2):
                    db = 4 + cnt % 2
                    cnt += 1
                    gt = half * 2 + tt
                    sc.op("pe", lambda e, wv_=wv_, tt=tt, db=db: mm_group(
                        e, ps[:, db, :], [(wv_[:, f, :], aT[:, f, tt * 512:(tt + 1) * 512]) for f in range(NF)]),
                        reads=[wkey] + [("aT", f, tt) for f in range(NF)], writes=[("ps", db)])
                    sc.op("dve", lambda e, dc=dc, gt=gt, db=db: e.scalar_tensor_tensor(
                        out=xT[:, dc, gt * 512:(gt + 1) * 512], in0=ps[:, db, :], scalar=0.5,
                        in1=xT[:, dc, gt * 512:(gt + 1) * 512], op0=ALU.mult, op1=ALU.add),
                        reads=[("ps", db), ("x", dc, gt)], writes=[("x", dc, gt)])
            if half == 0 and which == 0 and layer % 2 == 0:
                want_pre = True

    def ple(layer):
        sc.barrier(("pe", "act", "dve", "sp"))
        pTb = av(38912, [2, T])
        wpp = av(52224, [2, 1024])
        sgm = av(14336, [2, 512], F32)
        tmp = av(16384, [2, 512], F32)
        sc.op("pool", lambda e: [e.dma_start(out=pTb, in_=pT_d[layer].rearrange("(k p) t -> p k t", p=128)),
                                 e.dma_start(out=wpp, in_=wpp_d[layer])],
              writes=["pTb", "wpp"], slot="plew", ndma=2)
        cnt = 0
        hTs = {0: norm_half(0, 3, layer)}
        for half in range(2):
            hT = hTs[half]
            hkey = "hT" if half == 0 else "hT2"
            for dc in range(8):
                if half == 0 and dc == 4:
                    hTs[1] = norm_half(1, 3, layer, hoff=30720, hkey="hT2")
                b, wkey = load_w8(wpg_d[layer, dc])
                wv_ = w8[:, b, :].rearrange("p (k c) -> p k c", k=8)
                for tt in range(2):
                    gb = cnt % 2
                    cnt += 1
                    gt = half * 2 + tt
                    sc.op("pe", lambda e, wv_=wv_, tt=tt, gb=gb, dc=dc, gt=gt, hT=hT: [
                        mm_group(e, ps[:, gb, :], [(wv_[:, k, :], hT[:, k, tt * 512:(tt + 1) * 512]) for k in range(8)]),
                        mm_group(e, ps[:, 2 + gb, :], [(wpp[:, k, dc * 128:(dc + 1) * 128], pTb[:, k, gt * 512:(gt + 1) * 512]) for k in range(2)])],
                        reads=[wkey, "pTb", "wpp"] + [(hkey, k, tt) for k in range(8)], writes=[("ps", gb), ("ps", 2 + gb)])
                    sc.op("act", lambda e, gb=gb: e.activation(out=sgm[:, gb, :], in_=ps[:, gb, :], func=AF.Sigmoid),
                          reads=[("ps", gb)], writes=[("sgm", gb)])
                    sc.op("dve", lambda e, gb=gb: e.tensor_tensor(out=tmp[:, gb, :], in0=sgm[:, gb, :], in1=ps[:, 2 + gb, :], op=ALU.mult),
                          reads=[("sgm", gb), ("ps", 2 + gb)], writes=[("tmp", gb)])
                    sc.op("dve", lambda e, gb=gb, dc=dc, gt=gt: e.tensor_tensor(
                        out=xT[:, dc, gt * 512:(gt + 1) * 512], in0=tmp[:, gb, :], in1=xT[:, dc, gt * 512:(gt + 1) * 512], op=ALU.add),
                        reads=[("tmp", gb), ("x", dc, gt)], writes=[("x", dc, gt)])

    def pool_mixer(layer):
        j = layer // 2
        sc.barrier(("pe", "act", "dve", "sp"))
        U = [av(0, [8, 1040], F32), av(16640, [8, 1040], F32)]
        O_H = 33280
        hT = av(O_H, [8, 1024])
        tmpS = av(41472, [2, 1040], F32)
        hal = av(45632, [4, 128], F32)
        sc.op("pool", lambda e: e.dma_start(out=wgrp[:].rearrange("p a b c -> p (a b c)"), in_=wgrp_d[j]),
              writes=["wgrp"], slot="wgrp")

        def do_norm(half):
            sq = av(46656, [8, 512])
            lnv = tmpS[:, 0, 0:512]
            rstd = tmpS[:, 1, 0:512]
            for tt in range(2):
                gt = half * 2 + tt
                c0 = gt * 512
                sc.op("act", lambda e, c0=c0: e.activation(out=sq, in_=xT[:, :, c0:c0 + 512], func=AF.Square),
                      reads=[("x", k, gt) for k in range(8)], writes=["sq"])
                sc.op("pe", lambda e: mm_group(e, ps[:, 6, :], [(cmat[:, ONES_MS, :], sq[:, k, :]) for k in range(8)]),
                      reads=["sq", "consts"], writes=[("ps", 6)])
                sc.op("act", lambda e: e.activation(out=lnv, in_=ps[:, 6, :], func=AF.Ln, bias=EPS, scale=1.0),
                      reads=[("ps", 6)], writes=[("tmpS", 0)])
                sc.op("act", lambda e: e.activation(out=rstd, in_=lnv, func=AF.Exp, scale=-0.5),
                      reads=[("tmpS", 0)], writes=[("tmpS", 1)])
                for k in range(8):
                    sc.op("dve", lambda e, k=k, c0=c0, tt=tt: e.scalar_tensor_tensor(
                        out=hT[:, k, tt * 512:(tt + 1) * 512], in0=xT[:, k, c0:c0 + 512],
                        scalar=gn[:, 1, layer, k:k + 1], in1=rstd, op0=ALU.mult, op1=ALU.mult),
                        reads=[("x", k, gt), ("tmpS", 1), "consts"], writes=[("hT", k, tt), ("pl", k)])

        def compute_u(half):
            cnt = 0
            for uc in range(8):
                b, wkey = load_w8(wpi_d[j, uc])
                wv_ = w8[:, b, :].rearrange("p (k c) -> p k c", k=8)
                for tt in range(2):
                    gb = cnt % 2
                    cnt += 1
                    sc.op("pe", lambda e, wv_=wv_, tt=tt, gb=gb: mm_group(
                        e, ps[:, gb, :], [(wv_[:, k, :], hT[:, k, tt * 512:(tt + 1) * 512]) for k in range(8)]),
                        reads=[wkey] + [("hT", k, tt) for k in range(8)], writes=[("ps", gb)])
                    sc.op("act", lambda e, gb=gb, uc=uc, tt=tt: e.activation(
                        out=U[half][:, uc, 16 + tt * 512:16 + (tt + 1) * 512], in_=ps[:, gb, :], func=AF.Copy),
                        reads=[("ps", gb)], writes=[("U", half, uc)])

        def pool_and_mix(half):
            pooled = hT
            for uc in range(8):
                g = uc // 2
                w = 2 << g
                cur = U[half][:, uc, :]
                lo = 0
                nsteps = g + 1
                srcbuf = cur
                for st in range(nsteps):
                    sh = 1 << st
                    lo2 = lo + sh
                    dst = tmpS[:, st % 2, :]
                    sc.op("dve", lambda e, dst=dst, srcbuf=srcbuf, lo2=lo2, sh=sh: e.tensor_tensor(
                        out=dst[:, lo2:1040], in0=srcbuf[:, lo2:1040], in1=srcbuf[:, lo2 - sh:1040 - sh], op=ALU.add),
                        reads=[("U", half, uc), ("tmpS", 0), ("tmpS", 1)], writes=[("tmpS", st % 2)])
                    srcbuf = dst
                    lo = lo2
                sfin = srcbuf
                sc.op("dve", lambda e, sfin=sfin, cur=cur, uc=uc, w=w: e.scalar_tensor_tensor(
                    out=pooled[:, uc, :], in0=sfin[:, 16:1040], scalar=1.0 / w, in1=cur[:, 16:1040],
                    op0=ALU.mult, op1=ALU.subtract),
                    reads=[("tmpS", 0), ("tmpS", 1), ("U", half, uc)],
                    writes=[("pl", uc), ("hT", uc, 0), ("hT", uc, 1)])
                if half == 0:
                    t16 = tmpS[:, (nsteps) % 2, 0:16]
                    sc.op("dve", lambda e, t16=t16, sfin=sfin, g=g: e.tensor_tensor(
                        out=t16, in0=sfin[:, 16:32], in1=icnt[:, g, :], op=ALU.mult),
                        reads=[("tmpS", 0), ("tmpS", 1), "consts"], writes=[("tmpS", nsteps % 2)])
                    sc.op("dve", lambda e, t16=t16, cur=cur, uc=uc: e.tensor_tensor(
                        out=pooled[:, uc, 0:16], in0=t16, in1=cur[:, 16:32], op=ALU.subtract),
                        reads=[("tmpS", 0), ("tmpS", 1), ("U", half, uc), ("pl", uc)], writes=[("pl", uc)])
            cnt = 0
            for g in range(4):
                for dd in range(2):
                    dc = 2 * g + dd
                    for tt in range(2):
                        db = 4 + cnt % 2
                        cnt += 1
                        gt = half * 2 + tt
                        sc.op("pe", lambda e, g=g, dd=dd, tt=tt, db=db: mm_group(
                            e, ps[:, db, :], [(wgrp[:, g, cc, dd * 128:(dd + 1) * 128], pooled[:, 2 * g + cc, tt * 512:(tt + 1) * 512]) for cc in range(2)]),
                            reads=["wgrp", ("pl", 2 * g), ("pl", 2 * g + 1)], writes=[("ps", db)])
                        sc.op("dve", lambda e, dc=dc, gt=gt, db=db: e.scalar_tensor_tensor(
                            out=xT[:, dc, gt * 512:(gt + 1) * 512], in0=ps[:, db, :], scalar=psc[:, j, dc:dc + 1],
                            in1=xT[:, dc, gt * 512:(gt + 1) * 512], op0=ALU.mult, op1=ALU.add),
                            reads=[("ps", db), ("x", dc, gt), "consts"], writes=[("x", dc, gt)])

        do_norm(1)
        compute_u(1)
        sc.op("sp", lambda e: e.dma_start(out=hgin[j].rearrange("p (k c) -> p k c", k=8), in_=U[1][:, :, 1024:1040]),
              reads=[("U", 1, uc) for uc in range(8)], writes=["hgin"], slot="hgin")
        sc.op("pool", lambda e: e.collective_compute("AllGather", ALU.bypass, replica_groups=[[0, 1, 2, 3], [4, 5, 6, 7]],
                                                     ins=[hgin[j]], outs=[hgout[j]]),
              reads=["hgin"], writes=["hgout"], slot="cc_h")
        do_norm(0)
        compute_u(0)
        for uc in range(8):
            sc.op("dve", lambda e, uc=uc: e.tensor_copy(out=U[1][:, uc, 0:16], in_=U[0][:, uc, 1024:1040]),
                  reads=[("U", 0, uc)], writes=[("U", 1, uc)])
        pool_and_mix(1)
        sc.op("sp", lambda e: e.dma_start(out=hal, in_=hgout[j].rearrange("(i p) c -> p i c", p=128)),
              reads=["hgout"], writes=["hal"], slot="hal")
        for uc in range(8):
            halv = hal.rearrange("p i (k c) -> p i k c", k=8)
            sc.op("dve", lambda e, uc=uc, halv=halv: e.tensor_scalar(
                out=U[0][:, uc, 0:16], in0=halv[:, 0, uc, :], scalar1=sel[:, 0:1], scalar2=None, op0=ALU.mult),
                reads=["hal", "consts"], writes=[("U", 0, uc)])
            for i in range(1, 4):
                sc.op("dve", lambda e, uc=uc, i=i, halv=halv: e.scalar_tensor_tensor(
                    out=U[0][:, uc, 0:16], in0=halv[:, i, uc, :], scalar=sel[:, i:i + 1], in1=U[0][:, uc, 0:16],
                    op0=ALU.mult, op1=ALU.add),
                    reads=["hal", "consts", ("U", 0, uc)], writes=[("U", 0, uc)])
        pool_and_mix(0)

    def attention(layer):
        j = layer // 2
        sc.barrier(("pe", "act", "dve", "sp"))
        hTi = [av(8192, [8, 1024]), av(16384, [8, 1024])]
        wqb = av(24576, [4, 8 * 128])
        wvb = av(28672, [8, 256])
        qst = av(30720, [2, 512])
        vst = av(31744, [4, 256])
        sqh = av(32768, [2, 512])
        lnv = av(O_LN, [512], F32)
        rstd = av(O_RS, [512], F32)
        sc.op("pool", lambda e: [e.dma_start(out=wqb[:, qc, :], in_=wqk_d[j, qc].rearrange("p k c -> p (k c)")) for qc in range(4)]
              + [e.dma_start(out=wvb, in_=wv_d[j])],
              writes=["wqb", "wvb"], slot="wqv", ndma=5, after_barrier=True)
        mix_prenorm(layer, 1)
        hgv = hgout_a[j].rearrange("(c i k2 p) t -> c i p k2 t", c=8, i=4, k2=2)
        minev = mine[j].rearrange("(i r) c -> i r c", i=4)
        cq = 0
        cv = 0
        it = 0

        def hload(it_):
            half_, i_ = it_ // 4, it_ % 4
            hb_ = it_ % 2
            sc.op("pool", lambda e: [e.dma_start(out=hTi[hb_][:, 2 * q:2 * q + 2, :], in_=hgv[half_ * 4 + q, i_]) for q in range(4)],
                  reads=[("hgout_a", half_)], writes=[("hTi", hb_)], slot=f"hld{hb_}", ndma=4)
        hload(0)
        for half in range(2):
            for i in range(4):
                hb = it % 2
                it += 1
                hX = hTi[hb]
                if it < 8:
                    hload(it)
                PB = (0, 1, 6, 7)
                groups = [(qc, tt) for qc in range(4) for tt in range(2)]

                def g_mm(qc, tt, pbk, hX=hX, hb=hb):
                    wv_ = wqb[:, qc, :].rearrange("p (k c) -> p k c", k=8)
                    sc.op("pe", lambda e: mm_group(
                        e, ps[:, pbk, :], [(wv_[:, k, :], hX[:, k, tt * 512:(tt + 1) * 512]) for k in range(8)]),
                        reads=["wqb", ("hTi", hb)], writes=[("ps", pbk)])

                def g_rest(qc, tt, pbk, gb, i=i, half=half):
                    isk = qc // 2
                    c2 = qc % 2
                    sc.op("act", lambda e: e.activation(out=sqh[:, gb, :], in_=ps[:, pbk, :], func=AF.Square),
                          reads=[("ps", pbk)], writes=[("sqh", gb)])
                    sc.op("pe", lambda e: mm_group(e, ps[:, 2 + gb, :], [(cmat[:, ONES_HD, :], sqh[:, gb, :])]),
                          reads=[("sqh", gb), "consts"], writes=[("ps", 2 + gb)])
                    sc.op("act", lambda e: e.activation(out=lnv, in_=ps[:, 2 + gb, :], func=AF.Ln, bias=EPS, scale=1.0),
                          reads=[("ps", 2 + gb)], writes=["lnv"])
                    sc.op("act", lambda e: e.activation(out=rstd, in_=lnv, func=AF.Exp, scale=-0.5),
                          reads=["lnv"], writes=["rstd"])
                    gsc = qkg[:, 1, j:j + 1] if isk else qg8[:, j:j + 1]
                    sc.op("dve", lambda e: e.scalar_tensor_tensor(
                        out=qst[:, gb, :], in0=ps[:, pbk, :], scalar=gsc, in1=rstd, op0=ALU.mult, op1=ALU.mult),
                        reads=[("ps", pbk), "rstd", "consts", "qg8"], writes=[("qst", gb)])
                    r0 = isk * 256 + c2 * 128
                    col = half * 1024 + tt * 512
                    sc.op("sp", lambda e: e.dma_start(out=minev[i, r0:r0 + 128, col:col + 512], in_=qst[:, gb, :]),
                          reads=[("qst", gb)], writes=["mine"], slot=f"qst{gb}")

                idx = [cq + n for n in range(len(groups))]
                cq += len(groups)
                g_mm(*groups[0], PB[idx[0] % 4])
                for gi in range(len(groups)):
                    if gi + 1 < len(groups):
                        g_mm(*groups[gi + 1], PB[idx[gi + 1] % 4])
                    g_rest(*groups[gi], PB[idx[gi] % 4], idx[gi] % 2)
                vreg = minev[i, 512:768, :].rearrange("r (t8 c) -> (r t8) c", c=256)
                for tb in range(8):
                    gb = cv % 2
                    vb = cv % 4
                    cv += 1
                    sc.op("pe", lambda e, tb=tb, gb=gb, hX=hX: mm_group(
                        e, ps[:, 4 + gb, 0:256], [(hX[:, k, tb * 128:(tb + 1) * 128], wvb[:, k, :]) for k in range(8)]),
                        reads=["wvb", ("hTi", hb)], writes=[("ps", 4 + gb)])
                    sc.op("act", lambda e, gb=gb, vb=vb: e.activation(out=vst[:, vb, :], in_=ps[:, 4 + gb, 0:256], func=AF.Copy),
                          reads=[("ps", 4 + gb)], writes=[("vst", vb)])
                    tok0 = half * 1024 + tb * 128
                    sc.op("sp", lambda e, vb=vb, tok0=tok0, vreg=vreg: e.dma_start(out=vreg[tok0:tok0 + 128, :], in_=vst[:, vb, :]),
                          reads=[("vst", vb)], writes=["mine"], slot=f"vst{vb}")
        sc.barrier(("pe", "act", "dve", "sp"))

        Kst = av(0, [2, S])
        Vp = [av(16384, [64, 128]), av(24576, [64, 128])]
        E = av(32768, [2, 1024], F32)
        PP = av(36864, [2, 2 * 1024])
        A = av(40960, [2, 1024])
        Qd = av(43008, [2, 1024])
        Qz = av(45056, [2, 1024])
        Osb = av(47104, [2, 512])
        Pd = av(48128, [3, 1024])
        Ad = av(51200, [3, 1024])
        minev = mine[j].rearrange("(i r) c -> i r c", i=4)
        minevv = mine[j].rearrange("(i r) (t8 c) -> i (r t8) c", i=4, c=256)[:, 4096:6144, :].rearrange(
            "i (b s) c -> i s b c", s=128)

        def mysl(e):
            return e.partition_id() % 4

        def ag_o(hp_, tc):
            sc.op("pool", lambda e: e.collective_compute("AllGather", ALU.bypass, replica_groups=[[0, 1, 2, 3], [4, 5, 6, 7]],
                                                         ins=[ogin[j][(tc * 2 + hp_) * 128:(tc * 2 + hp_ + 1) * 128, :]],
                                                         outs=[ogout[j][(tc * 2 + hp_) * 512:(tc * 2 + hp_ + 1) * 512, :]]),
                  reads=[("ogin", hp_, tc)], writes=[("ogout", hp_, tc)], slot="cc_o", ndma=1)

        for hpi, hp in enumerate((0, 1)):
            if hpi >= 1:
                sc.barrier(("pe", "act", "dve", "sp"))
                for tc_ in range(4):
                    ag_o(0, tc_)
            sc.op("dve", lambda e: e.memset(arena[:, 16384:32768], 0.0), writes=["Vp"])
            sc.op("dve", lambda e: e.memset(arena[:, 45056:47104], 0.0), writes=["Q2z", ("Q2", 0), ("Q2", 1)])
            sc.op("dve", lambda e: e.memset(arena[:, 48128:54272], 0.0), writes=[("Pd", n_) for n_ in range(3)] + [("Ad", n_) for n_ in range(3)])
            sc.op("dve", lambda e: e.memset(Kst[64:128, 0, S - 128:S], 0.0), writes=["Kst"])
            sc.op("dve", lambda e: e.memset(Kst[64:128, 1, S - 128:S], 0.0), writes=["Kst"])

            def ldk(e, hp=hp):
                r = []
                for i in range(4):
                    for h in range(2):
                        rr = 256 + hp * 128 + h * 64
                        src = minev[i, rr:rr + 64, :]
                        r.append(e.dma_start(out=Kst[0:64, h, i * 2048:(i + 1) * 2048], in_=src))
                        if i == 0:
                            r.append(e.dma_start(out=Kst[64:128, h, 0:1920], in_=src[:, 128:2048]))
                        else:
                            r.append(e.dma_start(out=Kst[64:128, h, i * 2048 - 128:(i + 1) * 2048 - 128], in_=src))
                return r
            sc.op("sp", ldk, reads=["mine"], writes=["Kst"], slot="kld", ndma=16)

            def ldv(e, hp=hp):
                r = []
                for i in range(4):
                    for h in range(2):
                        c0 = hp * 128 + h * 64
                        for q4 in range(4):
                            src = minevv[i, :, q4 * 4:q4 * 4 + 4, c0:c0 + 64]
                            r.append(e.dma_start(out=Vp[h][:, i * 16 + q4 * 4:i * 16 + q4 * 4 + 4, h * 64:(h + 1) * 64], in_=src))
                return r
            sc.op("sp", ldv, reads=["mine"], writes=["Vp"], slot="vld", ndma=32)
            sc.op("dve", lambda e: e.tensor_scalar(out=Kst[64:128, :, 0:S - 128], in0=Kst[64:128, :, 0:S - 128],
                                                   scalar1=-1.0, scalar2=None, op0=ALU.mult),
                  reads=["Kst"], writes=["Kst"])

            steps = [(qt, kb) for qt in range(16) for kb in range(4 * qt + 3, -1, -1)]
            ns = len(steps)

            def ldq(qt, hp=hp):
                qb = qt % 2

                def fn(e):
                    i = qt // 4
                    c0 = (qt % 4) * 512
                    rr = hp * 128
                    src = minev[i, rr:rr + 128, c0:c0 + 512].rearrange("(h d) c -> d h c", h=2)
                    return [e.dma_start(out=Qd[0:64, qb, :].rearrange("p (h c) -> p h c", h=2), in_=src),
                            e.dma_start(out=Qd[64:128, qb, :].rearrange("p (h c) -> p h c", h=2), in_=src),
                            e.dma_start(out=Qz[0:64, qb, :].rearrange("p (h c) -> p h c", h=2), in_=src)]
                sc.op("sp", fn, reads=["mine"], writes=[("Q2", qb)], slot=f"q2{qb}", ndma=3)

            def pbuf(s):
                qt, kb = steps[s]
                i = kb - 4 * qt
                if i >= 1:
                    return Pd[:, i - 1, :], ("Pd", i - 1)
                pb_, st_ = (s // 2) % 2, s % 2
                return PP[:, pb_, st_ * 1024:(st_ + 1) * 1024], ("PP", pb_, st_)

            def abuf(s):
                qt, kb = steps[s]
                i = kb - 4 * qt
                if i >= 1:
                    return Ad[:, i - 1, :], ("Ad", i - 1)
                return A[:, s % 2, :], ("A", s % 2)

            def stA(s):
                qt, kb = steps[s]
                zb, qb = s % 2, qt % 2
                c0 = max(0, (kb - 4 * qt)) * 128

                def fn(e):
                    r = None
                    for h in range(2):
                        q = Qz[:, qb, h * 512 + c0:(h + 1) * 512]
                        r = mm_group(e, ps[:, 2 * zb + h, c0:512], [(Kst[:, h, kb * 128:(kb + 1) * 128], q)])
                    return r
                sc.op("pe", fn, reads=["Kst", ("Q2", qb), "Q2z"], writes=[("Z", zb)])

            def stS1(s):
                qt, kb = steps[s]
                zb = s % 2
                c0 = max(0, (kb - 4 * qt)) * 128
                Zv = ps[:, 2 * zb:2 * zb + 2, c0:512]
                Ev = E[:, zb, :].rearrange("p (h c) -> p h c", h=2)[:, :, c0:512]
                pt, pkey = pbuf(s)
                Pv = pt.rearrange("p (h c) -> p h c", h=2)
                sc.op("act", lambda e: e.activation(out=Ev, in_=Zv, func=AF.Exp),
                      reads=[("Z", zb)], writes=[("E", zb)])
                sc.op("act", lambda e: e.activation(out=Pv[:, :, c0:512], in_=Ev, func=AF.Ln, bias=1.0, scale=1.0),
                      reads=[("E", zb)], writes=[pkey])
                if kb >= 4 * qt:
                    sc.op("dve", lambda e: [e.tensor_tensor(out=Pv[:, h, c0:c0 + 128], in0=Pv[:, h, c0:c0 + 128],
                                                            in1=cmat[:, TRI01, :], op=ALU.mult) for h in range(2)],
                          reads=["consts", pkey], writes=[pkey])

            def stB(s):
                qt, kb = steps[s]
                pb, qb = s % 3, qt % 2
                first = kb == 4 * qt + 3
                diag = kb >= 4 * qt
                i = kb - 4 * qt

                pt, pkey = pbuf(s)

                def fn(e):
                    r = None
                    for h in range(2):
                        q = (Qz if first else Qd)[:, qb, h * 512:(h + 1) * 512]
                        pairs = [(Kst[:, h, kb * 128:(kb + 1) * 128], q),
                                 (cmat[:, NEGTRI, :], pt[:, h * 512:(h + 1) * 512])]
                        r = mm_group(e, ps[:, 4 + h, :], pairs, start=first)
                    return r
                sc.op("pe", fn, reads=["Kst", ("Q2", qb), "Q2z", pkey, "consts"], writes=["B"])

            def stS2(s):
                qt, kb = steps[s]
                c0 = max(0, (kb - 4 * qt)) * 128
                at, akey = abuf(s)
                Av = at.rearrange("p (h c) -> p h c", h=2)
                sc.op("act", lambda e: e.activation(out=Av[:, :, c0:512], in_=ps[:, 4:6, c0:512], func=AF.Exp),
                      reads=["B"], writes=[akey])
                if kb >= 4 * qt:
                    sc.op("dve", lambda e: [e.tensor_tensor(out=Av[:, h, c0:c0 + 128], in0=Av[:, h, c0:c0 + 128],
                                                            in1=cmat[:, TRI01, :], op=ALU.mult) for h in range(2)],
                          reads=["consts", akey], writes=[akey])

            def stC1(s):
                qt, kb = steps[s]
                pb = s % 3
                last = kb == 0
                diag = kb >= 4 * qt
                i = kb - 4 * qt
                if last:
                    return

                pt, pkey = pbuf(s)

                def fn(e):
                    r = None
                    for h in range(2):
                        pairs = [(cmat[:, NEGREST, :], pt[:, h * 512:(h + 1) * 512])]
                        r = mm_group(e, ps[:, 4 + h, :], pairs, start=False)
                    return r
                sc.op("pe", fn, reads=[pkey, "consts"], writes=["B"])

            def stPV(s, hp=hp):
                qt, kb = steps[s]
                ab, ob = s % 2, qt % 2
                first = kb == 4 * qt + 3
                last = kb == 0

                at, akey = abuf(s)

                def fn(e):
                    pairs = [(Vp[h][:, kb, :], at[:, h * 512:(h + 1) * 512]) for h in range(2)]
                    return mm_group(e, ps[:, 6 + ob, :], pairs, start=first, stop=last)
                sc.op("pe", fn, reads=[akey, "Vp"], writes=[("O", ob)])
                if last:
                    sc.op("dve", lambda e: e.tensor_copy(out=Osb[:, ob, :], in_=ps[:, 6 + ob, :]),
                          reads=[("O", ob)], writes=[("Osb", ob)])
                    sc.op("sp", lambda e: e.dma_start(out=ogin[j][(qt // 4) * 256 + hp * 128:(qt // 4) * 256 + (hp + 1) * 128, (qt % 4) * 512:(qt % 4 + 1) * 512], in_=Osb[:, ob, :]),
                          reads=[("Osb", ob)], writes=[("ogin", hp, qt // 4)], slot=f"osb{ob}")
                    if qt % 4 == 3 and hp == 1:
                        ag_o(hp, qt // 4)

            def doA(s):
                if s > 0 and steps[s][0] != steps[s - 1][0]:
                    ldq(steps[s][0])
                stA(s)

            def is_diag(s):
                qt, kb = steps[s]
                return kb >= 4 * qt

            def s1_part(m, part):
                a = 2 * m
                if is_diag(a):
                    stS1(a + part)
                    return
                pb_ = m % 2
                if part == 0:
                    sc.op("act", lambda e: e.activation(out=E[:, :, :], in_=ps[:, 0:4, :].rearrange("p a b -> p (a b)").rearrange("p (a b) -> p a b", a=2), func=AF.Exp),
                          reads=[("Z", 0), ("Z", 1)], writes=[("E", 0), ("E", 1)])
                else:
                    sc.op("act", lambda e: e.activation(out=PP[:, pb_, :], in_=E[:, :, :].rearrange("p a b -> p (a b)"), func=AF.Ln, bias=1.0, scale=1.0),
                          reads=[("E", 0), ("E", 1)], writes=[("PP", pb_, 0), ("PP", pb_, 1)])

            ldq(0)
            doA(0)
            doA(1)
            s1_part(0, 0)
            s1_part(0, 1)
            doA(2)
            doA(3)
            for m in range(ns // 2):
                a, b = 2 * m, 2 * m + 1
                nxt = (2 * m + 2) < ns
                if nxt:
                    s1_part(m + 1, 0)
                if a >= 1:
                    stC1(a - 1)
                stB(a)
                stS2(a)
                if a >= 1:
                    stPV(a - 1)
                if a + 4 < ns:
                    doA(a + 4)
                if nxt:
                    s1_part(m + 1, 1)
                stC1(a)
                stB(b)
                stS2(b)
                stPV(a)
                if b + 4 < ns:
                    doA(b + 4)
            stPV(ns - 1)

        sc.barrier(("pe", "act", "dve", "sp"))
        oT = av(0, [8, T])
        if debug and layer == 0:
            sc.op("sp", lambda e: [e.dma_start(out=dbg_mine, in_=mine[0]), e.dma_start(out=dbg_ogin, in_=ogin[0])],
                  reads=["mine"] + [("ogin", h_, t_) for h_ in range(2) for t_ in range(4)], writes=["dbg"], slot="dbg", ndma=2)

        def ldo(e):
            ov = ogout[j].rearrange("(tc rh i p) t -> tc rh p i t", tc=4, rh=2, i=4)
            g = bass.ds(mysl(e), 1)
            o4 = oT.rearrange("p (i k2) t -> p i k2 t", k2=2)
            return [e.dma_start(out=o4[:, :, rh, :], in_=ov[g, rh, :, :, :].rearrange("o p i t -> (o p) i t")) for rh in range(2)]
        sc.op("pool", ldo, reads=[("ogout", h_, t_) for h_ in range(2) for t_ in range(4)], writes=["oT"], slot="oT", ndma=2)
        cnt = 0
        for dc in range(8):
            b, wkey = load_w8(wo_d[j, dc])
            wv_ = w8[:, b, :].rearrange("p (k c) -> p k c", k=8)
            for gt in range(4):
                db = 4 + cnt % 2
                cnt += 1
                sc.op("pe", lambda e, wv_=wv_, gt=gt, db=db: mm_group(
                    e, ps[:, db, :], [(wv_[:, k, :], oT[:, k, gt * 512:(gt + 1) * 512]) for k in range(8)]),
                    reads=[wkey, "oT"], writes=[("ps", db)])
                sc.op("dve", lambda e, dc=dc, gt=gt, db=db: e.tensor_tensor(
                    out=xT[:, dc, gt * 512:(gt + 1) * 512], in0=ps[:, db, :], in1=xT[:, dc, gt * 512:(gt + 1) * 512], op=ALU.add),
                    reads=[("ps", db), ("x", dc, gt)], writes=[("x", dc, gt)])

    stages = []
    for l in range(DEPTH):
        stages += [("ffn", l, 0), ("mix", l), ("ffn", l, 1), ("ple", l)]
    for st in stages:
        if st[0] == "ffn":
            ffn(st[1], st[2])
        elif st[0] == "mix":
            if st[1] % 2 == 0:
                attention(st[1])
            else:
                pool_mixer(st[1])
        else:
            ple(st[1])
        if stop_after is not None and st == stop_after:
            break

    sc.barrier(("sp",))
    for k in range(8):
        sc.op("sp", lambda e, k=k: e.dma_start(out=yT_d[k * 128:(k + 1) * 128, :], in_=xT[:, k, :]),
              reads=[("x", k, t) for t in range(4)], writes=["yT"], slot="yst")
    final_tok = sc.slot("yst")

    with nc.Block() as block:
        @block.sync
        def _(e):
            sc.replay("sp", e)
            e.wait_ge(final_tok[0], final_tok[1])
            if "dbg" in sc.slots:
                e.wait_ge(sc.slots["dbg"][0], sc.slots["dbg"][1])

        @block.gpsimd
        def _(e):
            sc.replay("pool", e)

        @block.tensor
        def _(e):
            sc.replay("pe", e)

        @block.scalar
        def _(e):
            sc.replay("act", e)

        @block.vector
        def _(e):
            sc.replay("dve", e)
    es.close()
    return nc


def _bf(a):
    return np.ascontiguousarray(a.astype(ml_dtypes.bfloat16))


def host_layout(inp, nl=DEPTH):
    f = lambda a: np.ascontiguousarray(np.asarray(a, dtype=np.float32))
    sh = {}
    gu = np.stack([f(inp["w_ffn1_gu"][:nl]), f(inp["w_ffn2_gu"][:nl])], 1)
    gate = gu[..., :DFF].reshape(nl, 2, 8, 128, NF, 128)
    up = gu[..., DFF:].reshape(nl, 2, 8, 128, NF, 128)
    g2 = np.concatenate([gate.transpose(0, 1, 4, 3, 2, 5), up.transpose(0, 1, 4, 3, 2, 5)], -1)
    sh["wgu"] = np.ascontiguousarray(g2)
    dn = np.stack([f(inp["w_ffn1_down"][:nl]), f(inp["w_ffn2_down"][:nl])], 1)
    dn = dn.reshape(nl, 2, NF, 128, 8, 128).transpose(0, 1, 4, 3, 2, 5)
    sh["wdn"] = np.ascontiguousarray(dn).reshape(nl, 2, 8, 128, NF * 128)
    wqkv = f(inp["w_qkv"])
    qk = wqkv[:, :, :2048].reshape(2, 8, 128, 16, 128).transpose(0, 3, 2, 1, 4)
    sh["wqk"] = np.ascontiguousarray(qk)
    sh["wv"] = np.ascontiguousarray(wqkv[:, :, 2048:].reshape(2, 8, 128, 1024).transpose(0, 2, 1, 3))
    c8 = lambda w: np.ascontiguousarray(w.reshape(w.shape[0], 8, 128, 8, 128).transpose(0, 3, 2, 1, 4))
    sh["wo"] = c8(f(inp["w_o"]))
    sh["wpi"] = c8(f(inp["w_pool_in"]))
    sh["wpg"] = c8(f(inp["w_ple_gate"]))
    wg = f(inp["w_pool_grp"]).reshape(2, 4, 2, 128, 256).transpose(0, 3, 1, 2, 4)
    sh["wgrp"] = np.ascontiguousarray(wg).reshape(2, 128, 2048)
    sh["wpp"] = np.ascontiguousarray(f(inp["w_ple_proj"]).reshape(DEPTH, 2, 128, 1024).transpose(0, 2, 1, 3))
    gn = np.stack([f(inp["norm_ffn1"]), f(inp["norm_mix"]), f(inp["norm_ffn2"]), f(inp["norm_ple"])], 0)
    sh["gn"] = np.ascontiguousarray(gn.reshape(4, DEPTH, 8, 128).transpose(3, 0, 1, 2)).reshape(128, 128)
    qk_g = np.stack([f(inp["q_norm"]), f(inp["k_norm"])], 0)
    qk_g = np.concatenate([qk_g, qk_g], -1)
    sh["qkg"] = np.ascontiguousarray(qk_g.transpose(2, 0, 1)).reshape(128, 4)
    sh["psc"] = np.ascontiguousarray(f(inp["pool_scale"]).reshape(2, 8, 128).transpose(2, 0, 1)).reshape(128, 16)
    cm = np.zeros((128, 7, 128), np.float32)
    cm[:, 0] = np.eye(128)
    cm[:, 1] = 1.0 / 1024
    cm[:64, 2, :64] = 1.0 / 64
    cm[64:, 2, 64:] = 1.0 / 64
    jj, ss = np.meshgrid(np.arange(128), np.arange(128), indexing="ij")
    cm[:, 3] = -1.0 * (jj >= ss)
    cm[:, 4] = -1.0 * (jj < ss)
    cm[:, 5] = -np.eye(128)
    cm[:, 6] = 1.0 * (jj < ss)
    sh["cmat"] = _bf(cm.reshape(128, 896))
    mk = np.zeros((128, 4, 512), np.float32)
    for i in range(4):
        kpos = 128 * i + np.arange(128)[:, None]
        mk[:, i] = np.where(kpos >= np.arange(512)[None, :], MASKV, 0.0)
    sh["mneg"] = _bf(mk.reshape(128, 2048))
    x = f(inp["x"])
    p = f(inp["p"])
    maps = []
    for r in range(8):
        b, c = r // 4, r % 4
        m = dict(sh)
        m["xT"] = np.ascontiguousarray(x[b, c * T:(c + 1) * T, :].T)
        m["wqk"] = np.ascontiguousarray(sh["wqk"][:, [2 * c, 2 * c + 1, 8 + 2 * c, 8 + 2 * c + 1]])
        m["wv"] = np.ascontiguousarray(sh["wv"][:, :, :, 256 * c:256 * (c + 1)])
        m["pT"] = np.ascontiguousarray(p[:, b, c * T:(c + 1) * T, :].transpose(0, 2, 1))
        s = np.zeros((128, 4), np.float32)
        if c > 0:
            s[:, c - 1] = 1.0
        m["sel"] = s
        ic = np.zeros((128, 4, 16), np.float32)
        for g in range(4):
            w = 2 << g
            if c == 0:
                ic[:, g] = 1.0 / np.minimum(np.arange(16) + 1, w)
            else:
                ic[:, g] = 1.0 / w
        m["icnt"] = ic.reshape(128, 64)
        maps.append(m)
    return maps


_NC_CACHE = {}


def kernel(_stop_after=None, _debug=False, **inputs):
    if _stop_after is not None:
        nl = _stop_after[1] + 1
        maps = host_layout(inputs, nl)
        nc = build(_stop_after, nl=nl, debug=_debug)
        res = run_bass_kernel_spmd(nc, maps, core_ids=list(range(8)))
        _NC_CACHE["res"] = res
        out = np.zeros((2, S, D), np.float32)
        for r in range(8):
            b, c = r // 4, r % 4
            out[b, c * T:(c + 1) * T, :] = np.asarray(res.results[r]["yT"]).T
        return out
    maps = host_layout(inputs)
    if False:
        _NC_CACHE["nc"] = build(_stop_after)
    if "nc" not in _NC_CACHE:
        _NC_CACHE["nc"] = build()
    res = run_bass_kernel_spmd(_NC_CACHE["nc"], maps, core_ids=list(range(8)))
    out = np.zeros((2, S, D), np.float32)
    for r in range(8):
        b, c = r // 4, r % 4
        out[b, c * T:(c + 1) * T, :] = np.asarray(res.results[r]["yT"]).T
    return out
```

```python
import contextlib
import numpy as np
import ml_dtypes
import concourse.bass as bass
import concourse.mybir as mybir
from concourse.bass_utils import run_bass_kernel_spmd

F32 = mybir.dt.float32
BF16 = mybir.dt.bfloat16
AF = mybir.ActivationFunctionType
ALU = mybir.AluOpType

D = 1024
T = 2048
S = 8192
DFF = 2816
NF = 22
DEPTH = 4
EPS = 1e-6
MASKV = -128.0
ARENA = 54272
COMPUTE = ("pe", "act", "dve")


class Sched:
    def __init__(self, nc, es):
        self.nc = nc
        self.es = es
        self.prog = {e: [] for e in ("pe", "act", "dve", "pool", "sp")}
        self.esem = {e: es.enter_context(nc.semaphore("s_" + e)) for e in COMPUTE}
        self.ecnt = {e: 0 for e in COMPUTE}
        self.slots = {}
        self.waited = {e: {} for e in self.prog}
        self.lastw = {}
        self.readers = {}
        self.barrier_toks = []
        self.pending = {e: [] for e in self.prog}
        self.last_barrier = []

    def slot(self, name):
        if name not in self.slots:
            self.slots[name] = [self.es.enter_context(self.nc.semaphore("d_" + name)), 0]
        return self.slots[name]

    def barrier(self, engines=("pe", "act", "dve", "sp", "pool")):
        toks = [(self.esem[e], self.ecnt[e], e) for e in COMPUTE if self.ecnt[e] > 0]
        toks += [(s[0], s[1], None) for s in self.slots.values() if s[1] > 0]
        self.last_barrier = list(toks)
        for e in engines:
            self.pending[e] = list(toks)

    def op(self, eng, fn, reads=(), writes=(), slot=None, ndma=1, after_barrier=False):
        toks = list(self.pending[eng])
        self.pending[eng] = []
        if after_barrier:
            toks += self.last_barrier
        for k in reads:
            if k in self.lastw:
                toks.append(self.lastw[k])
        for k in writes:
            if k in self.lastw:
                toks.append(self.lastw[k])
            toks.extend(self.readers.get(k, ()))
        waits = {}
        for (sem, val, src) in toks:
            if eng == "pe" and src == "pe":
                continue
            key = id(sem)
            if key not in waits or waits[key][1] < val:
                waits[key] = (sem, val)
        wl = []
        for key, (sem, val) in waits.items():
            if self.waited[eng].get(key, 0) >= val:
                continue
            self.waited[eng][key] = val
            wl.append((sem, val))
        if eng in COMPUTE:
            self.ecnt[eng] += 1
            tok = (self.esem[eng], self.ecnt[eng], eng)
            inc = (self.esem[eng], 1)
        else:
            sl = self.slot(slot)
            step = 1 if slot.startswith("cc_") else 16
            sl[1] += step * ndma
            tok = (sl[0], sl[1], None)
            inc = (sl[0], step)
        self.prog[eng].append((wl, fn, inc))
        for k in writes:
            self.lastw[k] = tok
            self.readers[k] = []
        for k in reads:
            self.readers.setdefault(k, []).append(tok)
        return tok

    def replay(self, eng_name, eng):
        compute = eng_name in COMPUTE
        for (wl, fn, inc) in self.prog[eng_name]:
            for (sem, val) in wl:
                eng.wait_ge(sem, val)
            r = fn(eng)
            if r is None:
                continue
            if not isinstance(r, (list, tuple)):
                r = [r]
            if compute:
                r[-1].then_inc(inc[0], inc[1])
            else:
                for ins in r:
                    ins.then_inc(inc[0], inc[1])


def build(stop_after=None, nl=DEPTH, debug=False):
    nc = bass.Bass("TRN2", target_bir_lowering=False)
    es = contextlib.ExitStack()

    def din(name, shape, dt=F32):
        return nc.dram_tensor(name, list(shape), dt, kind="ExternalInput").ap()

    xT_d = din("xT", [D, T])
    pT_d = din("pT", [DEPTH, 256, T])
    wgu_d = din("wgu", [nl, 2, NF, 128, 8, 256])
    wdn_d = din("wdn", [nl, 2, 8, 128, NF * 128])
    wqk_d = din("wqk", [2, 4, 128, 8, 128])
    wv_d = din("wv", [2, 128, 8, 256])
    wo_d = din("wo", [2, 8, 128, 8, 128])
    wpi_d = din("wpi", [2, 8, 128, 8, 128])
    wpg_d = din("wpg", [DEPTH, 8, 128, 8, 128])
    wgrp_d = din("wgrp", [2, 128, 4 * 2 * 256])
    wpp_d = din("wpp", [DEPTH, 128, 2, 1024])
    gn_d = din("gn", [128, 4 * DEPTH * 8])
    qkg_d = din("qkg", [128, 4])
    psc_d = din("psc", [128, 16])
    sel_d = din("sel", [128, 4])
    icnt_d = din("icnt", [128, 64])
    cmat_d = din("cmat", [128, 7 * 128], BF16)
    mneg_d = din("mneg", [128, 4 * 512], BF16)
    yT_d = nc.dram_tensor("yT", [D, T], F32, kind="ExternalOutput").ap()
    if debug:
        dbg_mine = nc.dram_tensor("dbg_mine", [4 * 768, 2048], BF16, kind="ExternalOutput").ap()
        dbg_ogin = nc.dram_tensor("dbg_ogin", [1024, 2048], BF16, kind="ExternalOutput").ap()

    hgin_a = [nc.dram_tensor(f"hgina{j}", [2048, 1024], BF16, kind="Internal").ap() for j in range(2)]
    hgout_a = [nc.dram_tensor(f"hgouta{j}", [4 * 2048, 1024], BF16, kind="Internal").ap() for j in range(2)]
    ogin = [nc.dram_tensor(f"ogin{j}", [1024, 2048], BF16, kind="Internal").ap() for j in range(2)]
    ogout = [nc.dram_tensor(f"ogout{j}", [4096, 2048], BF16, kind="Internal").ap() for j in range(2)]
    mine = [nc.dram_tensor(f"mine{j}", [4 * 768, 2048], BF16, kind="Internal").ap() for j in range(2)]
    hgin = [nc.dram_tensor(f"hgin{j}", [128, 128], F32, kind="Internal").ap() for j in range(2)]
    hgout = [nc.dram_tensor(f"hgout{j}", [4 * 128, 128], F32, kind="Internal").ap() for j in range(2)]

    def sb(name, shape, dt):
        return es.enter_context(nc.sbuf_tensor(name, list(shape), dt))

    xT = sb("xT_sb", [128, 8, T], F32)
    arena = sb("arena", [128, ARENA], BF16)
    wgu = sb("wgu_sb", [128, 2, 8 * 256], BF16)
    wdn = sb("wdn_sb", [128, 2, NF * 128], BF16)
    w8 = sb("w8_sb", [128, 3, 8 * 128], BF16)
    cmat = sb("cmat_sb", [128, 7, 128], BF16)
    mneg = sb("mneg_sb", [128, 4, 512], BF16)
    gn = sb("gn_sb", [128, 4, DEPTH, 8], F32)
    qkg = sb("qkg_sb", [128, 2, 2], F32)
    qg8 = sb("qg8_sb", [128, 2], F32)
    psc = sb("psc_sb", [128, 2, 8], F32)
    sel = sb("sel_sb", [128, 4], F32)
    icnt = sb("icnt_sb", [128, 4, 16], F32)
    wgrp = sb("wgrp_sb", [128, 4, 2, 256], BF16)
    ps = es.enter_context(nc.psum_tensor("ps", [128, 8, 512], F32))

    IDENT, ONES_MS, ONES_HD, NEGTRI, NEGREST, NEGIDENT, TRI01 = range(7)

    def av(off, shape, dt=BF16):
        n = int(np.prod(shape))
        if dt == F32:
            v = arena[:, off:off + 2 * n].bitcast(F32)
        else:
            v = arena[:, off:off + n]
        if len(shape) == 1:
            return v
        if len(shape) == 2:
            return v.rearrange("p (a b) -> p a b", a=shape[0])
        return v.rearrange("p (a b c) -> p a b c", a=shape[0], b=shape[1])

    sc = Sched(nc, es)
    me4 = {}

    def ld_consts(e):
        return [
            e.dma_start(out=cmat[:].rearrange("p a b -> p (a b)"), in_=cmat_d),
            e.dma_start(out=mneg[:].rearrange("p a b -> p (a b)"), in_=mneg_d),
            e.dma_start(out=gn[:].rearrange("p a b c -> p (a b c)"), in_=gn_d),
            e.dma_start(out=qkg[:].rearrange("p a b -> p (a b)"), in_=qkg_d),
            e.dma_start(out=psc[:].rearrange("p a b -> p (a b)"), in_=psc_d),
            e.dma_start(out=sel[:], in_=sel_d),
            e.dma_start(out=icnt[:].rearrange("p a b -> p (a b)"), in_=icnt_d),
        ]
    sc.op("sp", ld_consts, writes=["consts"], slot="consts", ndma=7)
    for k in range(8):
        sc.op("sp", lambda e, k=k: e.dma_start(out=xT[:, k, :], in_=xT_d[k * 128:(k + 1) * 128, :]),
              writes=[("x", k, t) for t in range(4)], slot=f"xld{k}")
    sc.op("dve", lambda e: e.tensor_scalar(out=qg8[:], in0=qkg[:, 0, :], scalar1=0.125, scalar2=None, op0=ALU.mult),
          reads=["consts"], writes=["qg8"])

    ring_cnt = {"wgu": 0, "wdn": 0, "w8": 0}

    def load_w(kind, src_ap, nbuf, view):
        i = ring_cnt[kind]
        ring_cnt[kind] += 1
        b = i % nbuf
        key = (kind, b)
        sc.op("pool", lambda e: e.dma_start(out=view(b), in_=src_ap), writes=[key], slot=f"{kind}{b}")
        return b, key

    def load_wgu(l, which, f):
        return load_w("wgu", wgu_d[l, which, f].rearrange("p k c -> p (k c)"), 2, lambda b: wgu[:, b, :])

    def load_wdn(l, which, dc):
        i = ring_cnt["wdn"]
        ring_cnt["wdn"] += 1
        b = i % 2
        key = ("wdn", b)
        h = NF * 64

        def fn(e):
            return [e.dma_start(out=wdn[:, b, 0:h], in_=wdn_d[l, which, dc, :, 0:h]),
                    e.dma_start(out=wdn[:, b, h:2 * h], in_=wdn_d[l, which, dc, :, h:2 * h])]
        sc.op("pool", fn, writes=[key], slot=f"wdn{b}", ndma=2)
        return b, key

    def load_w8(src3):
        return load_w("w8", src3.rearrange("p k c -> p (k c)"), 3, lambda b: w8[:, b, :])

    O_HT = 0
    O_SQ, O_LN, O_RS, O_SG = 44032, 48128, 49152, 50176

    def mm_group(e, out, pairs, start=True, stop=True):
        r = None
        n = len(pairs)
        for i, (l, rh) in enumerate(pairs):
            r = e.matmul(out, lhsT=l, rhs=rh, start=(start and i == 0), stop=(stop and i == n - 1),
                         skip_group_check=True)
        return r

    def norm_half(half, kind, layer, ssbank=6, hoff=0, hkey="hT"):
        hT = av(hoff, [8, 1024])
        sq = av(O_SQ, [8, 512])
        lnv = av(O_LN, [512], F32)
        rstd = av(O_RS, [512], F32)
        for tt in range(2):
            gt = half * 2 + tt
            c0 = gt * 512
            sc.op("act", lambda e, c0=c0: e.activation(out=sq, in_=xT[:, :, c0:c0 + 512], func=AF.Square),
                  reads=[("x", k, gt) for k in range(8)], writes=["sq"])
            sc.op("pe", lambda e: mm_group(e, ps[:, ssbank, :], [(cmat[:, ONES_MS, :], sq[:, k, :]) for k in range(8)]),
                  reads=["sq", "consts"], writes=[("ps", ssbank)])
            sc.op("act", lambda e: e.activation(out=lnv, in_=ps[:, ssbank, :], func=AF.Ln, bias=EPS, scale=1.0),
                  reads=[("ps", ssbank)], writes=["lnv"])
            sc.op("act", lambda e: e.activation(out=rstd, in_=lnv, func=AF.Exp, scale=-0.5),
                  reads=["lnv"], writes=["rstd"])
            for k in range(8):
                sc.op("dve", lambda e, k=k, c0=c0, tt=tt: e.scalar_tensor_tensor(
                    out=hT[:, k, tt * 512:(tt + 1) * 512], in0=xT[:, k, c0:c0 + 512],
                    scalar=gn[:, kind, layer, k:k + 1], in1=rstd, op0=ALU.mult, op1=ALU.mult),
                    reads=[("x", k, gt), "rstd", "consts"], writes=[(hkey, k, tt)])
        return hT

    def mix_prenorm(layer, half, defer=False):
        j = layer // 2
        hT = norm_half(half, 1, layer)
        hv = hgin_a[j].rearrange("(h k p) t -> h p k t", h=2, p=128)
        sc.op("sp", lambda e: e.dma_start(out=hv[half], in_=hT),
              reads=[("hT", k, tt) for k in range(8) for tt in range(2)], writes=[("hgin_a", half)], slot="hgst")
        def emit_ag():
            sc.op("pool", lambda e: [e.collective_compute("AllGather", ALU.bypass, replica_groups=[[0, 1, 2, 3], [4, 5, 6, 7]],
                                                          ins=[hgin_a[j][(half * 4 + q) * 256:(half * 4 + q + 1) * 256, :]],
                                                          outs=[hgout_a[j][(half * 4 + q) * 1024:(half * 4 + q + 1) * 1024, :]])
                                     for q in range(4)],
                  reads=[("hgin_a", half)], writes=[("hgout_a", half)], slot="cc_a", ndma=4)
        if defer:
            return emit_ag
        emit_ag()

    def ffn(layer, which):
        sc.barrier(("pe", "act", "dve", "sp"))
        kind = 0 if which == 0 else 2
        aT = av(8192, [NF, 1024])
        sgt = av(O_SG, [2, 512], F32)
        hTs = {0: norm_half(0, kind, layer)}
        deferred = []
        want_pre = False
        for half in range(2):
            hT = hTs[half]
            hkey = "hT" if half == 0 else "hT2"
            cnt = 0
            for f in range(NF):
                if half == 0 and f == NF // 2:
                    hTs[1] = norm_half(1, kind, layer, hoff=30720, hkey="hT2")
                b, wkey = load_wgu(layer, which, f)
                if half == 1 and f == 2 and want_pre:
                    deferred.append(mix_prenorm(layer, 0, defer=True))
                if half == 1 and f == 10 and deferred:
                    deferred.pop()()
                for tt in range(2):
                    gb = cnt % 2
                    cnt += 1
                    wv_ = wgu[:, b, :].rearrange("p (k c) -> p k c", k=8)
                    sc.op("pe", lambda e, wv_=wv_, tt=tt, gb=gb, hT=hT: [
                        mm_group(e, ps[:, gb, :], [(wv_[:, k, 0:128], hT[:, k, tt * 512:(tt + 1) * 512]) for k in range(8)]),
                        mm_group(e, ps[:, 2 + gb, :], [(wv_[:, k, 128:256], hT[:, k, tt * 512:(tt + 1) * 512]) for k in range(8)])],
                        reads=[wkey] + [(hkey, k, tt) for k in range(8)], writes=[("ps", gb), ("ps", 2 + gb)])
                    sc.op("act", lambda e, gb=gb: e.activation(out=sgt[:, gb, :], in_=ps[:, gb, :], func=AF.Silu),
                          reads=[("ps", gb)], writes=[("sgt", gb)])
                    sc.op("dve", lambda e, gb=gb, f=f, tt=tt: e.tensor_tensor(
                        out=aT[:, f, tt * 512:(tt + 1) * 512], in0=sgt[:, gb, :], in1=ps[:, 2 + gb, :], op=ALU.mult),
                        reads=[("sgt", gb), ("ps", 2 + gb)], writes=[("aT", f, tt)])
            cnt = 0
            for dc in range(8):
                b, wkey = load_wdn(layer, which, dc)
                wv_ = wdn[:, b, :].rearrange("p (f c) -> p f c", f=NF)
                for tt in range(2):
                    db = 4 + cnt % 2
                    cnt += 1
                    gt = half * 2 + tt
                    sc.op("pe", lambda e, wv_=wv_, tt=tt, db=db: mm_group(
                        e, ps[:, db, :], [(wv_[:, f, :], aT[:, f, tt * 512:(tt + 1) * 512]) for f in range(NF)]),
                        reads=[wkey] + [("aT", f, tt) for f in range(NF)], writes=[("ps", db)])
                    sc.op("dve", lambda e, dc=dc, gt=gt, db=db: e.scalar_tensor_tensor(
                        out=xT[:, dc, gt * 512:(gt + 1) * 512], in0=ps[:, db, :], scalar=0.5,
                        in1=xT[:, dc, gt * 512:(gt + 1) * 512], op0=ALU.mult, op1=ALU.add),
                        reads=[("ps", db), ("x", dc, gt)], writes=[("x", dc, gt)])
            if half == 0 and which == 0 and layer % 2 == 0:
                want_pre = True

    def ple(layer):
        sc.barrier(("pe", "act", "dve", "sp"))
        pTb = av(38912, [2, T])
        wpp = av(52224, [2, 1024])
        sgm = av(14336, [2, 512], F32)
        tmp = av(16384, [2, 512], F32)
        sc.op("pool", lambda e: [e.dma_start(out=pTb, in_=pT_d[layer].rearrange("(k p) t -> p k t", p=128)),
                                 e.dma_start(out=wpp, in_=wpp_d[layer])],
              writes=["pTb", "wpp"], slot="plew", ndma=2)
        cnt = 0
        hTs = {0: norm_half(0, 3, layer)}
        for half in range(2):
            hT = hTs[half]
            hkey = "hT" if half == 0 else "hT2"
            for dc in range(8):
                if half == 0 and dc == 4:
                    hTs[1] = norm_half(1, 3, layer, hoff=30720, hkey="hT2")
                b, wkey = load_w8(wpg_d[layer, dc])
                wv_ = w8[:, b, :].rearrange("p (k c) -> p k c", k=8)
                for tt in range(2):
                    gb = cnt % 2
                    cnt += 1
                    gt = half * 2 + tt
                    sc.op("pe", lambda e, wv_=wv_, tt=tt, gb=gb, dc=dc, gt=gt, hT=hT: [
                        mm_group(e, ps[:, gb, :], [(wv_[:, k, :], hT[:, k, tt * 512:(tt + 1) * 512]) for k in range(8)]),
                        mm_group(e, ps[:, 2 + gb, :], [(wpp[:, k, dc * 128:(dc + 1) * 128], pTb[:, k, gt * 512:(gt + 1) * 512]) for k in range(2)])],
                        reads=[wkey, "pTb", "wpp"] + [(hkey, k, tt) for k in range(8)], writes=[("ps", gb), ("ps", 2 + gb)])
                    sc.op("act", lambda e, gb=gb: e.activation(out=sgm[:, gb, :], in_=ps[:, gb, :], func=AF.Sigmoid),
                          reads=[("ps", gb)], writes=[("sgm", gb)])
                    sc.op("dve", lambda e, gb=gb: e.tensor_tensor(out=tmp[:, gb, :], in0=sgm[:, gb, :], in1=ps[:, 2 + gb, :], op=ALU.mult),
                          reads=[("sgm", gb), ("ps", 2 + gb)], writes=[("tmp", gb)])
                    sc.op("dve", lambda e, gb=gb, dc=dc, gt=gt: e.tensor_tensor(
                        out=xT[:, dc, gt * 512:(gt + 1) * 512], in0=tmp[:, gb, :], in1=xT[:, dc, gt * 512:(gt + 1) * 512], op=ALU.add),
                        reads=[("tmp", gb), ("x", dc, gt)], writes=[("x", dc, gt)])

    def pool_mixer(layer):
        j = layer // 2
        sc.barrier(("pe", "act", "dve", "sp"))
        U = [av(0, [8, 1040], F32), av(16640, [8, 1040], F32)]
        O_H = 33280
        hT = av(O_H, [8, 1024])
        tmpS = av(41472, [2, 1040], F32)
        hal = av(45632, [4, 128], F32)
        sc.op("pool", lambda e: e.dma_start(out=wgrp[:].rearrange("p a b c -> p (a b c)"), in_=wgrp_d[j]),
              writes=["wgrp"], slot="wgrp")

        def do_norm(half):
            sq = av(46656, [8, 512])
            lnv = tmpS[:, 0, 0:512]
            rstd = tmpS[:, 1, 0:512]
            for tt in range(2):
                gt = half * 2 + tt
                c0 = gt * 512
                sc.op("act", lambda e, c0=c0: e.activation(out=sq, in_=xT[:, :, c0:c0 + 512], func=AF.Square),
                      reads=[("x", k, gt) for k in range(8)], writes=["sq"])
                sc.op("pe", lambda e: mm_group(e, ps[:, 6, :], [(cmat[:, ONES_MS, :], sq[:, k, :]) for k in range(8)]),
                      reads=["sq", "consts"], writes=[("ps", 6)])
                sc.op("act", lambda e: e.activation(out=lnv, in_=ps[:, 6, :], func=AF.Ln, bias=EPS, scale=1.0),
                      reads=[("ps", 6)], writes=[("tmpS", 0)])
                sc.op("act", lambda e: e.activation(out=rstd, in_=lnv, func=AF.Exp, scale=-0.5),
                      reads=[("tmpS", 0)], writes=[("tmpS", 1)])
                for k in range(8):
                    sc.op("dve", lambda e, k=k, c0=c0, tt=tt: e.scalar_tensor_tensor(
                        out=hT[:, k, tt * 512:(tt + 1) * 512], in0=xT[:, k, c0:c0 + 512],
                        scalar=gn[:, 1, layer, k:k + 1], in1=rstd, op0=ALU.mult, op1=ALU.mult),
                        reads=[("x", k, gt), ("tmpS", 1), "consts"], writes=[("hT", k, tt), ("pl", k)])

        def compute_u(half):
            cnt = 0
            for uc in range(8):
                b, wkey = load_w8(wpi_d[j, uc])
                wv_ = w8[:, b, :].rearrange("p (k c) -> p k c", k=8)
                for tt in range(2):
                    gb = cnt % 2
                    cnt += 1
                    sc.op("pe", lambda e, wv_=wv_, tt=tt, gb=gb: mm_group(
                        e, ps[:, gb, :], [(wv_[:, k, :], hT[:, k, tt * 512:(tt + 1) * 512]) for k in range(8)]),
                        reads=[wkey] + [("hT", k, tt) for k in range(8)], writes=[("ps", gb)])
                    sc.op("act", lambda e, gb=gb, uc=uc, tt=tt: e.activation(
                        out=U[half][:, uc, 16 + tt * 512:16 + (tt + 1) * 512], in_=ps[:, gb, :], func=AF.Copy),
                        reads=[("ps", gb)], writes=[("U", half, uc)])

        def pool_and_mix(half):
            pooled = hT
            for uc in range(8):
                g = uc // 2
                w = 2 << g
                cur = U[half][:, uc, :]
                lo = 0
                nsteps = g + 1
                srcbuf = cur
                for st in range(nsteps):
                    sh = 1 << st
                    lo2 = lo + sh
                    dst = tmpS[:, st % 2, :]
                    sc.op("dve", lambda e, dst=dst, srcbuf=srcbuf, lo2=lo2, sh=sh: e.tensor_tensor(
                        out=dst[:, lo2:1040], in0=srcbuf[:, lo2:1040], in1=srcbuf[:, lo2 - sh:1040 - sh], op=ALU.add),
                        reads=[("U", half, uc), ("tmpS", 0), ("tmpS", 1)], writes=[("tmpS", st % 2)])
                    srcbuf = dst
                    lo = lo2
                sfin = srcbuf
                sc.op("dve", lambda e, sfin=sfin, cur=cur, uc=uc, w=w: e.scalar_tensor_tensor(
                    out=pooled[:, uc, :], in0=sfin[:, 16:1040], scalar=1.0 / w, in1=cur[:, 16:1040],
                    op0=ALU.mult, op1=ALU.subtract),
                    reads=[("tmpS", 0), ("tmpS", 1), ("U", half, uc)],
                    writes=[("pl", uc), ("hT", uc, 0), ("hT", uc, 1)])
                if half == 0:
                    t16 = tmpS[:, (nsteps) % 2, 0:16]
                    sc.op("dve", lambda e, t16=t16, sfin=sfin, g=g: e.tensor_tensor(
                        out=t16, in0=sfin[:, 16:32], in1=icnt[:, g, :], op=ALU.mult),
                        reads=[("tmpS", 0), ("tmpS", 1), "consts"], writes=[("tmpS", nsteps % 2)])
                    sc.op("dve", lambda e, t16=t16, cur=cur, uc=uc: e.tensor_tensor(
                        out=pooled[:, uc, 0:16], in0=t16, in1=cur[:, 16:32], op=ALU.subtract),
                        reads=[("tmpS", 0), ("tmpS", 1), ("U", half, uc), ("pl", uc)], writes=[("pl", uc)])
            cnt = 0
            for g in range(4):
                for dd in range(2):
                    dc = 2 * g + dd
                    for tt in range(2):
                        db = 4 + cnt % 2
                        cnt += 1
                        gt = half * 2 + tt
                        sc.op("pe", lambda e, g=g, dd=dd, tt=tt, db=db: mm_group(
                            e, ps[:, db, :], [(wgrp[:, g, cc, dd * 128:(dd + 1) * 128], pooled[:, 2 * g + cc, tt * 512:(tt + 1) * 512]) for cc in range(2)]),
                            reads=["wgrp", ("pl", 2 * g), ("pl", 2 * g + 1)], writes=[("ps", db)])
                        sc.op("dve", lambda e, dc=dc, gt=gt, db=db: e.scalar_tensor_tensor(
                            out=xT[:, dc, gt * 512:(gt + 1) * 512], in0=ps[:, db, :], scalar=psc[:, j, dc:dc + 1],
                            in1=xT[:, dc, gt * 512:(gt + 1) * 512], op0=ALU.mult, op1=ALU.add),
                            reads=[("ps", db), ("x", dc, gt), "consts"], writes=[("x", dc, gt)])

        do_norm(1)
        compute_u(1)
        sc.op("sp", lambda e: e.dma_start(out=hgin[j].rearrange("p (k c) -> p k c", k=8), in_=U[1][:, :, 1024:1040]),
              reads=[("U", 1, uc) for uc in range(8)], writes=["hgin"], slot="hgin")
        sc.op("pool", lambda e: e.collective_compute("AllGather", ALU.bypass, replica_groups=[[0, 1, 2, 3], [4, 5, 6, 7]],
                                                     ins=[hgin[j]], outs=[hgout[j]]),
              reads=["hgin"], writes=["hgout"], slot="cc_h")
        do_norm(0)
        compute_u(0)
        for uc in range(8):
            sc.op("dve", lambda e, uc=uc: e.tensor_copy(out=U[1][:, uc, 0:16], in_=U[0][:, uc, 1024:1040]),
                  reads=[("U", 0, uc)], writes=[("U", 1, uc)])
        pool_and_mix(1)
        sc.op("sp", lambda e: e.dma_start(out=hal, in_=hgout[j].rearrange("(i p) c -> p i c", p=128)),
              reads=["hgout"], writes=["hal"], slot="hal")
        for uc in range(8):
            halv = hal.rearrange("p i (k c) -> p i k c", k=8)
            sc.op("dve", lambda e, uc=uc, halv=halv: e.tensor_scalar(
                out=U[0][:, uc, 0:16], in0=halv[:, 0, uc, :], scalar1=sel[:, 0:1], scalar2=None, op0=ALU.mult),
                reads=["hal", "consts"], writes=[("U", 0, uc)])
            for i in range(1, 4):
                sc.op("dve", lambda e, uc=uc, i=i, halv=halv: e.scalar_tensor_tensor(
                    out=U[0][:, uc, 0:16], in0=halv[:, i, uc, :], scalar=sel[:, i:i + 1], in1=U[0][:, uc, 0:16],
                    op0=ALU.mult, op1=ALU.add),
                    reads=["hal", "consts", ("U", 0, uc)], writes=[("U", 0, uc)])
        pool_and_mix(0)

    def attention(layer):
        j = layer // 2
        sc.barrier(("pe", "act", "dve", "sp"))
        hTi = [av(8192, [8, 1024]), av(16384, [8, 1024])]
        wqb = av(24576, [4, 8 * 128])
        wvb = av(28672, [8, 256])
        qst = av(30720, [2, 512])
        vst = av(31744, [4, 256])
        sqh = av(32768, [2, 512])
        lnv = av(O_LN, [512], F32)
        rstd = av(O_RS, [512], F32)
        sc.op("pool", lambda e: [e.dma_start(out=wqb[:, qc, :], in_=wqk_d[j, qc].rearrange("p k c -> p (k c)")) for qc in range(4)]
              + [e.dma_start(out=wvb, in_=wv_d[j])],
              writes=["wqb", "wvb"], slot="wqv", ndma=5, after_barrier=True)
        mix_prenorm(layer, 1)
        hgv = hgout_a[j].rearrange("(c i k2 p) t -> c i p k2 t", c=8, i=4, k2=2)
        minev = mine[j].rearrange("(i r) c -> i r c", i=4)
        cq = 0
        cv = 0
        it = 0

        def hload(it_):
            half_, i_ = it_ // 4, it_ % 4
            hb_ = it_ % 2
            sc.op("pool", lambda e: [e.dma_start(out=hTi[hb_][:, 2 * q:2 * q + 2, :], in_=hgv[half_ * 4 + q, i_]) for q in range(4)],
                  reads=[("hgout_a", half_)], writes=[("hTi", hb_)], slot=f"hld{hb_}", ndma=4)
        hload(0)
        for half in range(2):
            for i in range(4):
                hb = it % 2
                it += 1
                hX = hTi[hb]
                if it < 8:
                    hload(it)
                PB = (0, 1, 6, 7)
                groups = [(qc, tt) for qc in range(4) for tt in range(2)]

                def g_mm(qc, tt, pbk, hX=hX, hb=hb):
                    wv_ = wqb[:, qc, :].rearrange("p (k c) -> p k c", k=8)
                    sc.op("pe", lambda e: mm_group(
                        e, ps[:, pbk, :], [(wv_[:, k, :], hX[:, k, tt * 512:(tt + 1) * 512]) for k in range(8)]),
                        reads=["wqb", ("hTi", hb)], writes=[("ps", pbk)])

                def g_rest(qc, tt, pbk, gb, i=i, half=half):
                    isk = qc // 2
                    c2 = qc % 2
                    sc.op("act", lambda e: e.activation(out=sqh[:, gb, :], in_=ps[:, pbk, :], func=AF.Square),
                          reads=[("ps", pbk)], writes=[("sqh", gb)])
                    sc.op("pe", lambda e: mm_group(e, ps[:, 2 + gb, :], [(cmat[:, ONES_HD, :], sqh[:, gb, :])]),
                          reads=[("sqh", gb), "consts"], writes=[("ps", 2 + gb)])
                    sc.op("act", lambda e: e.activation(out=lnv, in_=ps[:, 2 + gb, :], func=AF.Ln, bias=EPS, scale=1.0),
                          reads=[("ps", 2 + gb)], writes=["lnv"])
                    sc.op("act", lambda e: e.activation(out=rstd, in_=lnv, func=AF.Exp, scale=-0.5),
                          reads=["lnv"], writes=["rstd"])
                    gsc = qkg[:, 1, j:j + 1] if isk else qg8[:, j:j + 1]
                    sc.op("dve", lambda e: e.scalar_tensor_tensor(
                        out=qst[:, gb, :], in0=ps[:, pbk, :], scalar=gsc, in1=rstd, op0=ALU.mult, op1=ALU.mult),
                        reads=[("ps", pbk), "rstd", "consts", "qg8"], writes=[("qst", gb)])
                    r0 = isk * 256 + c2 * 128
                    col = half * 1024 + tt * 512
                    sc.op("sp", lambda e: e.dma_start(out=minev[i, r0:r0 + 128, col:col + 512], in_=qst[:, gb, :]),
                          reads=[("qst", gb)], writes=["mine"], slot=f"qst{gb}")

                idx = [cq + n for n in range(len(groups))]
                cq += len(groups)
                g_mm(*groups[0], PB[idx[0] % 4])
                for gi in range(len(groups)):
                    if gi + 1 < len(groups):
                        g_mm(*groups[gi + 1], PB[idx[gi + 1] % 4])
                    g_rest(*groups[gi], PB[idx[gi] % 4], idx[gi] % 2)
                vreg = minev[i, 512:768, :].rearrange("r (t8 c) -> (r t8) c", c=256)
                for tb in range(8):
                    gb = cv % 2
                    vb = cv % 4
                    cv += 1
                    sc.op("pe", lambda e, tb=tb, gb=gb, hX=hX: mm_group(
                        e, ps[:, 4 + gb, 0:256], [(hX[:, k, tb * 128:(tb + 1) * 128], wvb[:, k, :]) for k in range(8)]),
                        reads=["wvb", ("hTi", hb)], writes=[("ps", 4 + gb)])
                    sc.op("act", lambda e, gb=gb, vb=vb: e.activation(out=vst[:, vb, :], in_=ps[:, 4 + gb, 0:256], func=AF.Copy),
                          reads=[("ps", 4 + gb)], writes=[("vst", vb)])
                    tok0 = half * 1024 + tb * 128
                    sc.op("sp", lambda e, vb=vb, tok0=tok0, vreg=vreg: e.dma_start(out=vreg[tok0:tok0 + 128, :], in_=vst[:, vb, :]),
                          reads=[("vst", vb)], writes=["mine"], slot=f"vst{vb}")
        sc.barrier(("pe", "act", "dve", "sp"))

        Kst = av(0, [2, S])
        Vp = [av(16384, [64, 128]), av(24576, [64, 128])]
        E = av(32768, [2, 1024], F32)
        PP = av(36864, [2, 2 * 1024])
        A = av(40960, [2, 1024])
        Qd = av(43008, [2, 1024])
        Qz = av(45056, [2, 1024])
        Osb = av(47104, [2, 512])
        Pd = av(48128, [3, 1024])
        Ad = av(51200, [3, 1024])
        minev = mine[j].rearrange("(i r) c -> i r c", i=4)
        minevv = mine[j].rearrange("(i r) (t8 c) -> i (r t8) c", i=4, c=256)[:, 4096:6144, :].rearrange(
            "i (b s) c -> i s b c", s=128)

        def mysl(e):
            return e.partition_id() % 4

        def ag_o(hp_, tc):
            sc.op("pool", lambda e: e.collective_compute("AllGather", ALU.bypass, replica_groups=[[0, 1, 2, 3], [4, 5, 6, 7]],
                                                         ins=[ogin[j][(tc * 2 + hp_) * 128:(tc * 2 + hp_ + 1) * 128, :]],
                                                         outs=[ogout[j][(tc * 2 + hp_) * 512:(tc * 2 + hp_ + 1) * 512, :]]),
                  reads=[("ogin", hp_, tc)], writes=[("ogout", hp_, tc)], slot="cc_o", ndma=1)

        for hpi, hp in enumerate((0, 1)):
            if hpi >= 1:
                sc.barrier(("pe", "act", "dve", "sp"))
                for tc_ in range(4):
                    ag_o(0, tc_)
            sc.op("dve", lambda e: e.memset(arena[:, 16384:32768], 0.0), writes=["Vp"])
            sc.op("dve", lambda e: e.memset(arena[:, 45056:47104], 0.0), writes=["Q2z", ("Q2", 0), ("Q2", 1)])
            sc.op("dve", lambda e: e.memset(arena[:, 48128:54272], 0.0), writes=[("Pd", n_) for n_ in range(3)] + [("Ad", n_) for n_ in range(3)])
            sc.op("dve", lambda e: e.memset(Kst[64:128, 0, S - 128:S], 0.0), writes=["Kst"])
            sc.op("dve", lambda e: e.memset(Kst[64:128, 1, S - 128:S], 0.0), writes=["Kst"])

            def ldk(e, hp=hp):
                r = []
                for i in range(4):
                    for h in range(2):
                        rr = 256 + hp * 128 + h * 64
                        src = minev[i, rr:rr + 64, :]
                        r.append(e.dma_start(out=Kst[0:64, h, i * 2048:(i + 1) * 2048], in_=src))
                        if i == 0:
                            r.append(e.dma_start(out=Kst[64:128, h, 0:1920], in_=src[:, 128:2048]))
                        else:
                            r.append(e.dma_start(out=Kst[64:128, h, i * 2048 - 128:(i + 1) * 2048 - 128], in_=src))
                return r
            sc.op("sp", ldk, reads=["mine"], writes=["Kst"], slot="kld", ndma=16)

            def ldv(e, hp=hp):
                r = []
                for i in range(4):
                    for h in range(2):
                        c0 = hp * 128 + h * 64
                        for q4 in range(4):
                            src = minevv[i, :, q4 * 4:q4 * 4 + 4, c0:c0 + 64]
                            r.append(e.dma_start(out=Vp[h][:, i * 16 + q4 * 4:i * 16 + q4 * 4 + 4, h * 64:(h + 1) * 64], in_=src))
                return r
            sc.op("pool", ldv, reads=["mine"], writes=["Vp"], slot="vld", ndma=32)
            sc.op("dve", lambda e: e.tensor_scalar(out=Kst[64:128, :, 0:S - 128], in0=Kst[64:128, :, 0:S - 128],
                                                   scalar1=-1.0, scalar2=None, op0=ALU.mult),
                  reads=["Kst"], writes=["Kst"])

            steps = [(qt, kb) for qt in range(16) for kb in range(4 * qt + 3, -1, -1)]
            ns = len(steps)

            def ldq(qt, hp=hp):
                qb = qt % 2

                def fn(e):
                    i = qt // 4
                    c0 = (qt % 4) * 512
                    rr = hp * 128
                    src = minev[i, rr:rr + 128, c0:c0 + 512].rearrange("(h d) c -> d h c", h=2)
                    return [e.dma_start(out=Qd[0:64, qb, :].rearrange("p (h c) -> p h c", h=2), in_=src),
                            e.dma_start(out=Qd[64:128, qb, :].rearrange("p (h c) -> p h c", h=2), in_=src),
                            e.dma_start(out=Qz[0:64, qb, :].rearrange("p (h c) -> p h c", h=2), in_=src)]
                sc.op("sp", fn, reads=["mine"], writes=[("Q2", qb)], slot=f"q2{qb}", ndma=3)

            def pbuf(s):
                qt, kb = steps[s]
                i = kb - 4 * qt
                if i >= 1:
                    return Pd[:, i - 1, :], ("Pd", i - 1)
                pb_, st_ = (s // 2) % 2, s % 2
                return PP[:, pb_, st_ * 1024:(st_ + 1) * 1024], ("PP", pb_, st_)

            def abuf(s):
                qt, kb = steps[s]
                i = kb - 4 * qt
                if i >= 1:
                    return Ad[:, i - 1, :], ("Ad", i - 1)
                return A[:, s % 2, :], ("A", s % 2)

            def stA(s):
                qt, kb = steps[s]
                zb, qb = s % 2, qt % 2
                c0 = max(0, (kb - 4 * qt)) * 128

                def fn(e):
                    r = None
                    for h in range(2):
                        q = Qz[:, qb, h * 512 + c0:(h + 1) * 512]
                        r = mm_group(e, ps[:, 2 * zb + h, c0:512], [(Kst[:, h, kb * 128:(kb + 1) * 128], q)])
                    return r
                sc.op("pe", fn, reads=["Kst", ("Q2", qb), "Q2z"], writes=[("Z", zb)])

            def stS1(s):
                qt, kb = steps[s]
                zb = s % 2
                c0 = max(0, (kb - 4 * qt)) * 128
                Zv = ps[:, 2 * zb:2 * zb + 2, c0:512]
                Ev = E[:, zb, :].rearrange("p (h c) -> p h c", h=2)[:, :, c0:512]
                pt, pkey = pbuf(s)
                Pv = pt.rearrange("p (h c) -> p h c", h=2)
                sc.op("act", lambda e: e.activation(out=Ev, in_=Zv, func=AF.Exp),
                      reads=[("Z", zb)], writes=[("E", zb)])
                sc.op("act", lambda e: e.activation(out=Pv[:, :, c0:512], in_=Ev, func=AF.Ln, bias=1.0, scale=1.0),
                      reads=[("E", zb)], writes=[pkey])
                if kb >= 4 * qt:
                    sc.op("dve", lambda e: [e.tensor_tensor(out=Pv[:, h, c0:c0 + 128], in0=Pv[:, h, c0:c0 + 128],
                                                            in1=cmat[:, TRI01, :], op=ALU.mult) for h in range(2)],
                          reads=["consts", pkey], writes=[pkey])

            def stB(s):
                qt, kb = steps[s]
                pb, qb = s % 3, qt % 2
                first = kb == 4 * qt + 3
                diag = kb >= 4 * qt
                i = kb - 4 * qt

                pt, pkey = pbuf(s)

                def fn(e):
                    r = None
                    for h in range(2):
                        q = (Qz if first else Qd)[:, qb, h * 512:(h + 1) * 512]
                        pairs = [(Kst[:, h, kb * 128:(kb + 1) * 128], q),
                                 (cmat[:, NEGTRI, :], pt[:, h * 512:(h + 1) * 512])]
                        r = mm_group(e, ps[:, 4 + h, :], pairs, start=first)
                    return r
                sc.op("pe", fn, reads=["Kst", ("Q2", qb), "Q2z", pkey, "consts"], writes=["B"])

            def stS2(s):
                qt, kb = steps[s]
                c0 = max(0, (kb - 4 * qt)) * 128
                at, akey = abuf(s)
                Av = at.rearrange("p (h c) -> p h c", h=2)
                sc.op("act", lambda e: e.activation(out=Av[:, :, c0:512], in_=ps[:, 4:6, c0:512], func=AF.Exp),
                      reads=["B"], writes=[akey])
                if kb >= 4 * qt:
                    sc.op("dve", lambda e: [e.tensor_tensor(out=Av[:, h, c0:c0 + 128], in0=Av[:, h, c0:c0 + 128],
                                                            in1=cmat[:, TRI01, :], op=ALU.mult) for h in range(2)],
                          reads=["consts", akey], writes=[akey])

            def stC1(s):
                qt, kb = steps[s]
                pb = s % 3
                last = kb == 0
                diag = kb >= 4 * qt
                i = kb - 4 * qt
                if last:
                    return

                pt, pkey = pbuf(s)

                def fn(e):
                    r = None
                    for h in range(2):
                        pairs = [(cmat[:, NEGREST, :], pt[:, h * 512:(h + 1) * 512])]
                        r = mm_group(e, ps[:, 4 + h, :], pairs, start=False)
                    return r
                sc.op("pe", fn, reads=[pkey, "consts"], writes=["B"])

            def stPV(s, hp=hp):
                qt, kb = steps[s]
                ab, ob = s % 2, qt % 2
                first = kb == 4 * qt + 3
                last = kb == 0

                at, akey = abuf(s)

                def fn(e):
                    pairs = [(Vp[h][:, kb, :], at[:, h * 512:(h + 1) * 512]) for h in range(2)]
                    return mm_group(e, ps[:, 6 + ob, :], pairs, start=first, stop=last)
                sc.op("pe", fn, reads=[akey, "Vp"], writes=[("O", ob)])
                if last:
                    sc.op("dve", lambda e: e.tensor_copy(out=Osb[:, ob, :], in_=ps[:, 6 + ob, :]),
                          reads=[("O", ob)], writes=[("Osb", ob)])
                    sc.op("sp", lambda e: e.dma_start(out=ogin[j][(qt // 4) * 256 + hp * 128:(qt // 4) * 256 + (hp + 1) * 128, (qt % 4) * 512:(qt % 4 + 1) * 512], in_=Osb[:, ob, :]),
                          reads=[("Osb", ob)], writes=[("ogin", hp, qt // 4)], slot=f"osb{ob}")
                    if qt % 4 == 3 and hp == 1:
                        ag_o(hp, qt // 4)

            def doA(s):
                if s > 0 and steps[s][0] != steps[s - 1][0]:
                    ldq(steps[s][0])
                stA(s)

            def is_diag(s):
                qt, kb = steps[s]
                return kb >= 4 * qt

            def s1_part(m, part):
                a = 2 * m
                if is_diag(a):
                    stS1(a + part)
                    return
                pb_ = m % 2
                if part == 0:
                    sc.op("act", lambda e: e.activation(out=E[:, :, :], in_=ps[:, 0:4, :].rearrange("p a b -> p (a b)").rearrange("p (a b) -> p a b", a=2), func=AF.Exp),
                          reads=[("Z", 0), ("Z", 1)], writes=[("E", 0), ("E", 1)])
                else:
                    sc.op("act", lambda e: e.activation(out=PP[:, pb_, :], in_=E[:, :, :].rearrange("p a b -> p (a b)"), func=AF.Ln, bias=1.0, scale=1.0),
                          reads=[("E", 0), ("E", 1)], writes=[("PP", pb_, 0), ("PP", pb_, 1)])

            ldq(0)
            doA(0)
            doA(1)
            s1_part(0, 0)
            s1_part(0, 1)
            doA(2)
            doA(3)
            for m in range(ns // 2):
                a, b = 2 * m, 2 * m + 1
                nxt = (2 * m + 2) < ns
                if nxt:
                    s1_part(m + 1, 0)
                if a >= 1:
                    stC1(a - 1)
                stB(a)
                stS2(a)
                if a >= 1:
                    stPV(a - 1)
                if a + 4 < ns:
                    doA(a + 4)
                if nxt:
                    s1_part(m + 1, 1)
                stC1(a)
                stB(b)
                stS2(b)
                stPV(a)
                if b + 4 < ns:
                    doA(b + 4)
            stPV(ns - 1)

        sc.barrier(("pe", "act", "dve", "sp"))
        oT = av(0, [8, T])
        if debug and layer == 0:
            sc.op("sp", lambda e: [e.dma_start(out=dbg_mine, in_=mine[0]), e.dma_start(out=dbg_ogin, in_=ogin[0])],
                  reads=["mine"] + [("ogin", h_, t_) for h_ in range(2) for t_ in range(4)], writes=["dbg"], slot="dbg", ndma=2)

        def ldo(e):
            ov = ogout[j].rearrange("(tc rh i p) t -> tc rh p i t", tc=4, rh=2, i=4)
            g = bass.ds(mysl(e), 1)
            o4 = oT.rearrange("p (i k2) t -> p i k2 t", k2=2)
            return [e.dma_start(out=o4[:, :, rh, :], in_=ov[g, rh, :, :, :].rearrange("o p i t -> (o p) i t")) for rh in range(2)]
        sc.op("pool", ldo, reads=[("ogout", h_, t_) for h_ in range(2) for t_ in range(4)], writes=["oT"], slot="oT", ndma=2)
        cnt = 0
        for dc in range(8):
            b, wkey = load_w8(wo_d[j, dc])
            wv_ = w8[:, b, :].rearrange("p (k c) -> p k c", k=8)
            for gt in range(4):
                db = 4 + cnt % 2
                cnt += 1
                sc.op("pe", lambda e, wv_=wv_, gt=gt, db=db: mm_group(
                    e, ps[:, db, :], [(wv_[:, k, :], oT[:, k, gt * 512:(gt + 1) * 512]) for k in range(8)]),
                    reads=[wkey, "oT"], writes=[("ps", db)])
                sc.op("dve", lambda e, dc=dc, gt=gt, db=db: e.tensor_tensor(
                    out=xT[:, dc, gt * 512:(gt + 1) * 512], in0=ps[:, db, :], in1=xT[:, dc, gt * 512:(gt + 1) * 512], op=ALU.add),
                    reads=[("ps", db), ("x", dc, gt)], writes=[("x", dc, gt)])

    stages = []
    for l in range(DEPTH):
        stages += [("ffn", l, 0), ("mix", l), ("ffn", l, 1), ("ple", l)]
    for st in stages:
        if st[0] == "ffn":
            ffn(st[1], st[2])
        elif st[0] == "mix":
            if st[1] % 2 == 0:
                attention(st[1])
            else:
                pool_mixer(st[1])
        else:
            ple(st[1])
        if stop_after is not None and st == stop_after:
            break

    sc.barrier(("sp",))
    for k in range(8):
        sc.op("sp", lambda e, k=k: e.dma_start(out=yT_d[k * 128:(k + 1) * 128, :], in_=xT[:, k, :]),
              reads=[("x", k, t) for t in range(4)], writes=["yT"], slot="yst")
    final_tok = sc.slot("yst")

    with nc.Block() as block:
        @block.sync
        def _(e):
            sc.replay("sp", e)
            e.wait_ge(final_tok[0], final_tok[1])
            if "dbg" in sc.slots:
                e.wait_ge(sc.slots["dbg"][0], sc.slots["dbg"][1])

        @block.gpsimd
        def _(e):
            sc.replay("pool", e)

        @block.tensor
        def _(e):
            sc.replay("pe", e)

        @block.scalar
        def _(e):
            sc.replay("act", e)

        @block.vector
        def _(e):
            sc.replay("dve", e)
    es.close()
    return nc


def _bf(a):
    return np.ascontiguousarray(a.astype(ml_dtypes.bfloat16))


def host_layout(inp, nl=DEPTH):
    f = lambda a: np.ascontiguousarray(np.asarray(a, dtype=np.float32))
    sh = {}
    gu = np.stack([f(inp["w_ffn1_gu"][:nl]), f(inp["w_ffn2_gu"][:nl])], 1)
    gate = gu[..., :DFF].reshape(nl, 2, 8, 128, NF, 128)
    up = gu[..., DFF:].reshape(nl, 2, 8, 128, NF, 128)
    g2 = np.concatenate([gate.transpose(0, 1, 4, 3, 2, 5), up.transpose(0, 1, 4, 3, 2, 5)], -1)
    sh["wgu"] = np.ascontiguousarray(g2)
    dn = np.stack([f(inp["w_ffn1_down"][:nl]), f(inp["w_ffn2_down"][:nl])], 1)
    dn = dn.reshape(nl, 2, NF, 128, 8, 128).transpose(0, 1, 4, 3, 2, 5)
    sh["wdn"] = np.ascontiguousarray(dn).reshape(nl, 2, 8, 128, NF * 128)
    wqkv = f(inp["w_qkv"])
    qk = wqkv[:, :, :2048].reshape(2, 8, 128, 16, 128).transpose(0, 3, 2, 1, 4)
    sh["wqk"] = np.ascontiguousarray(qk)
    sh["wv"] = np.ascontiguousarray(wqkv[:, :, 2048:].reshape(2, 8, 128, 1024).transpose(0, 2, 1, 3))
    c8 = lambda w: np.ascontiguousarray(w.reshape(w.shape[0], 8, 128, 8, 128).transpose(0, 3, 2, 1, 4))
    sh["wo"] = c8(f(inp["w_o"]))
    sh["wpi"] = c8(f(inp["w_pool_in"]))
    sh["wpg"] = c8(f(inp["w_ple_gate"]))
    wg = f(inp["w_pool_grp"]).reshape(2, 4, 2, 128, 256).transpose(0, 3, 1, 2, 4)
    sh["wgrp"] = np.ascontiguousarray(wg).reshape(2, 128, 2048)
    sh["wpp"] = np.ascontiguousarray(f(inp["w_ple_proj"]).reshape(DEPTH, 2, 128, 1024).transpose(0, 2, 1, 3))
    gn = np.stack([f(inp["norm_ffn1"]), f(inp["norm_mix"]), f(inp["norm_ffn2"]), f(inp["norm_ple"])], 0)
    sh["gn"] = np.ascontiguousarray(gn.reshape(4, DEPTH, 8, 128).transpose(3, 0, 1, 2)).reshape(128, 128)
    qk_g = np.stack([f(inp["q_norm"]), f(inp["k_norm"])], 0)
    qk_g = np.concatenate([qk_g, qk_g], -1)
    sh["qkg"] = np.ascontiguousarray(qk_g.transpose(2, 0, 1)).reshape(128, 4)
    sh["psc"] = np.ascontiguousarray(f(inp["pool_scale"]).reshape(2, 8, 128).transpose(2, 0, 1)).reshape(128, 16)
    cm = np.zeros((128, 7, 128), np.float32)
    cm[:, 0] = np.eye(128)
    cm[:, 1] = 1.0 / 1024
    cm[:64, 2, :64] = 1.0 / 64
    cm[64:, 2, 64:] = 1.0 / 64
    jj, ss = np.meshgrid(np.arange(128), np.arange(128), indexing="ij")
    cm[:, 3] = -1.0 * (jj >= ss)
    cm[:, 4] = -1.0 * (jj < ss)
    cm[:, 5] = -np.eye(128)
    cm[:, 6] = 1.0 * (jj < ss)
    sh["cmat"] = _bf(cm.reshape(128, 896))
    mk = np.zeros((128, 4, 512), np.float32)
    for i in range(4):
        kpos = 128 * i + np.arange(128)[:, None]
        mk[:, i] = np.where(kpos >= np.arange(512)[None, :], MASKV, 0.0)
    sh["mneg"] = _bf(mk.reshape(128, 2048))
    x = f(inp["x"])
    p = f(inp["p"])
    maps = []
    for r in range(8):
        b, c = r // 4, r % 4
        m = dict(sh)
        m["xT"] = np.ascontiguousarray(x[b, c * T:(c + 1) * T, :].T)
        m["wqk"] = np.ascontiguousarray(sh["wqk"][:, [2 * c, 2 * c + 1, 8 + 2 * c, 8 + 2 * c + 1]])
        m["wv"] = np.ascontiguousarray(sh["wv"][:, :, :, 256 * c:256 * (c + 1)])
        m["pT"] = np.ascontiguousarray(p[:, b, c * T:(c + 1) * T, :].transpose(0, 2, 1))
        s = np.zeros((128, 4), np.float32)
        if c > 0:
            s[:, c - 1] = 1.0
        m["sel"] = s
        ic = np.zeros((128, 4, 16), np.float32)
        for g in range(4):
            w = 2 << g
            if c == 0:
                ic[:, g] = 1.0 / np.minimum(np.arange(16) + 1, w)
            else:
                ic[:, g] = 1.0 / w
        m["icnt"] = ic.reshape(128, 64)
        maps.append(m)
    return maps


_NC_CACHE = {}


def kernel(_stop_after=None, _debug=False, **inputs):
    if _stop_after is not None:
        nl = _stop_after[1] + 1
        maps = host_layout(inputs, nl)
        nc = build(_stop_after, nl=nl, debug=_debug)
        res = run_bass_kernel_spmd(nc, maps, core_ids=list(range(8)))
        _NC_CACHE["res"] = res
        out = np.zeros((2, S, D), np.float32)
        for r in range(8):
            b, c = r // 4, r % 4
            out[b, c * T:(c + 1) * T, :] = np.asarray(res.results[r]["yT"]).T
        return out
    maps = host_layout(inputs)
    if False:
        _NC_CACHE["nc"] = build(_stop_after)
    if "nc" not in _NC_CACHE:
        _NC_CACHE["nc"] = build()
    res = run_bass_kernel_spmd(_NC_CACHE["nc"], maps, core_ids=list(range(8)))
    out = np.zeros((2, S, D), np.float32)
    for r in range(8):
        b, c = r // 4, r % 4
        out[b, c * T:(c + 1) * T, :] = np.asarray(res.results[r]["yT"]).T
    return out
```
